# Optimizing a Trainium2 kernel written in Bass

```python
import math
import numpy as np
import jax
import jax.numpy as jnp
from jax import lax

D_MODEL = 1024
BATCH = 16
SEQ = 2048
DEPTH = 2

HEAD_DIM = 64
MOBA_HEADS = 4
MOBA_BLOCK = 256
MOBA_TOPK = 3
MOBA_QCHUNK = 16
NSA_HEADS = 4
NSA_CMP_LEN = 32
NSA_CMP_STRIDE = 16
NSA_SEL_LEN = 64
NSA_SEL_TOPN = 16
NSA_WINDOW = 512
NSA_QCHUNK = 64
NSA_FORCE_BONUS = 1e4
WIN_QBLOCK = 128
GLA_HEADS = 4
GLA_DK = 64
GLA_DV = 128
GLA_GATE_RANK = 16
GLA_GATE_NORM = 16.0
GLA_CHUNK = 64
N_BRANCH = 3
D_FF = 2816
CONV_WIDTH = 3
REL_BUCKETS = 32
REL_MAX_DIST = 128
N_SOFTMAX_HEADS = MOBA_HEADS + NSA_HEADS

NORM_EPS = 1e-6
NEG_INF = -1e30

SPLIT_SIZES = (
    MOBA_HEADS * HEAD_DIM,
    MOBA_HEADS * HEAD_DIM,
    MOBA_HEADS * HEAD_DIM,
    NSA_HEADS * HEAD_DIM,
    6 * HEAD_DIM,
    NSA_HEADS * 3,
    GLA_HEADS * GLA_DK,
    GLA_HEADS * GLA_DK,
    GLA_HEADS * GLA_DV,
    GLA_GATE_RANK,
    GLA_HEADS * GLA_DV,
    N_BRANCH * D_MODEL,
)
D_IN = sum(SPLIT_SIZES)
SPLIT_POINTS = tuple(int(v) for v in np.cumsum(SPLIT_SIZES)[:-1])

kernel_name = "hybrid_moba_nsa_gla_convffn"


def rms_norm(x, gain):
    xf = x.astype(jnp.float32)
    y = xf * lax.rsqrt(jnp.mean(xf * xf, axis=-1, keepdims=True) + NORM_EPS)
    return (y * gain.astype(jnp.float32)).astype(x.dtype)


def masked_softmax(logits, mask):
    s = jnp.where(mask, logits.astype(jnp.float32), NEG_INF)
    return jax.nn.softmax(s, axis=-1) * mask.astype(jnp.float32)


def rel_bucket(dist):
    n = jnp.maximum(dist, 0)
    max_exact = REL_BUCKETS // 2
    nf = jnp.maximum(n, 1).astype(jnp.float32)
    large = max_exact + (jnp.log(nf / max_exact) / math.log(REL_MAX_DIST / max_exact)
                         * (REL_BUCKETS - max_exact)).astype(jnp.int32)
    large = jnp.minimum(large, REL_BUCKETS - 1)
    return jnp.where(n < max_exact, n, large)


def moba_attention(q, k, v, rel_tab):
    B, S, H, Dh = q.shape
    nb = -(-S // MOBA_BLOCK)
    s_pad = nb * MOBA_BLOCK
    pad = ((0, 0), (0, s_pad - S), (0, 0), (0, 0))
    q, k, v = (jnp.pad(a, pad).transpose(0, 2, 1, 3) for a in (q, k, v))
    kb = k.reshape(B, H, nb, MOBA_BLOCK, Dh)
    vb = v.reshape(B, H, nb, MOBA_BLOCK, Dh)
    k_mean = jnp.mean(kb.astype(jnp.float32), axis=3)
    qblk = jnp.arange(s_pad) // MOBA_BLOCK
    past = jnp.arange(nb)[None, :] < qblk[:, None]
    gate = jnp.where(past, jnp.einsum('bhsd,bhnd->bhsn', q.astype(jnp.float32), k_mean), NEG_INF)
    kk = min(MOBA_TOPK, nb)
    _, sel = lax.top_k(gate, kk)
    sel_valid = jnp.arange(kk)[None, :] < qblk[:, None]
    scale = Dh ** -0.5
    tab_t = rel_tab.T
    head_idx = jnp.arange(H)[None, :, None, None]
    gather_blocks = jax.vmap(jax.vmap(lambda blocks, ix: blocks[ix]))
    n_sel = kk * MOBA_BLOCK

    def chunk(c):
        start = c * MOBA_QCHUNK
        qc = lax.dynamic_slice_in_dim(q, start, MOBA_QCHUNK, axis=2)
        sc = lax.dynamic_slice_in_dim(sel, start, MOBA_QCHUNK, axis=2)
        valid = lax.dynamic_slice_in_dim(sel_valid, start, MOBA_QCHUNK, axis=0)
        tpos = start + jnp.arange(MOBA_QCHUNK)
        own = start // MOBA_BLOCK
        kg = gather_blocks(kb, sc).reshape(B, H, MOBA_QCHUNK, n_sel, Dh)
        vg = gather_blocks(vb, sc).reshape(B, H, MOBA_QCHUNK, n_sel, Dh)
        ko = lax.dynamic_index_in_dim(kb, own, axis=2, keepdims=False)
        vo = lax.dynamic_index_in_dim(vb, own, axis=2, keepdims=False)
        kpos_sel = (sc[..., None] * MOBA_BLOCK + jnp.arange(MOBA_BLOCK)).reshape(B, H, MOBA_QCHUNK, n_sel)
        dist_own = tpos[:, None] - (own * MOBA_BLOCK + jnp.arange(MOBA_BLOCK))[None, :]
        bias_sel = tab_t[head_idx, rel_bucket(tpos[None, None, :, None] - kpos_sel)]
        bias_own = rel_tab[rel_bucket(dist_own)].transpose(2, 0, 1)
        s_sel = jnp.einsum('bhqd,bhqkd->bhqk', qc, kg) * scale + bias_sel
        s_own = jnp.einsum('bhqd,bhkd->bhqk', qc, ko) * scale + bias_own
        mask = jnp.concatenate([jnp.repeat(valid, MOBA_BLOCK, axis=1), dist_own >= 0], axis=-1)
        p = masked_softmax(jnp.concatenate([s_sel, s_own], axis=-1), mask).astype(v.dtype)
        return (jnp.einsum('bhqk,bhqkd->bhqd', p[..., :n_sel], vg)
                + jnp.einsum('bhqk,bhkd->bhqd', p[..., n_sel:], vo))

    out = lax.map(chunk, jnp.arange(s_pad // MOBA_QCHUNK))
    out = out.transpose(1, 2, 0, 3, 4).reshape(B, H, s_pad, Dh).transpose(0, 2, 1, 3)
    return out[:, :S]


def compress_tokens(x, pos_emb, w1, w2):
    B, S, Dh = x.shape
    r = NSA_CMP_LEN // NSA_CMP_STRIDE
    n_cmp = S // NSA_CMP_STRIDE - r + 1
    xs = x.reshape(B, S // NSA_CMP_STRIDE, NSA_CMP_STRIDE, Dh)
    blocks = jnp.concatenate([xs[:, j:j + n_cmp] for j in range(r)], axis=2)
    h = (blocks + pos_emb).reshape(B, n_cmp, NSA_CMP_LEN * Dh)
    return jax.nn.gelu(h @ w1) @ w2


def overlap_matrix(n_cmp, n_blk):
    starts = np.arange(n_cmp) * NSA_CMP_STRIDE
    tok = np.arange(n_blk * NSA_SEL_LEN)
    inside = (tok[None, :] >= starts[:, None]) & (tok[None, :] < starts[:, None] + NSA_CMP_LEN)
    m = inside.reshape(n_cmp, n_blk, NSA_SEL_LEN).sum(-1) / NSA_CMP_LEN
    return jnp.asarray(m, jnp.float32)


def window_attention(q, k, v, rel_tab):
    B, S, H, Dh = q.shape
    nqb = S // WIN_QBLOCK
    nkb = NSA_WINDOW // WIN_QBLOCK
    kp = jnp.pad(k, ((0, 0), (NSA_WINDOW, 0), (0, 0))).reshape(B, nqb + nkb, WIN_QBLOCK, Dh)
    vp = jnp.pad(v, ((0, 0), (NSA_WINDOW, 0), (0, 0))).reshape(B, nqb + nkb, WIN_QBLOCK, Dh)
    kband = jnp.concatenate([kp[:, j:j + nqb] for j in range(nkb + 1)], axis=2)
    vband = jnp.concatenate([vp[:, j:j + nqb] for j in range(nkb + 1)], axis=2)
    qb = q.reshape(B, nqb, WIN_QBLOCK, H, Dh)
    qoff = jnp.arange(WIN_QBLOCK)
    koff = jnp.arange((nkb + 1) * WIN_QBLOCK) - NSA_WINDOW
    dist = qoff[:, None] - koff[None, :]
    kpos = jnp.arange(nqb)[:, None] * WIN_QBLOCK + koff[None, :]
    mask = ((dist >= 0) & (dist < NSA_WINDOW))[None] & (kpos >= 0)[:, None, :]
    bias = rel_tab[rel_bucket(dist)].transpose(2, 0, 1)
    s = jnp.einsum('bnqhd,bnkd->bhnqk', qb, kband) * (Dh ** -0.5) + bias[:, None]
    p = masked_softmax(s, mask).astype(v.dtype)
    return jnp.einsum('bhnqk,bnkd->bnqhd', p, vband).reshape(B, S, H, Dh)


def nsa_attention(q, kc, vc, k_slc, v_slc, k_win, v_win, gates, rel_tab):
    B, S, H, Dh = q.shape
    scale = Dh ** -0.5
    n_cmp = kc.shape[1]
    t = jnp.arange(S)
    cend = jnp.arange(n_cmp) * NSA_CMP_STRIDE + NSA_CMP_LEN - 1
    p_cmp = masked_softmax(jnp.einsum('bshd,bnd->bhsn', q, kc) * scale, cend[None, :] <= t[:, None])
    o_cmp = jnp.einsum('bhsn,bnd->bshd', p_cmp.astype(vc.dtype), vc)
    n_blk = S // NSA_SEL_LEN
    imp = jnp.einsum('bhsn,nj->bsj', p_cmp, overlap_matrix(n_cmp, n_blk))
    cur = t // NSA_SEL_LEN
    blk = jnp.arange(n_blk)[None, :]
    forced = (blk == 0) | (blk == cur[:, None]) | (blk == cur[:, None] - 1)
    imp = jnp.where(blk <= cur[:, None], imp + jnp.where(forced, NSA_FORCE_BONUS, 0.0), NEG_INF)
    n_top = min(NSA_SEL_TOPN, n_blk)
    _, sel = lax.top_k(imp, n_top)
    sel_valid = jnp.arange(n_top)[None, :] < (cur + 1)[:, None]
    ksb = k_slc.reshape(B, n_blk, NSA_SEL_LEN, Dh)
    vsb = v_slc.reshape(B, n_blk, NSA_SEL_LEN, Dh)
    gather_blocks = jax.vmap(lambda blocks, ix: blocks[ix])
    n_keys = n_top * NSA_SEL_LEN

    def chunk(c):
        start = c * NSA_QCHUNK
        qc = lax.dynamic_slice_in_dim(q, start, NSA_QCHUNK, axis=1)
        sc = lax.dynamic_slice_in_dim(sel, start, NSA_QCHUNK, axis=1)
        valid = lax.dynamic_slice_in_dim(sel_valid, start, NSA_QCHUNK, axis=0)
        tpos = start + jnp.arange(NSA_QCHUNK)
        kg = gather_blocks(ksb, sc).reshape(B, NSA_QCHUNK, n_keys, Dh)
        vg = gather_blocks(vsb, sc).reshape(B, NSA_QCHUNK, n_keys, Dh)
        kpos = (sc[..., None] * NSA_SEL_LEN + jnp.arange(NSA_SEL_LEN)).reshape(B, NSA_QCHUNK, n_keys)
        dist = tpos[None, :, None] - kpos
        bias = rel_tab[rel_bucket(dist)].transpose(0, 3, 1, 2)
        mask = (dist >= 0) & jnp.repeat(valid, NSA_SEL_LEN, axis=1)[None]
        s = jnp.einsum('bqhd,bqkd->bhqk', qc, kg) * scale + bias
        p = masked_softmax(s, mask[:, None]).astype(vg.dtype)
        return jnp.einsum('bhqk,bqkd->bqhd', p, vg)

    o_slc = lax.map(chunk, jnp.arange(S // NSA_QCHUNK))
    o_slc = o_slc.transpose(1, 0, 2, 3, 4).reshape(B, S, H, Dh)
    o_win = window_attention(q, k_win, v_win, rel_tab)
    return gates[..., 0:1] * o_cmp + gates[..., 1:2] * o_slc + gates[..., 2:3] * o_win


def gla_attention(q, k, v, log_a):
    B, S, H, Dk = q.shape
    Dv = v.shape[-1]
    n_chunk = S // GLA_CHUNK

    def to_chunks(a):
        return a.astype(jnp.float32).reshape(B, n_chunk, GLA_CHUNK, H, -1).transpose(1, 0, 3, 2, 4)

    qc, kc, vc, gc = (to_chunks(a) for a in (q * (Dk ** -0.5), k, v, log_a))
    causal = jnp.tril(jnp.ones((GLA_CHUNK, GLA_CHUNK), dtype=bool))

    def step(state, inp):
        qi, ki, vi, gi = inp
        b = jnp.cumsum(gi, axis=2)
        decay = jnp.exp(jnp.where(causal[:, :, None], b[:, :, :, None, :] - b[:, :, None, :, :], NEG_INF))
        a_intra = jnp.einsum('bhid,bhjd,bhijd->bhij', qi, ki, decay)
        o = (jnp.einsum('bhij,bhjv->bhiv', a_intra, vi)
             + jnp.einsum('bhid,bhdv->bhiv', qi * jnp.exp(b), state))
        b_last = b[:, :, -1:, :]
        new_state = (state * jnp.exp(b_last[:, :, 0, :, None])
                     + jnp.einsum('bhjd,bhjv->bhdv', ki * jnp.exp(b_last - b), vi))
        return new_state, o

    state0 = jnp.zeros((B, H, Dk, Dv), jnp.float32)
    _, o = lax.scan(step, state0, (qc, kc, vc, gc))
    return o.transpose(1, 0, 3, 2, 4).reshape(B, S, H, Dv)


def hybrid_mixer(h, rel_bias, w_in, moba_q_norm, moba_k_norm, nsa_q_norm, nsa_k_norm,
                 cmp_pos_k, cmp_pos_v, cmp_k_w1, cmp_k_w2, cmp_v_w1, cmp_v_w2,
                 gla_gate_w, gla_gate_b, gla_out_norm,
                 w_branch_moba, w_branch_nsa, w_branch_gla, w_out):
    B, S, _ = h.shape
    (mq, mk, mv, nq, nkv, ngate, gq, gk, gv, g_lr, g_out, merge) = jnp.split(h @ w_in, SPLIT_POINTS, axis=-1)

    def heads(a, n):
        return a.reshape(B, S, n, -1)

    o_moba = moba_attention(rms_norm(heads(mq, MOBA_HEADS), moba_q_norm),
                            rms_norm(heads(mk, MOBA_HEADS), moba_k_norm),
                            heads(mv, MOBA_HEADS), rel_bias[:, :MOBA_HEADS]).reshape(B, S, -1)
    k_c, v_c, k_s, v_s, k_w, v_w = jnp.split(nkv, 6, axis=-1)
    kc = rms_norm(compress_tokens(k_c, cmp_pos_k, cmp_k_w1, cmp_k_w2), nsa_k_norm[0])
    vc = compress_tokens(v_c, cmp_pos_v, cmp_v_w1, cmp_v_w2)
    o_nsa = nsa_attention(rms_norm(heads(nq, NSA_HEADS), nsa_q_norm), kc, vc,
                          rms_norm(k_s, nsa_k_norm[1]), v_s, rms_norm(k_w, nsa_k_norm[2]), v_w,
                          jax.nn.sigmoid(heads(ngate, NSA_HEADS)), rel_bias[:, MOBA_HEADS:]).reshape(B, S, -1)
    log_a = jax.nn.log_sigmoid((g_lr @ gla_gate_w + gla_gate_b).astype(jnp.float32)) / GLA_GATE_NORM
    o_gla = gla_attention(heads(gq, GLA_HEADS), heads(gk, GLA_HEADS), heads(gv, GLA_HEADS),
                          heads(log_a, GLA_HEADS)).astype(h.dtype)
    o_gla = (rms_norm(o_gla, gla_out_norm) * jax.nn.silu(heads(g_out, GLA_HEADS))).reshape(B, S, -1)
    g_a, g_b, g_c = jnp.split(jax.nn.sigmoid(merge), N_BRANCH, axis=-1)
    z = g_a * (o_moba @ w_branch_moba) + g_b * (o_nsa @ w_branch_nsa) + g_c * (o_gla @ w_branch_gla)
    return z @ w_out


def conv_ffn(h, w_up, conv_w, conv_b, w_down):
    a, g = jnp.split(h @ w_up, 2, axis=-1)
    a = lax.conv_general_dilated(a, conv_w[:, None, :].astype(a.dtype), (1,), [(CONV_WIDTH - 1, 0)],
                                 dimension_numbers=('NWC', 'WIO', 'NWC'), feature_group_count=D_FF) + conv_b
    return (jax.nn.gelu(a) * g) @ w_down


def setup_inputs(seed: int = 0) -> dict:
    key = jax.random.key(seed)
    ks = jax.random.split(key, 32)
    L = DEPTH
    qkv_w = MOBA_HEADS * HEAD_DIM

    def nrm(k, shape, scale):
        return jax.random.normal(k, shape, jnp.float32) * scale

    def gain(k, shape):
        return 1.0 + 0.02 * jax.random.normal(k, shape, jnp.float32)

    return {
        "x": nrm(ks[0], (BATCH, SEQ, D_MODEL), 1.0),
        "rel_bias": nrm(ks[1], (REL_BUCKETS, N_SOFTMAX_HEADS), 0.1),
        "attn_norm": gain(ks[2], (L, D_MODEL)),
        "w_in": nrm(ks[3], (L, D_MODEL, D_IN), D_MODEL ** -0.5),
        "moba_q_norm": gain(ks[4], (L, HEAD_DIM)),
        "moba_k_norm": gain(ks[5], (L, HEAD_DIM)),
        "nsa_q_norm": gain(ks[6], (L, HEAD_DIM)),
        "nsa_k_norm": gain(ks[7], (L, 3, HEAD_DIM)),
        "cmp_pos_k": nrm(ks[8], (L, NSA_CMP_LEN, HEAD_DIM), 0.1),
        "cmp_pos_v": nrm(ks[9], (L, NSA_CMP_LEN, HEAD_DIM), 0.1),
        "cmp_k_w1": nrm(ks[10], (L, NSA_CMP_LEN * HEAD_DIM, HEAD_DIM), (NSA_CMP_LEN * HEAD_DIM) ** -0.5),
        "cmp_k_w2": nrm(ks[11], (L, HEAD_DIM, HEAD_DIM), HEAD_DIM ** -0.5),
        "cmp_v_w1": nrm(ks[12], (L, NSA_CMP_LEN * HEAD_DIM, HEAD_DIM), (NSA_CMP_LEN * HEAD_DIM) ** -0.5),
        "cmp_v_w2": nrm(ks[13], (L, HEAD_DIM, HEAD_DIM), HEAD_DIM ** -0.5),
        "gla_gate_w": nrm(ks[14], (L, GLA_GATE_RANK, GLA_HEADS * GLA_DK), GLA_GATE_RANK ** -0.5),
        "gla_gate_b": nrm(ks[15], (L, GLA_HEADS * GLA_DK), 0.1),
        "gla_out_norm": gain(ks[16], (L, GLA_DV)),
        "w_branch_moba": nrm(ks[17], (L, qkv_w, D_MODEL), qkv_w ** -0.5),
        "w_branch_nsa": nrm(ks[18], (L, NSA_HEADS * HEAD_DIM, D_MODEL), (NSA_HEADS * HEAD_DIM) ** -0.5),
        "w_branch_gla": nrm(ks[19], (L, GLA_HEADS * GLA_DV, D_MODEL), (GLA_HEADS * GLA_DV) ** -0.5),
        "w_out": nrm(ks[20], (L, D_MODEL, D_MODEL), D_MODEL ** -0.5),
        "ffn_norm": gain(ks[21], (L, D_MODEL)),
        "w_up": nrm(ks[22], (L, D_MODEL, 2 * D_FF), D_MODEL ** -0.5),
        "conv_w": nrm(ks[23], (L, CONV_WIDTH, D_FF), CONV_WIDTH ** -0.5),
        "conv_b": nrm(ks[24], (L, D_FF), 0.02),
        "w_down": nrm(ks[25], (L, D_FF, D_MODEL), D_FF ** -0.5),
    }


def reference(x, rel_bias, attn_norm, w_in, moba_q_norm, moba_k_norm, nsa_q_norm, nsa_k_norm,
              cmp_pos_k, cmp_pos_v, cmp_k_w1, cmp_k_w2, cmp_v_w1, cmp_v_w2,
              gla_gate_w, gla_gate_b, gla_out_norm,
              w_branch_moba, w_branch_nsa, w_branch_gla, w_out,
              ffn_norm, w_up, conv_w, conv_b, w_down):
    for l in range(DEPTH):
        h = rms_norm(x, attn_norm[l])
        x = x + hybrid_mixer(h, rel_bias, w_in[l], moba_q_norm[l], moba_k_norm[l], nsa_q_norm[l], nsa_k_norm[l],
                             cmp_pos_k[l], cmp_pos_v[l], cmp_k_w1[l], cmp_k_w2[l], cmp_v_w1[l], cmp_v_w2[l],
                             gla_gate_w[l], gla_gate_b[l], gla_out_norm[l],
                             w_branch_moba[l], w_branch_nsa[l], w_branch_gla[l], w_out[l])
        h = rms_norm(x, ffn_norm[l])
        x = x + conv_ffn(h, w_up[l], conv_w[l], conv_b[l], w_down[l])
    return x
```

```python
import math
from contextlib import ExitStack
import numpy as np
import concourse.bass as bass
import concourse.mybir as mybir
from concourse.bass_utils import run_bass_kernel_spmd

F32 = mybir.dt.float32
BF16 = mybir.dt.bfloat16
AF = mybir.ActivationFunctionType
ALU = mybir.AluOpType
AX = mybir.AxisListType

SEQ = 2048
DM = 1024
NTOK = 4096
DFF = 2816
NEGM = -8192.0
EPS = 1e-6


class Res:
    __slots__ = ("name", "w", "rs", "t")

    def __init__(self, name, t=None):
        self.name = name
        self.w = None
        self.rs = {}
        self.t = t

    def __getitem__(self, k):
        return self.t[k]


class Sched:
    ENG = ("pe", "act", "dve", "pool", "sp")

    def __init__(self, nc, n_dma_sems=20):
        self.nc = nc
        self.ops = {e: [] for e in self.ENG}
        self.cnt = {e: 0 for e in self.ENG}
        self.waited = {e: {} for e in self.ENG}
        self.n_dma_sems = n_dma_sems
        self.dma_cnt = {}
        self.dma_rr = {"sp": 0, "pool": 0, "act": 0}
        self.ntile = 0

    def make_arena(self, nbytes):
        self.arena_t = self.nc.alloc_sbuf_tensor("arena", [128, nbytes], mybir.dt.uint8)
        self.arena_n = nbytes

    def tile(self, shape, dtype, name, offset):
        self.ntile += 1
        name = "%s_%d" % (name, self.ntile)
        esz = 4 if dtype == F32 else 2
        n = int(np.prod(shape[1:]))
        assert offset % 4 == 0 and offset + n * esz <= self.arena_n, (name, offset, n * esz, self.arena_n)
        v = self.arena_t[0:shape[0], offset:offset + n * esz].bitcast(dtype)
        if len(shape) == 3:
            v = v.rearrange("p (a b) -> p a b", a=shape[1])
        elif len(shape) == 4:
            v = v.rearrange("p (a b c) -> p a b c", a=shape[1], b=shape[2])
        return Res(name, v)

    def psum(self, shape, dtype, name):
        self.ntile += 1
        return Res(name, self.nc.alloc_psum_tensor("%s_%d" % (name, self.ntile), list(shape), dtype))

    def _deps(self, reads, writes):
        deps = {}
        for r in reads:
            if r.w is not None:
                k, v = r.w
                if deps.get(k, 0) < v:
                    deps[k] = v
        for w in writes:
            if w.w is not None:
                k, v = w.w
                if deps.get(k, 0) < v:
                    deps[k] = v
            for k, v in w.rs.items():
                if deps.get(k, 0) < v:
                    deps[k] = v
        return deps

    def add(self, eng, fn, reads=(), writes=()):
        deps = self._deps(reads, writes)
        waits = []
        wd = self.waited[eng]
        for k, v in deps.items():
            if wd.get(k, 0) >= v:
                continue
            if eng == "pe" and k == "pe":
                continue
            waits.append((k, v))
            wd[k] = v
        self.cnt[eng] += 1
        key, val = eng, self.cnt[eng]
        self.ops[eng].append((waits, fn, key, 1))
        for r in reads:
            if r.rs.get(key, 0) < val:
                r.rs[key] = val
        for w in writes:
            w.w = (key, val)
            w.rs = {}

    def dma(self, q, fn, reads=(), writes=()):
        deps = self._deps(reads, writes)
        i = self.dma_rr[q]
        self.dma_rr[q] = (i + 1) % self.n_dma_sems
        key = "d_%s_%d" % (q, i)
        prev = self.dma_cnt.get(key, 0)
        if prev:
            deps[key] = max(deps.get(key, 0), 16 * prev)
        waits = []
        wd = self.waited[q]
        for k, v in deps.items():
            if wd.get(k, 0) >= v:
                continue
            waits.append((k, v))
            wd[k] = v
        self.dma_cnt[key] = prev + 1
        val = 16 * (prev + 1)
        self.ops[q].append((waits, fn, key, 16))
        for r in reads:
            if r.rs.get(key, 0) < val:
                r.rs[key] = val
        for w in writes:
            w.w = (key, val)
            w.rs = {}

    def barrier(self):
        tot = {e: self.cnt[e] for e in self.ENG if self.cnt[e]}
        for k, c in self.dma_cnt.items():
            tot[k] = 16 * c
        for e in self.ENG:
            waits = []
            wd = self.waited[e]
            for k, v in tot.items():
                if k == e or wd.get(k, 0) >= v:
                    continue
                waits.append((k, v))
                wd[k] = v
            if waits:
                self.ops[e].append((waits, None, None, 0))

    def emit(self):
        nc = self.nc
        keys = list(self.ENG) + sorted(self.dma_cnt.keys())
        with ExitStack() as es:
            sems = {k: es.enter_context(nc.semaphore("s_" + k)) for k in keys}
            block = es.enter_context(nc.Block())
            decs = {"pe": block.tensor, "act": block.scalar, "dve": block.vector,
                    "pool": block.gpsimd, "sp": block.sync}
            final = {k: 16 * c for k, c in self.dma_cnt.items()}
            for e in self.ENG:
                def body(engine, ops=self.ops[e], e=e):
                    for waits, fn, key, inc in ops:
                        for k, v in waits:
                            engine.wait_ge(sems[k], v)
                        if fn is None:
                            continue
                        fn(engine).then_inc(sems[key], inc)
                    if e == "sp":
                        for k, v in final.items():
                            engine.wait_ge(sems[k], v)
                        for k in self.ENG:
                            if k != "sp" and self.cnt[k]:
                                engine.wait_ge(sems[k], self.cnt[k])
                decs[e](body)


class Alloc:
    def __init__(self, S, base):
        self.S = S
        self.off = base

    def t(self, shape, dtype, name):
        esz = 4 if dtype == F32 else 2
        nb = (int(np.prod(shape[1:])) * esz + 63) // 64 * 64
        r = self.S.tile(shape, dtype, name, self.off)
        self.off += nb
        return r


def _rel_bucket(dist):
    n = np.maximum(dist, 0)
    nf = np.maximum(n, 1).astype(np.float32)
    large = 16 + (np.log(nf / np.float32(16)) / np.float32(math.log(128 / 16)) * np.float32(16)).astype(np.int32)
    large = np.minimum(large, 31)
    return np.where(n < 16, n, large)


def make_consts():
    c = {}
    m = np.arange(384) - 128
    c["c_onehot"] = (_rel_bucket(m)[None, :] == np.arange(32)[:, None]).astype(np.float32)
    tt = np.arange(16)
    blk = np.arange(8)
    pb = np.where(blk[None, :] < (tt // 2)[:, None], 0.0, -30000.0).astype(np.float32)
    own = (blk[None, :] == (tt // 2)[:, None]).astype(np.float32)
    c["c_pb"] = np.ascontiguousarray(np.broadcast_to(pb[None, :, None, :], (128, 16, 4, 8))).astype(np.float32)
    c["c_own"] = np.ascontiguousarray(np.broadcast_to(own[None, :, None, :], (128, 16, 4, 8))).astype(np.float32)
    t = np.arange(SEQ)
    cur = t // 64
    b32 = np.arange(32)
    forced = (b32[None, :] == 0) | (b32[None, :] == cur[:, None]) | (b32[None, :] == cur[:, None] - 1)
    fb = np.where(b32[None, :] <= cur[:, None], np.where(forced, 1e4, 0.0), -30000.0).astype(np.float32)
    c["c_fb"] = np.ascontiguousarray(fb.reshape(16, 128, 32).transpose(1, 0, 2))
    starts = np.arange(127) * 16
    tok = np.arange(32 * 64)
    inside = (tok[None, :] >= starts[:, None]) & (tok[None, :] < starts[:, None] + 32)
    ovl = (inside.reshape(127, 32, 64).sum(-1) / 32).astype(np.float32)
    c["c_ovl"] = np.concatenate([ovl, np.zeros((1, 32), np.float32)], 0)
    j = np.arange(128)
    c["c_cmask"] = (j[None, :] >= j[:, None]).astype(np.float32)
    return c


W_SHAPES = {
    "rel_bias": (32, 8), "attn_norm": (2, 1024), "w_in": (2, 1024, 6044), "moba_q_norm": (2, 64),
    "moba_k_norm": (2, 64), "nsa_q_norm": (2, 64), "nsa_k_norm": (2, 3, 64), "cmp_pos_k": (2, 32, 64),
    "cmp_pos_v": (2, 32, 64), "cmp_k_w1": (2, 2048, 64), "cmp_k_w2": (2, 64, 64), "cmp_v_w1": (2, 2048, 64),
    "cmp_v_w2": (2, 64, 64), "gla_gate_w": (2, 16, 256), "gla_gate_b": (2, 256), "gla_out_norm": (2, 128),
    "w_branch_moba": (2, 256, 1024), "w_branch_nsa": (2, 256, 1024), "w_branch_gla": (2, 512, 1024),
    "w_out": (2, 1024, 1024), "ffn_norm": (2, 1024), "w_up": (2, 1024, 5632), "conv_w": (2, 3, 2816),
    "conv_b": (2, 2816), "w_down": (2, 2816, 1024),
}


def build_program(layers=(0, 1), seqs=(0, 1), dbg=False, phases="NABCMF"):
    nc = bass.Bass("TRN2", target_bir_lowering=False)
    consts = make_consts()
    I = {}
    I["x"] = nc.dram_tensor("x", [NTOK, DM], F32, kind="ExternalInput").ap()
    for k, shp in W_SHAPES.items():
        I[k] = nc.dram_tensor(k, list(shp), F32, kind="ExternalInput").ap()
    for k, v in consts.items():
        I[k] = nc.dram_tensor(k, list(v.shape), F32, kind="ExternalInput").ap()
    yout = nc.dram_tensor("y", [NTOK, DM], F32, kind="ExternalOutput").ap()
    skind = "ExternalOutput" if dbg else "Internal"
    xA = nc.dram_tensor("xA", [NTOK, DM], F32, kind=skind).ap()
    xB = nc.dram_tensor("xB", [NTOK, DM], F32, kind=skind).ap()
    oM = nc.dram_tensor("oM", [4, 64, NTOK], BF16, kind=skind).ap()
    oN = nc.dram_tensor("oN", [4, 64, NTOK], BF16, kind=skind).ap()
    oG = nc.dram_tensor("oG", [4, 128, NTOK], BF16, kind=skind).ap()
    Zh = nc.dram_tensor("Zsk", [8, 128, 384], F32)
    Zap = Zh.ap()

    S = Sched(nc)
    S.make_arena(207 * 1024)

    dres = {}

    def DR(name, idx):
        k = (name, idx)
        if k not in dres:
            dres[k] = Res("%s_%s" % (name, idx))
        return dres[k]

    def MM(out, lhsT, rhs, start, stop, rd, wr):
        S.add("pe", lambda e: e.matmul(out, lhsT=lhsT, rhs=rhs, start=start, stop=stop), reads=rd, writes=wr)

    def TR(out, in_, ident, rd, wr):
        S.add("pe", lambda e: e.transpose(out, in_, ident), reads=rd, writes=wr)

    def ACT(out, in_, func, rd, wr, bias=None, scale=None, accum=None):
        kw = {}
        if bias is not None:
            kw["bias"] = bias
        if scale is not None:
            kw["scale"] = scale
        if accum is not None:
            kw["accum_out"] = accum
        S.add("act", lambda e: e.activation(out, in_, func, **kw), reads=rd, writes=wr)

    def TT(eng, out, in0, in1, op, rd, wr):
        S.add(eng, lambda e: e.tensor_tensor(out=out, in0=in0, in1=in1, op=op), reads=rd, writes=wr)

    def TS(eng, out, in0, s1, s2, op0, op1, rd, wr):
        if s2 is None:
            S.add(eng, lambda e: e.tensor_scalar(out=out, in0=in0, scalar1=s1, scalar2=None, op0=op0), reads=rd, writes=wr)
        else:
            S.add(eng, lambda e: e.tensor_scalar(out=out, in0=in0, scalar1=s1, scalar2=s2, op0=op0, op1=op1), reads=rd, writes=wr)

    def STT(out, in0, scalar, in1, op0, op1, rd, wr):
        S.add("dve", lambda e: e.scalar_tensor_tensor(out=out, in0=in0, scalar=scalar, in1=in1, op0=op0, op1=op1),
              reads=rd, writes=wr)

    def CP(eng, out, in_, rd, wr):
        if eng == "act":
            S.add("act", lambda e: e.copy(out, in_), reads=rd, writes=wr)
        else:
            S.add(eng, lambda e: e.tensor_copy(out=out, in_=in_), reads=rd, writes=wr)

    def RECIP(out, in_, rd, wr):
        S.add("dve", lambda e: e.reciprocal(out, in_), reads=rd, writes=wr)

    def MEMSET(eng, out, val, wr):
        S.add(eng, lambda e: e.memset(out, val), writes=wr)

    def ASEL(out, in_, pattern, op, fill, base, cm, rd, wr):
        S.add("pool", lambda e: e.affine_select(out, in_, pattern, op, fill, base=base, channel_multiplier=cm),
              reads=rd, writes=wr)

    def DMA(q, out, in_, rd, wr):
        S.dma(q, lambda e: e.dma_start(out=out, in_=in_), reads=rd, writes=wr)

    P = [S.psum([128, 512], F32, "pb%d" % i) for i in range(7)]
    pst = S.psum([128, 8, 128], BF16, "pst")

    A0 = Alloc(S, 0)
    identb = A0.t([128, 128], BF16, "identb")
    identf = A0.t([128, 128], F32, "identf")
    onesb = A0.t([128, 128], BF16, "onesb")
    onesf = A0.t([128, 128], F32, "onesf")
    tb = A0.t([128, 8, 2, 128], F32, "tb")
    tw4 = A0.t([128, 128], F32, "tw4")
    cb = A0.t([128, 8], F32, "cb")
    cmk = A0.t([128, SEQ], BF16, "cmk")
    pbt = A0.t([128, 16, 4, 8], F32, "pbt")
    ownt = A0.t([128, 16, 4, 8], F32, "ownt")
    fbt = A0.t([128, 16, 32], F32, "fbt")
    ovl = A0.t([128, 32], F32, "ovl")
    cmask = A0.t([128, 128], F32, "cmask")
    egate = A0.t([12, 12, 64], BF16, "egate")
    CONST_END = A0.off
    hT = A0.t([128, 8, SEQ], BF16, "hT")
    hTf0 = A0.t([128, 8, 128], F32, "hTf0")
    PH_BASE = A0.off

    def prologue():
        A = Alloc(S, PH_BASE)
        tab = A.t([32, 8], F32, "tab")
        oneh = A.t([32, 384], F32, "oneh")
        tbc = A.t([32, 128], F32, "tbc")
        frs = [A.t([128, 384], F32, "frs%d" % i) for i in range(2)]
        e12 = A.t([12, 12, 64], F32, "e12")
        MEMSET("pool", identf[:], 0.0, [identf])
        ASEL(identf[:], identf[:], [[-1, 128]], ALU.not_equal, 1.0, 0, 1, [identf], [identf])
        CP("dve", identb[:], identf[:], [identf], [identb])
        MEMSET("dve", onesb[:], 1.0, [onesb])
        MEMSET("dve", onesf[:], 1.0, [onesf])
        DMA("sp", tab[:], I["rel_bias"], [], [tab])
        DMA("sp", oneh[:], I["c_onehot"], [], [oneh])
        DMA("sp", pbt[:], I["c_pb"], [], [pbt])
        DMA("sp", ownt[:], I["c_own"], [], [ownt])
        DMA("sp", fbt[:], I["c_fb"], [], [fbt])
        DMA("sp", ovl[:], I["c_ovl"], [], [ovl])
        DMA("sp", cmask[:], I["c_cmask"], [], [cmask])
        zres = Res("zsk")
        for h in range(8):
            fr = frs[h % 2]
            ACT(tbc[:], onesf[0:32, :], AF.Identity, [onesf, tab], [tbc], scale=tab[:, h:h + 1])
            MM(P[0][:, 0:384], tbc[:], oneh[:], True, True, [tbc, oneh], [P[0]])
            CP("act", cb[:, h:h + 1], P[0][:, 383:384], [P[0]], [cb])
            TS("dve", fr[:], P[0][:, 0:384], cb[:, h:h + 1], 8.0, ALU.subtract, ALU.mult, [P[0], cb], [fr])
            DMA("sp", Zap[h], fr[:], [fr], [zres])
        S.barrier()
        for h in range(8):
            for j in range(2):
                src = bass.AP(tensor=Zh, offset=h * 128 * 384 + 128 * (j + 1), ap=[[383, 128], [1, 128]])
                DMA("sp", tb[:, h, j, :], src, [zres], [tb])
        ASEL(tb[:, :, 0, :], tb[:, :, 0, :], [[0, 8], [1, 128]], ALU.is_ge, NEGM, 0, -1, [tb], [tb])
        MEMSET("pool", tw4[:], 0.0, [tw4])
        ASEL(tw4[:], tw4[:], [[-1, 128]], ALU.is_ge, NEGM, -1, 1, [tw4], [tw4])
        MEMSET("pool", cmk[:], 0.0, [cmk])
        ASEL(cmk[:], cmk[:], [[1, SEQ]], ALU.is_ge, NEGM, -31, -16, [cmk], [cmk])
        MEMSET("pool", e12[:], 1.0, [e12])
        ASEL(e12[:], e12[:], [[-1, 12], [0, 64]], ALU.is_equal, 0.0, 0, 1, [e12], [e12])
        CP("dve", egate[:], e12[:], [e12], [egate])
        S.barrier()

    def norm_T(A, xsrc, xname, row0, ntiles, gain_vec, dst, dcol0):
        h32 = A.t([128, DM], F32, "h32")
        grep = A.t([128, DM], F32, "grep")
        xin = [A.t([128, DM], F32, "xin%d" % i) for i in range(2)]
        sqj = A.t([128, DM], BF16, "sqj")
        hb = [A.t([128, DM], BF16, "hb%d" % i) for i in range(2)]
        st = [A.t([128, 4], F32, "st%d" % i) for i in range(2)]
        DMA("sp", grep[:], gain_vec.rearrange("(o d) -> o d", o=1).partition_broadcast(128), [], [grep])
        for t in range(ntiles):
            xi, h, s = xin[t % 2], hb[t % 2], st[t % 2]
            r0 = row0 + 128 * t
            DMA("sp", xi[:], xsrc[r0:r0 + 128, :], [DR(xname, r0 // 128)], [xi])
            ACT(sqj[:], xi[:], AF.Square, [xi], [sqj, s], accum=s[:, 0:1])
            ACT(s[:, 1:2], s[:, 0:1], AF.Sqrt, [s], [s], bias=EPS, scale=1.0 / DM)
            RECIP(s[:, 2:3], s[:, 1:2], [s], [s])
            STT(h[:], xi[:], s[:, 2:3], grep[:], ALU.mult, ALU.mult, [xi, s, grep], [h])
            for c in range(8):
                TR(pst[:, c, :], h[:, 128 * c:128 * (c + 1)], identb[:], [h, identb], [pst])
            CP("act", dst[:, :, dcol0 + 128 * t:dcol0 + 128 * (t + 1)], pst[:], [pst], [dst])
            if t == 0:
                STT(h32[:], xi[:], s[:, 2:3], grep[:], ALU.mult, ALU.mult, [xi, s, grep], [h32])
                for half in range(2):
                    pp = P[half]
                    for c in range(4):
                        TR(pp[:, 128 * c:128 * (c + 1)], h32[:, 128 * (4 * half + c):128 * (4 * half + c + 1)], identf[:],
                           [h32, identf], [pp])
                    CP("act", hTf0[:, 4 * half:4 * half + 4, :], pp[:, :].rearrange("p (c n) -> p c n", c=4), [pp], [hTf0])

    def fm_rmsnorm(ps, rows, n, gain, out_ap, out_res, sqb, rstd, ps2, ones_t):
        gain_col = gain[0:rows, 0:1]
        ACT(sqb[0:rows, 0:n], ps[0:rows, 0:n], AF.Square, [ps], [sqb])
        MM(ps2[0:rows, 0:n], ones_t[0:rows, 0:rows], sqb[0:rows, 0:n], True, True, [ones_t, sqb], [ps2])
        ACT(rstd[0:rows, 0:n], ps2[0:rows, 0:n], AF.Sqrt, [ps2], [rstd], bias=EPS, scale=1.0 / rows)
        RECIP(rstd[0:rows, 0:n], rstd[0:rows, 0:n], [rstd], [rstd])
        STT(out_ap, ps[0:rows, 0:n], gain_col, rstd[0:rows, 0:n], ALU.mult, ALU.mult, [ps, rstd, gain], [out_res])

    def attn_group(g, hb_idx, q_res, k_res, krows, v_res, v_ap_fn, kts, window, pnum, pden, pts, scnt):
        first = True
        for ki, kt in enumerate(kts):
            r = kt - 4 * g
            lo = max(r, 0)
            hi = min(r + 4, 3) if window else 3
            c0, c1 = 128 * lo, 128 * (hi + 1)
            ps = P[scnt[0] % 3]
            pt = pts[scnt[0] % 3]
            scnt[0] += 1
            biases = []
            if 0 <= r <= 3:
                biases.append((128 * r, tb[:, hb_idx, 0, :]))
            if 0 <= r + 1 <= 3:
                biases.append((128 * (r + 1), tb[:, hb_idx, 1, :]))
            if window and 0 <= r + 4 <= 3:
                biases.append((128 * (r + 4), tw4[:]))
            MM(ps[:, c0:c1], k_res[0:krows, 128 * kt:128 * (kt + 1)], q_res[0:krows, 512 * g + c0:512 * g + c1],
               True, len(biases) == 0, [k_res, q_res], [ps])
            for bi, (bc, bt) in enumerate(biases):
                MM(ps[:, bc:bc + 128], identf[:], bt, False, bi == len(biases) - 1, [identf, tb, tw4], [ps])
            ACT(pt[:, c0:c1], ps[:, c0:c1], AF.Exp, [ps, cb], [pt], bias=cb[:, hb_idx:hb_idx + 1], scale=0.125)
            last = ki == len(kts) - 1
            MM(pnum[0:64, c0:c1], v_ap_fn(kt), pt[:, c0:c1], first, last, [v_res, pt], [pnum])
            MM(pden[0:64, c0:c1], onesb[:, 0:64], pt[:, c0:c1], first, last, [onesb, pt], [pden])
            first = False

    def load_col(A, vec_ap, n, name):
        t = A.t([n, 1], F32, name)
        DMA("sp", t[:], vec_ap.rearrange("(d o) -> d o", o=1), [], [t])
        return t

    def sel_rows(kaug, nrows, blk):
        v = kaug[64:64 + nrows, :]
        MEMSET("pool", v, 8192.0, [kaug])
        ASEL(v, v, [[1, SEQ]], ALU.is_ge, 0.0, 0, -blk, [kaug], [kaug])
        ASEL(v, v, [[-1, SEQ]], ALU.is_ge, 0.0, blk - 1, blk, [kaug], [kaug])

    def phase_moba(l, s):
        A = Alloc(S, PH_BASE)
        wA = A.t([128, 8, 768], BF16, "wA")
        for k_ in range(8):
            DMA("pool", wA[:, k_, :], I["w_in"][l, 128 * k_:128 * (k_ + 1), 0:768], [], [wA])
        gq = load_col(A, I["moba_q_norm"][l], 64, "gq")
        gk = load_col(A, I["moba_k_norm"][l], 64, "gk")
        qaug = [A.t([72, SEQ], BF16, "qaug%d" % h) for h in range(4)]
        kaug = [A.t([72, SEQ], BF16, "kaug%d" % h) for h in range(4)]
        V = A.t([128, 16, 256], BF16, "V")
        kmf = A.t([64, 4, 8], F32, "kmf")
        kmb = A.t([64, 4, 8], BF16, "kmb")
        sqb = [A.t([64, 512], BF16, "sqb%d" % i) for i in range(2)]
        rstd = [A.t([64, 512], F32, "rstd%d" % i) for i in range(2)]
        gsb = A.t([128, 4, 8], F32, "gsb")
        m8 = A.t([128, 4, 8], F32, "m8")
        selm = [A.t([128, 4, 8], F32, "selm%d" % i) for i in range(2)]
        mts = A.t([32, SEQ], BF16, "mts")
        pts = [A.t([128, 512], BF16, "pt%d" % i) for i in range(3)]
        rden = [A.t([64, 512], F32, "rden%d" % i) for i in range(2)]
        ob = [A.t([64, 512], BF16, "ob%d" % i) for i in range(2)]
        for h in range(4):
            sel_rows(kaug[h], 8, 256)
        n = 0
        for g in range(4):
            cols = slice(512 * g, 512 * (g + 1))
            for h in range(4):
                for (c0, dst, gain) in ((64 * h, qaug[h], gq), (256 + 64 * h, kaug[h], gk)):
                    ps = P[n % 2]
                    ps2 = P[2 + n % 2]
                    for k in range(8):
                        MM(ps[0:64, :], wA[:, k, c0:c0 + 64], hT[:, k, cols], k == 0, k == 7, [wA, hT], [ps])
                    fm_rmsnorm(ps, 64, 512, gain, dst[0:64, cols], dst, sqb[n % 2], rstd[n % 2], ps2, onesb)
                    n += 1
        for t in range(16):
            ps = P[4 + t % 2]
            for k in range(8):
                MM(ps[:, 0:256], hT[:, k, 128 * t:128 * (t + 1)], wA[:, k, 512:768], k == 0, k == 7, [hT, wA], [ps])
            CP("act", V[:, t, :], ps[:, 0:256], [ps], [V])
        for h in range(4):
            S.add("dve", lambda e, h=h: e.tensor_reduce(out=kmf[:, h, :], in_=kaug[h][0:64, :].rearrange("p (b j) -> p b j", j=256),
                                                        axis=AX.X, op=ALU.add), reads=[kaug[h]], writes=[kmf])
        CP("dve", kmb[:], kmf[:], [kmf], [kmb])
        for t in range(16):
            ps = P[t % 2]
            sm = selm[t % 2]
            for h in range(4):
                MM(ps[:, 8 * h:8 * h + 8], qaug[h][0:64, 128 * t:128 * (t + 1)], kmb[:, h, :], True, True, [qaug[h], kmb], [ps])
            TT("dve", gsb[:], ps[:, 0:32].rearrange("p (h b) -> p h b", h=4), pbt[:, t], ALU.add, [ps, pbt], [gsb])
            for h in range(4):
                S.add("dve", lambda e, h=h: e.max(out=m8[:, h, :], in_=gsb[:, h, :]), reads=[gsb], writes=[m8])
            for h in range(4):
                TS("dve", sm[:, h, :], gsb[:, h, :], m8[:, h, 2:3], None, ALU.is_ge, None, [gsb, m8], [sm])
            TT("dve", sm[:], sm[:], ownt[:, t], ALU.max, [sm, ownt], [sm])
            TS("dve", sm[:], sm[:], -1.0, None, ALU.add, None, [sm], [sm])
            pt_ = P[2 + t % 2]
            TR(pt_[0:32, 0:128], sm[:].rearrange("p h b -> p (h b)"), identf[:], [sm, identf], [pt_])
            CP("act", mts[:, 128 * t:128 * (t + 1)], pt_[0:32, 0:128], [pt_], [mts])
        for h in range(4):
            DMA("sp", qaug[h][64:72, :], mts[8 * h:8 * h + 8, :], [mts], [qaug[h]])
        scnt = [0]
        n = 0
        for h in range(4):
            for g in range(4):
                pnum, pden = P[3 + 2 * (n % 2)], P[4 + 2 * (n % 2)]
                attn_group(g, h, qaug[h], kaug[h], 72, V, lambda kt, h=h: V[:, kt, 64 * h:64 * h + 64],
                           list(range(4 * g + 4)), False, pnum, pden, pts, scnt)
                rd, o = rden[n % 2], ob[n % 2]
                RECIP(rd[:], pden[0:64, :], [pden], [rd])
                TT("dve", o[:], pnum[0:64, :], rd[:], ALU.mult, [pnum, rd], [o])
                c0 = NTOK // 2 * s + 512 * g
                DMA("sp", oM[h, :, c0:c0 + 512], o[:], [o], [DR("oM", (s, g))])
                n += 1

    def phase_nsa(l, s):
        A = Alloc(S, PH_BASE)
        wB = A.t([128, 8, 652], BF16, "wB")
        for k_ in range(8):
            DMA("pool", wB[:, k_, :], I["w_in"][l, 128 * k_:128 * (k_ + 1), 768:1420], [], [wB])
        gq = load_col(A, I["nsa_q_norm"][l], 64, "gq")
        gkc = load_col(A, I["nsa_k_norm"][l, 0], 64, "gkc")
        gks = load_col(A, I["nsa_k_norm"][l, 1], 64, "gks")
        gkw = load_col(A, I["nsa_k_norm"][l, 2], 64, "gkw")
        w1 = [A.t([64, 32, 64], BF16, "w1_%d" % i) for i in range(2)]
        w2 = [A.t([64, 64], BF16, "w2_%d" % i) for i in range(2)]
        posr = [A.t([32, 64], F32, "posr%d" % i) for i in range(2)]
        posT = [A.t([64, 32], F32, "posT%d" % i) for i in range(2)]
        for i, (a, b, c) in enumerate((("cmp_k_w1", "cmp_k_w2", "cmp_pos_k"), ("cmp_v_w1", "cmp_v_w2", "cmp_pos_v"))):
            DMA("pool", w1[i][:], I[a][l].rearrange("(l d) o -> d l o", d=64), [], [w1[i]])
            DMA("pool", w2[i][:], I[b][l], [], [w2[i]])
            DMA("sp", posr[i][:], I[c][l], [], [posr[i]])
        nqaug = [A.t([96, SEQ], BF16, "nqaug%d" % h) for h in range(4)]
        ksaug = A.t([96, SEQ], BF16, "ksaug")
        kwT = A.t([64, SEQ], BF16, "kwT")
        kcT = A.t([64, SEQ], BF16, "kcT")
        vcT = A.t([64, SEQ], BF16, "vcT")
        vs = A.t([128, 16, 64], BF16, "vs")
        vw = A.t([128, 16, 64], BF16, "vw")
        sgT = A.t([12, SEQ], BF16, "sgT")
        kcs = A.t([64, 32, 127], BF16, "kcs")
        gl = A.t([64, 128], BF16, "gl")
        kcn = A.t([64, 128], BF16, "kcn")
        vcm = A.t([128, 64], F32, "vcm")
        sqb = [A.t([64, 512], BF16, "sqb%d" % i) for i in range(2)]
        rstd = [A.t([64, 512], F32, "rstd%d" % i) for i in range(2)]
        ptf = [A.t([128, 512], F32, "ptf%d" % i) for i in range(2)]
        rdr = A.t([128, 512], F32, "rdr")
        impT = A.t([32, 512], F32, "impT")
        vv = A.t([128, 32], F32, "vv")
        vv2 = A.t([128, 32], F32, "vv2")
        m8a = A.t([128, 8], F32, "m8a")
        m8b = A.t([128, 8], F32, "m8b")
        selm = [A.t([128, 32], F32, "selm%d" % i) for i in range(2)]
        mts = A.t([32, 512], BF16, "mts")
        pts = [A.t([128, 512], BF16, "pt%d" % i) for i in range(3)]
        gr = [A.t([64, 512], F32, "gr%d" % i) for i in range(2)]
        rden = [A.t([64, 512], F32, "rden%d" % i) for i in range(2)]
        tmp = [A.t([64, 512], F32, "tmp%d" % i) for i in range(2)]
        oacc = [A.t([64, 512], F32, "oacc%d" % h) for h in range(4)]
        ob = [A.t([64, 512], BF16, "ob%d" % i) for i in range(2)]
        sel_rows(ksaug, 32, 64)
        for i in range(2):
            TR(P[6][0:64, 0:32], posr[i][:], identf[0:32, 0:32], [posr[i], identf], [P[6]])
            CP("act", posT[i][:], P[6][0:64, 0:32], [P[6]], [posT[i]])
        n = 0
        for g in range(4):
            cols = slice(512 * g, 512 * (g + 1))
            outs = [(64 * h, nqaug[h], gq) for h in range(4)] + [(384, ksaug, gks), (512, kwT, gkw), (256, kcT, None), (320, vcT, None)]
            for (c0, dst, gain) in outs:
                ps = P[n % 2]
                ps2 = P[2 + n % 2]
                for k in range(8):
                    MM(ps[0:64, :], wB[:, k, c0:c0 + 64], hT[:, k, cols], k == 0, k == 7, [wB, hT], [ps])
                if gain is None:
                    CP("act", dst[0:64, cols], ps[0:64, :], [ps], [dst])
                else:
                    fm_rmsnorm(ps, 64, 512, gain, dst[0:64, cols], dst, sqb[n % 2], rstd[n % 2], ps2, onesb)
                n += 1
            ps = P[n % 2]
            n += 1
            for k in range(8):
                MM(ps[0:12, :], wB[:, k, 640:652], hT[:, k, cols], k == 0, k == 7, [wB, hT], [ps])
            ACT(sgT[:, cols], ps[0:12, :], AF.Sigmoid, [ps], [sgT])
        for t in range(16):
            ps = P[4 + t % 2]
            for k in range(8):
                MM(ps[:, 0:64], hT[:, k, 128 * t:128 * (t + 1)], wB[:, k, 448:512], k == 0, k == 7, [hT, wB], [ps])
            for k in range(8):
                MM(ps[:, 64:128], hT[:, k, 128 * t:128 * (t + 1)], wB[:, k, 576:640], k == 0, k == 7, [hT, wB], [ps])
            CP("act", vs[:, t, :], ps[:, 0:64], [ps], [vs])
            CP("act", vw[:, t, :], ps[:, 64:128], [ps], [vw])
        for i, src in enumerate((kcT, vcT)):
            for ll in range(32):
                TS("dve", kcs[:, ll, :], src[0:64, ll:ll + 2017:16], posT[i][:, ll:ll + 1], None, ALU.add, None,
                   [src, posT[i]], [kcs])
            ps = P[0]
            for ll in range(32):
                MM(ps[0:64, 0:127], w1[i][:, ll, :], kcs[:, ll, :], ll == 0, ll == 31, [w1[i], kcs], [ps])
            ACT(gl[:, 0:127], ps[0:64, 0:127], AF.Gelu_apprx_tanh, [ps], [gl])
            if i == 0:
                MM(P[1][0:64, 0:127], w2[0][:], gl[:, 0:127], True, True, [w2[0], gl], [P[1]])
                fm_rmsnorm(P[1], 64, 127, gkc, kcn[:, 0:127], kcn, sqb[0], rstd[0], P[2], onesb)
            else:
                MM(P[1][0:127, 0:64], gl[:, 0:127], w2[1][:], True, True, [w2[1], gl], [P[1]])
                CP("act", vcm[0:127, :], P[1][0:127, 0:64], [P[1]], [vcm])
        scnt = [0]
        nn = 0
        for g in range(4):
            cols = slice(512 * g, 512 * (g + 1))
            for h in range(4):
                ps = P[h % 2]
                pf = ptf[h % 2]
                MM(ps[0:127, :], kcn[:, 0:127], nqaug[h][0:64, cols], True, False, [kcn, nqaug[h]], [ps])
                MM(ps[0:127, :], identb[0:127, 0:127], cmk[0:127, cols], False, True, [identb, cmk], [ps])
                ACT(pf[0:127, :], ps[0:127, :], AF.Exp, [ps], [pf], scale=0.125)
                MM(P[2][:, :], onesf[0:127, :], pf[0:127, :], True, True, [onesf, pf], [P[2]])
                TS("dve", rdr[:], P[2][:, :], 1e-30, None, ALU.add, None, [P[2]], [rdr])
                RECIP(rdr[:], rdr[:], [rdr], [rdr])
                TT("dve", pf[0:127, :], pf[0:127, :], rdr[0:127, :], ALU.mult, [pf, rdr], [pf])
                MM(P[3][0:32, :], ovl[0:127, :], pf[0:127, :], h == 0, h == 3, [ovl, pf], [P[3]])
                MM(P[4][0:64, :], vcm[0:127, :], pf[0:127, :], True, True, [vcm, pf], [P[4]])
                gt = gr[h % 2]
                MM(P[5][0:64, :], egate[:, 3 * h + 0, :], sgT[:, cols], True, True, [egate, sgT], [P[5]])
                CP("act", gt[:], P[5][0:64, :], [P[5]], [gt])
                TT("dve", oacc[h][:], P[4][0:64, :], gt[:], ALU.mult, [P[4], gt], [oacc[h]])
            CP("act", impT[:], P[3][0:32, :], [P[3]], [impT])
            for tl in range(4):
                t = 4 * g + tl
                sm = selm[tl % 2]
                pp = P[tl % 2]
                TR(pp[:, 0:32], impT[:, 128 * tl:128 * (tl + 1)], identf[0:32, 0:32], [impT, identf], [pp])
                TT("dve", vv[:], pp[:, 0:32], fbt[:, t, :], ALU.add, [pp, fbt], [vv])
                S.add("dve", lambda e: e.max(out=m8a[:], in_=vv[:]), reads=[vv], writes=[m8a])
                S.add("dve", lambda e: e.match_replace(out=vv2[:], in_to_replace=m8a[:], in_values=vv[:], imm_value=-1e9),
                      reads=[vv, m8a], writes=[vv2])
                S.add("dve", lambda e: e.max(out=m8b[:], in_=vv2[:]), reads=[vv2], writes=[m8b])
                TS("dve", sm[:], vv[:], m8b[:, 7:8], -1.0, ALU.is_ge, ALU.add, [vv, m8b], [sm])
                pq = P[2 + tl % 2]
                TR(pq[0:32, 0:128], sm[:], identf[:], [sm, identf], [pq])
                CP("act", mts[:, 128 * tl:128 * (tl + 1)], pq[0:32, 0:128], [pq], [mts])
            for h in range(4):
                DMA("sp", nqaug[h][64:96, cols], mts[:, :], [mts], [nqaug[h]])
            for h in range(4):
                for br in (1, 2):
                    pnum, pden = P[3 + 2 * (nn % 2)], P[4 + 2 * (nn % 2)]
                    if br == 1:
                        attn_group(g, 4 + h, nqaug[h], ksaug, 96, vs, lambda kt: vs[:, kt, :],
                                   list(range(4 * g + 4)), False, pnum, pden, pts, scnt)
                    else:
                        attn_group(g, 4 + h, nqaug[h], kwT, 64, vw, lambda kt: vw[:, kt, :],
                                   list(range(max(0, 4 * g - 4), 4 * g + 4)), True, pnum, pden, pts, scnt)
                    rd, gt, tp = rden[nn % 2], gr[nn % 2], tmp[nn % 2]
                    MM(P[0][0:64, :], egate[:, 3 * h + br, :], sgT[:, cols], True, True, [egate, sgT], [P[0]])
                    RECIP(rd[:], pden[0:64, :], [pden], [rd])
                    TT("dve", rd[:], rd[:], P[0][0:64, :], ALU.mult, [rd, P[0]], [rd])
                    TT("dve", tp[:], pnum[0:64, :], rd[:], ALU.mult, [pnum, rd], [tp])
                    if br == 1:
                        TT("pool", oacc[h][:], oacc[h][:], tp[:], ALU.add, [oacc[h], tp], [oacc[h]])
                    else:
                        o = ob[h % 2]
                        TT("pool", o[:], oacc[h][:], tp[:], ALU.add, [oacc[h], tp], [o])
                        c0 = NTOK // 2 * s + 512 * g
                        DMA("sp", oN[h, :, c0:c0 + 512], o[:], [o], [DR("oN", (s, g))])
                    nn += 1

    def phase_gla(l, s):
        A = Alloc(S, PH_BASE)
        wC = A.t([128, 8, 1552], BF16, "wC")
        for k_ in range(8):
            DMA("pool", wC[:, k_, :], I["w_in"][l, 128 * k_:128 * (k_ + 1), 1420:2972], [], [wC])
        gwb = A.t([16, 256], BF16, "gwb")
        DMA("pool", gwb[:], I["gla_gate_w"][l], [], [gwb])
        gbias = A.t([128, 2], F32, "gbias")
        for c in range(2):
            DMA("sp", gbias[:, c:c + 1], I["gla_gate_b"][l, 128 * c:128 * (c + 1)].rearrange("(d o) -> d o", o=1), [], [gbias])
        ngb = A.t([128, 2], F32, "ngb")
        gon = load_col(A, I["gla_out_norm"][l], 128, "gon")
        gqT = A.t([128, 2, SEQ], BF16, "gqT")
        gkT = A.t([128, 2, SEQ], BF16, "gkT")
        gv = A.t([128, 16, 512], BF16, "gv")
        glrT = A.t([16, SEQ], BF16, "glrT")
        so = A.t([128, 4, SEQ], BF16, "so")
        csp = A.t([128, 2, SEQ], F32, "csp")
        ex = [A.t([128, 512], F32, "ex%d" % i) for i in range(2)]
        eb = [A.t([128, 128], F32, "eb%d" % i) for i in range(2)]
        ek = [A.t([128, 128], F32, "ek%d" % i) for i in range(2)]
        qt = [A.t([128, 128], BF16, "qt%d" % i) for i in range(2)]
        kt_ = [A.t([128, 128], BF16, "kt%d" % i) for i in range(2)]
        ktm = [A.t([128, 128], BF16, "ktm%d" % i) for i in range(2)]
        am = [A.t([128, 128], BF16, "am%d" % i) for i in range(4)]
        Sf = [A.t([128, 128], F32, "Sf%d" % i) for i in range(2)]
        Sb = [A.t([128, 128], BF16, "Sb%d" % i) for i in range(2)]
        sqb = [A.t([128, 512], BF16, "sqb%d" % i) for i in range(2)]
        rstd = [A.t([128, 512], F32, "rstd%d" % i) for i in range(2)]
        og = [A.t([128, 512], BF16, "og%d" % i) for i in range(2)]
        S.add("act", lambda e: e.mul(ngb[:], gbias[:], -1.0), reads=[gbias], writes=[ngb])
        wqk = A.t([128, 8, 512], F32, "wqk")
        for k_ in range(8):
            DMA("sp", wqk[:, k_, :], I["w_in"][l, 128 * k_:128 * (k_ + 1), 1420:1932], [], [wqk])
        q0f = A.t([128, 2, 128], F32, "q0f")
        k0f = A.t([128, 2, 128], F32, "k0f")
        qf = [A.t([128, 128], F32, "qf%d" % i) for i in range(2)]
        kf = [A.t([128, 128], F32, "kf%d" % i) for i in range(2)]
        n = 0
        for c in range(2):
            ps = P[n % 2]; n += 1
            for k in range(8):
                MM(ps[:, 0:128], wqk[:, k, 128 * c:128 * (c + 1)], hTf0[:, k, :], k == 0, k == 7, [wqk, hTf0], [ps])
            ACT(q0f[:, c, :], ps[:, 0:128], AF.Identity, [ps], [q0f], scale=0.125)
            ps = P[n % 2]; n += 1
            for k in range(8):
                MM(ps[:, 0:128], wqk[:, k, 256 + 128 * c:256 + 128 * (c + 1)], hTf0[:, k, :], k == 0, k == 7, [wqk, hTf0], [ps])
            CP("act", k0f[:, c, :], ps[:, 0:128], [ps], [k0f])
        for g in range(4):
            cols = slice(512 * g, 512 * (g + 1))
            for c in range(2):
                ps = P[n % 2]; n += 1
                for k in range(8):
                    MM(ps[:, :], wC[:, k, 128 * c:128 * (c + 1)], hT[:, k, cols], k == 0, k == 7, [wC, hT], [ps])
                ACT(gqT[:, c, cols], ps[:, :], AF.Identity, [ps], [gqT], scale=0.125)
                ps = P[n % 2]; n += 1
                for k in range(8):
                    MM(ps[:, :], wC[:, k, 256 + 128 * c:256 + 128 * (c + 1)], hT[:, k, cols], k == 0, k == 7, [wC, hT], [ps])
                CP("act", gkT[:, c, cols], ps[:, :], [ps], [gkT])
            ps = P[n % 2]; n += 1
            for k in range(8):
                MM(ps[0:16, :], wC[:, k, 1024:1040], hT[:, k, cols], k == 0, k == 7, [wC, hT], [ps])
            CP("act", glrT[:, cols], ps[0:16, :], [ps], [glrT])
            for c in range(4):
                ps = P[n % 2]; n += 1
                for k in range(8):
                    MM(ps[:, :], wC[:, k, 1040 + 128 * c:1040 + 128 * (c + 1)], hT[:, k, cols], k == 0, k == 7, [wC, hT], [ps])
                ACT(so[:, c, cols], ps[:, :], AF.Silu, [ps], [so])
            for c in range(2):
                ps = P[2 + c]
                e_ = ex[c]
                MM(ps[:, :], gwb[:, 128 * c:128 * (c + 1)], glrT[:, cols], True, True, [gwb, glrT], [ps])
                ACT(e_[:], ps[:, :], AF.Exp, [ps, ngb], [e_], bias=ngb[:, c:c + 1], scale=-1.0)
                ACT(e_[:], e_[:], AF.Ln, [e_], [e_], bias=1.0)
                for j in range(4):
                    cc = slice(512 * g + 128 * j, 512 * g + 128 * (j + 1))
                    S.add("dve", lambda e, c=c, cc=cc, j=j, e_=e_: e.tensor_tensor_scan(
                        out=csp[:, c, cc], data0=onesf[:, :], data1=e_[:, 128 * j:128 * (j + 1)], initial=0.0,
                        op0=ALU.mult, op1=ALU.add), reads=[e_, onesf], writes=[csp])
        for t in range(16):
            ps = P[4 + t % 2]
            for k in range(8):
                MM(ps[:, :], hT[:, k, 128 * t:128 * (t + 1)], wC[:, k, 512:1024], k == 0, k == 7, [hT, wC], [ps])
            CP("act", gv[:, t, :], ps[:, :], [ps], [gv])
        for c in range(2):
            MEMSET("pool", Sf[c][:], 0.0, [Sf[c]])
            MEMSET("pool", Sb[c][:], 0.0, [Sb[c]])
        S.barrier()
        po = [P[0], P[1], P[2], P[3]]
        pat = [Res("patA", P[4][:, 0:128]), Res("patB", P[4][:, 128:256])]
        pds = P[5]
        pss = P[6]
        for j in range(16):
            cc = slice(128 * j, 128 * (j + 1))
            jl = j % 4
            for c in range(2):
                b_, k_, q_, kk, km = eb[c], ek[c], qt[c], kt_[c], ktm[c]
                ACT(b_[:], csp[:, c, cc], AF.Exp, [csp], [b_], scale=-1.0 / 16)
                ACT(k_[:], csp[:, c, cc], AF.Exp, [csp], [k_], scale=1.0 / 16)
                TT("dve", q_[:], gqT[:, c, cc], b_[:], ALU.mult, [gqT, b_], [q_])
                TT("dve", kk[:], gkT[:, c, cc], k_[:], ALU.mult, [gkT, k_], [kk])
                TR(pst[:, c, :], kk[:], identb[:], [kk, identb], [pst])
                CP("act", km[:], pst[:, c, :], [pst], [km])
                if j == 0:
                    TT("dve", qf[c][:], q0f[:, c, :], b_[:], ALU.mult, [q0f, b_], [qf[c]])
                    TT("dve", kf[c][:], k0f[:, c, :], k_[:], ALU.mult, [k0f, k_], [kf[c]])
                for hh in range(2):
                    h = 2 * c + hh
                    rows = slice(64 * hh, 64 * (hh + 1))
                    a_ = am[2 * c + hh]
                    if j == 0:
                        MM(pat[hh][:, :], kf[c][rows, :], qf[c][rows, :], True, True, [kf[c], qf[c]], [pat[hh]])
                    else:
                        MM(pat[hh][:, :], kk[rows, :], q_[rows, :], True, True, [kk, q_], [pat[hh]])
                    TT("dve", a_[:], pat[hh][:, :], cmask[:], ALU.mult, [pat[hh], cmask], [a_])
                    MM(po[h][:, 128 * jl:128 * (jl + 1)], gv[:, j, 128 * h:128 * (h + 1)], a_[:], True, False, [gv, a_], [po[h]])
                    MM(po[h][:, 128 * jl:128 * (jl + 1)], Sb[c][rows, :], q_[rows, :], False, True, [Sb[c], q_], [po[h]])
                    MM(pds[rows, 0:128], km[:, rows], gv[:, j, 128 * h:128 * (h + 1)], True, True, [km, gv], [pds])
                TS("dve", Sf[c][:], Sf[c][:], b_[:, 127:128], None, ALU.mult, None, [Sf[c], b_], [Sf[c]])
                STT(Sf[c][:], pds[:, 0:128], b_[:, 127:128], Sf[c][:], ALU.mult, ALU.add, [pds, b_, Sf[c]], [Sf[c]])
                CP("act", Sb[c][:], Sf[c][:], [Sf[c]], [Sb[c]])
            if jl == 3:
                g = j // 4
                cols = slice(512 * g, 512 * (g + 1))
                for h in range(4):
                    sq_, rs_, o_ = sqb[h % 2], rstd[h % 2], og[h % 2]
                    ACT(sq_[:], po[h][:, :], AF.Square, [po[h]], [sq_])
                    MM(pss[:, :], onesb[:, :], sq_[:], True, True, [onesb, sq_], [pss])
                    ACT(rs_[:], pss[:, :], AF.Sqrt, [pss], [rs_], bias=EPS, scale=1.0 / 128)
                    RECIP(rs_[:], rs_[:], [rs_], [rs_])
                    STT(rs_[:], po[h][:, :], gon[:, 0:1], rs_[:], ALU.mult, ALU.mult, [po[h], gon, rs_], [rs_])
                    TT("dve", o_[:], rs_[:], so[:, h, cols], ALU.mult, [rs_, so], [o_])
                    c0 = NTOK // 2 * s + 512 * g
                    DMA("sp", oG[h, :, c0:c0 + 512], o_[:], [o_], [DR("oG", (s, g))])

    def phase_merge(l, s, xsrc, xsname, xdst, xdname):
        A = Alloc(S, PH_BASE)
        wM = A.t([128, 8, 3072], BF16, "wM")
        for k_ in range(8):
            DMA("pool", wM[:, k_, :], I["w_in"][l, 128 * k_:128 * (k_ + 1), 2972:6044], [], [wM])
        wbm = A.t([64, 4, DM], BF16, "wbm")
        wbn = A.t([64, 4, DM], BF16, "wbn")
        wbg = A.t([128, 4, DM], BF16, "wbg")
        wo = A.t([128, 8, DM], BF16, "wo")
        DMA("pool", wbm[:], I["w_branch_moba"][l].rearrange("(h d) n -> d h n", d=64), [], [wbm])
        DMA("pool", wbn[:], I["w_branch_nsa"][l].rearrange("(h d) n -> d h n", d=64), [], [wbn])
        DMA("pool", wbg[:], I["w_branch_gla"][l].rearrange("(h d) n -> d h n", d=128), [], [wbg])
        for k_ in range(8):
            DMA("pool", wo[:, k_, :], I["w_out"][l, 128 * k_:128 * (k_ + 1), :], [], [wo])
        om = [A.t([64, 4, 512], BF16, "om%d" % i) for i in range(2)]
        on = [A.t([64, 4, 512], BF16, "on%d" % i) for i in range(2)]
        ogt = [A.t([128, 4, 512], BF16, "ogt%d" % i) for i in range(2)]
        gt = [A.t([128, 512], F32, "gt%d" % i) for i in range(3)]
        t3 = [A.t([128, 512], F32, "t3_%d" % i) for i in range(3)]
        zT = A.t([128, 8, 512], BF16, "zT")
        xin = [A.t([128, DM], F32, "xin%d" % i) for i in range(2)]
        for g in range(4):
            cols = slice(512 * g, 512 * (g + 1))
            c0 = NTOK // 2 * s + 512 * g
            o1, o2, o3 = om[g % 2], on[g % 2], ogt[g % 2]
            DMA("sp", o1[:], oM[:, :, c0:c0 + 512].rearrange("h d n -> d h n"), [DR("oM", (s, g))], [o1])
            DMA("sp", o2[:], oN[:, :, c0:c0 + 512].rearrange("h d n -> d h n"), [DR("oN", (s, g))], [o2])
            DMA("sp", o3[:], oG[:, :, c0:c0 + 512].rearrange("h d n -> d h n"), [DR("oG", (s, g))], [o3])
            for c in range(8):
                for b in range(3):
                    ps = P[b]
                    for k in range(8):
                        MM(ps[:, :], wM[:, k, 1024 * b + 128 * c:1024 * b + 128 * (c + 1)], hT[:, k, cols], k == 0, k == 7,
                           [wM, hT], [ps])
                    ACT(gt[b][:], ps[:, :], AF.Sigmoid, [ps], [gt[b]])
                for b, (wb, ot) in enumerate(((wbm, o1), (wbn, o2), (wbg, o3))):
                    ps = P[3 + b]
                    rows = 64 if b < 2 else 128
                    for h in range(4):
                        MM(ps[:, :], wb[0:rows, h, 128 * c:128 * (c + 1)], ot[0:rows, h, :], h == 0, h == 3, [wb, ot], [ps])
                    TT("dve", t3[b][:], ps[:, :], gt[b][:], ALU.mult, [ps, gt[b]], [t3[b]])
                TT("pool", t3[0][:], t3[0][:], t3[1][:], ALU.add, [t3[0], t3[1]], [t3[0]])
                TT("pool", zT[:, c, :], t3[0][:], t3[2][:], ALU.add, [t3[0], t3[2]], [zT])
            for tl in range(4):
                r0 = NTOK // 2 * s + 512 * g + 128 * tl
                xi = xin[tl % 2]
                DMA("sp", xi[:], xsrc[r0:r0 + 128, :], [DR(xsname, r0 // 128)], [xi])
                for hf in range(2):
                    ps = P[6] if hf == 0 else P[0]
                    for c in range(8):
                        MM(ps[:, :], zT[:, c, 128 * tl:128 * (tl + 1)], wo[:, c, 512 * hf:512 * (hf + 1)], c == 0, c == 7,
                           [zT, wo], [ps])
                    TT("dve", xi[:, 512 * hf:512 * (hf + 1)], xi[:, 512 * hf:512 * (hf + 1)], ps[:, :], ALU.add, [xi, ps], [xi])
                DMA("sp", xdst[r0:r0 + 128, :], xi[:], [xi], [DR(xdname, r0 // 128)])

    def phase_ffn(l, xsrc, xsname, xdst, xdname, seq_list):
        A = Alloc(S, CONST_END)
        wu = A.t([128, 8, 2 * DFF], BF16, "wu")
        wd = A.t([128, 22, DM], BF16, "wd")
        for k in range(8):
            DMA("pool", wu[:, k, :], I["w_up"][l, 128 * k:128 * (k + 1), :], [], [wu])
        for c in range(0, 22, 2):
            DMA("pool", wd[:, c:c + 2, :], I["w_down"][l, 128 * c:128 * (c + 2), :].rearrange("(c p) n -> p c n", p=128), [], [wd])
        cwr = A.t([22, 4, 128], F32, "cwr")
        cw = A.t([128, 4, 22], F32, "cw")
        for j in range(3):
            DMA("sp", cwr[:, j, :], I["conv_w"][l, j].rearrange("(c p) -> c p", p=128), [], [cwr])
        DMA("sp", cwr[:, 3, :], I["conv_b"][l].rearrange("(c p) -> c p", p=128), [], [cwr])
        for j in range(4):
            TR(P[6][:, 22 * j:22 * (j + 1)], cwr[:, j, :], identf[0:22, 0:22], [cwr, identf], [P[6]])
        CP("act", cw[:].rearrange("p j c -> p (j c)"), P[6][:, 0:88], [P[6]], [cw])
        h2T = A.t([128, 8, 256], BF16, "h2T")
        uT = A.t([128, 22, 256], BF16, "uT")
        ab = [A.t([128, 258], F32, "ab%d" % i) for i in range(2)]
        t1 = [A.t([128, 256], F32, "t1_%d" % i) for i in range(2)]
        ge = [A.t([128, 256], F32, "ge%d" % i) for i in range(2)]
        halo = A.t([128, 22, 2], F32, "halo")
        grep = A.t([128, DM], F32, "grep")
        xin = [A.t([128, DM], F32, "xin%d" % i) for i in range(2)]
        sqj = A.t([128, DM], BF16, "sqj")
        hb = [A.t([128, DM], BF16, "hb%d" % i) for i in range(2)]
        st = [A.t([128, 4], F32, "st%d" % i) for i in range(2)]
        DMA("sp", grep[:], I["ffn_norm"][l].rearrange("(o d) -> o d", o=1).partition_broadcast(128), [], [grep])
        n = 0
        for s in seq_list:
            MEMSET("pool", halo[:], 0.0, [halo])
            for gi in range(8):
                row0 = NTOK // 2 * s + 256 * gi
                for t in range(2):
                    xi, h, s_ = xin[t], hb[t], st[t]
                    r0 = row0 + 128 * t
                    DMA("sp", xi[:], xsrc[r0:r0 + 128, :], [DR(xsname, r0 // 128)], [xi])
                    ACT(sqj[:], xi[:], AF.Square, [xi], [sqj, s_], accum=s_[:, 0:1])
                    ACT(s_[:, 1:2], s_[:, 0:1], AF.Sqrt, [s_], [s_], bias=EPS, scale=1.0 / DM)
                    RECIP(s_[:, 2:3], s_[:, 1:2], [s_], [s_])
                    STT(h[:], xi[:], s_[:, 2:3], grep[:], ALU.mult, ALU.mult, [xi, s_, grep], [h])
                    for c in range(8):
                        TR(pst[:, c, :], h[:, 128 * c:128 * (c + 1)], identb[:], [h, identb], [pst])
                    CP("act", h2T[:, :, 128 * t:128 * (t + 1)], pst[:], [pst], [h2T])
                for c in range(22):
                    pa, pg = P[2 * (n % 2)], P[2 * (n % 2) + 1]
                    a_, t_, g_ = ab[n % 2], t1[n % 2], ge[n % 2]
                    n += 1
                    for k in range(8):
                        MM(pa[:, 0:256], wu[:, k, 128 * c:128 * (c + 1)], h2T[:, k, :], k == 0, k == 7, [wu, h2T], [pa])
                    for k in range(8):
                        MM(pg[:, 0:256], wu[:, k, DFF + 128 * c:DFF + 128 * (c + 1)], h2T[:, k, :], k == 0, k == 7, [wu, h2T], [pg])
                    CP("pool", a_[:, 0:2], halo[:, c, :], [halo], [a_])
                    CP("act", a_[:, 2:258], pa[:, 0:256], [pa], [a_])
                    CP("pool", halo[:, c, :], a_[:, 256:258], [a_], [halo])
                    TS("dve", t_[:], a_[:, 2:258], cw[:, 2, c:c + 1], cw[:, 3, c:c + 1], ALU.mult, ALU.add, [a_, cw], [t_])
                    STT(t_[:], a_[:, 1:257], cw[:, 1, c:c + 1], t_[:], ALU.mult, ALU.add, [a_, cw, t_], [t_])
                    STT(t_[:], a_[:, 0:256], cw[:, 0, c:c + 1], t_[:], ALU.mult, ALU.add, [a_, cw, t_], [t_])
                    ACT(g_[:], t_[:], AF.Gelu_apprx_tanh, [t_], [g_])
                    TT("dve", uT[:, c, :], g_[:], pg[:, 0:256], ALU.mult, [g_, pg], [uT])
                for t in range(2):
                    xi = xin[t]
                    r0 = row0 + 128 * t
                    for hf in range(2):
                        ps = P[4 + hf]
                        for c in range(22):
                            MM(ps[:, :], uT[:, c, 128 * t:128 * (t + 1)], wd[:, c, 512 * hf:512 * (hf + 1)], c == 0, c == 21,
                               [uT, wd], [ps])
                        TT("dve", xi[:, 512 * hf:512 * (hf + 1)], xi[:, 512 * hf:512 * (hf + 1)], ps[:, :], ALU.add, [xi, ps], [xi])
                    DMA("sp", xdst[r0:r0 + 128, :], xi[:], [xi], [DR(xdname, r0 // 128)])

    prologue()
    cur, cname = I["x"], "x"
    for l in layers:
        mid, mname = (xA, "xA")
        for s in seqs:
            if "N" in phases:
                A = Alloc(S, PH_BASE)
                norm_T(A, cur, cname, NTOK // 2 * s, 16, I["attn_norm"][l], hT, 0)
                S.barrier()
            if "A" in phases:
                phase_moba(l, s)
                S.barrier()
            if "B" in phases:
                phase_nsa(l, s)
                S.barrier()
            if "C" in phases:
                phase_gla(l, s)
                S.barrier()
            if "M" in phases:
                phase_merge(l, s, cur, cname, mid, mname)
                S.barrier()
        last = (l == layers[-1])
        dst, dname = (yout, "y") if (last and not dbg) else (xB, "xB")
        if "F" in phases:
            phase_ffn(l, mid, mname, dst, dname, seqs)
            S.barrier()
        cur, cname = dst, dname
    S.emit()
    return nc, consts


_CACHE = {}


def kernel(**inputs):
    n = 8
    x = np.ascontiguousarray(np.asarray(inputs["x"], dtype=np.float32))
    if "prog" not in _CACHE:
        _CACHE["prog"] = build_program()
    nc, consts = _CACHE["prog"]
    base = {k: np.ascontiguousarray(np.asarray(inputs[k], dtype=np.float32)) for k in W_SHAPES}
    base.update(consts)
    in_maps = []
    for i in range(n):
        m = dict(base)
        m["x"] = x[2 * i:2 * i + 2].reshape(NTOK, DM)
        in_maps.append(m)
    res = run_bass_kernel_spmd(nc, in_maps, core_ids=list(range(n)))
    out = np.stack([np.asarray(r["y"], dtype=np.float32).reshape(2, SEQ, DM) for r in res.results], 0)
    return out.reshape(16, SEQ, DM)
```

```python
import math
from contextlib import ExitStack
import numpy as np
import concourse.bass as bass
import concourse.mybir as mybir
from concourse.bass_utils import run_bass_kernel_spmd

F32 = mybir.dt.float32
BF16 = mybir.dt.bfloat16
AF = mybir.ActivationFunctionType
ALU = mybir.AluOpType
AX = mybir.AxisListType

SEQ = 2048
DM = 1024
NTOK = 4096
DFF = 2816
NEGM = -8192.0
EPS = 1e-6


class Res:
    __slots__ = ("name", "w", "rs", "t")

    def __init__(self, name, t=None):
        self.name = name
        self.w = None
        self.rs = {}
        self.t = t

    def __getitem__(self, k):
        return self.t[k]


class Sched:
    ENG = ("pe", "act", "dve", "pool", "sp")

    def __init__(self, nc, n_dma_sems=20):
        self.nc = nc
        self.ops = {e: [] for e in self.ENG}
        self.cnt = {e: 0 for e in self.ENG}
        self.waited = {e: {} for e in self.ENG}
        self.n_dma_sems = n_dma_sems
        self.dma_cnt = {}
        self.dma_rr = {"sp": 0, "pool": 0, "act": 0}
        self.ntile = 0

    def make_arena(self, nbytes):
        self.arena_t = self.nc.alloc_sbuf_tensor("arena", [128, nbytes], mybir.dt.uint8)
        self.arena_n = nbytes

    def tile(self, shape, dtype, name, offset):
        self.ntile += 1
        name = "%s_%d" % (name, self.ntile)
        esz = 4 if dtype == F32 else 2
        n = int(np.prod(shape[1:]))
        assert offset % 4 == 0 and offset + n * esz <= self.arena_n, (name, offset, n * esz, self.arena_n)
        v = self.arena_t[0:shape[0], offset:offset + n * esz].bitcast(dtype)
        if len(shape) == 3:
            v = v.rearrange("p (a b) -> p a b", a=shape[1])
        elif len(shape) == 4:
            v = v.rearrange("p (a b c) -> p a b c", a=shape[1], b=shape[2])
        return Res(name, v)

    def psum(self, shape, dtype, name):
        self.ntile += 1
        return Res(name, self.nc.alloc_psum_tensor("%s_%d" % (name, self.ntile), list(shape), dtype))

    def _deps(self, reads, writes):
        deps = {}
        for r in reads:
            if r.w is not None:
                k, v = r.w
                if deps.get(k, 0) < v:
                    deps[k] = v
        for w in writes:
            if w.w is not None:
                k, v = w.w
                if deps.get(k, 0) < v:
                    deps[k] = v
            for k, v in w.rs.items():
                if deps.get(k, 0) < v:
                    deps[k] = v
        return deps

    def add(self, eng, fn, reads=(), writes=()):
        deps = self._deps(reads, writes)
        waits = []
        wd = self.waited[eng]
        for k, v in deps.items():
            if wd.get(k, 0) >= v:
                continue
            if eng == "pe" and k == "pe":
                continue
            waits.append((k, v))
            wd[k] = v
        self.cnt[eng] += 1
        key, val = eng, self.cnt[eng]
        self.ops[eng].append((waits, fn, key, 1))
        for r in reads:
            if r.rs.get(key, 0) < val:
                r.rs[key] = val
        for w in writes:
            w.w = (key, val)
            w.rs = {}

    def dma(self, q, fn, reads=(), writes=()):
        deps = self._deps(reads, writes)
        i = self.dma_rr[q]
        self.dma_rr[q] = (i + 1) % self.n_dma_sems
        key = "d_%s_%d" % (q, i)
        prev = self.dma_cnt.get(key, 0)
        if prev:
            deps[key] = max(deps.get(key, 0), 16 * prev)
        waits = []
        wd = self.waited[q]
        for k, v in deps.items():
            if wd.get(k, 0) >= v:
                continue
            waits.append((k, v))
            wd[k] = v
        self.dma_cnt[key] = prev + 1
        val = 16 * (prev + 1)
        self.ops[q].append((waits, fn, key, 16))
        for r in reads:
            if r.rs.get(key, 0) < val:
                r.rs[key] = val
        for w in writes:
            w.w = (key, val)
            w.rs = {}

    def barrier(self):
        tot = {e: self.cnt[e] for e in self.ENG if self.cnt[e]}
        for k, c in self.dma_cnt.items():
            tot[k] = 16 * c
        for e in self.ENG:
            waits = []
            wd = self.waited[e]
            for k, v in tot.items():
                if k == e or wd.get(k, 0) >= v:
                    continue
                waits.append((k, v))
                wd[k] = v
            if waits:
                self.ops[e].append((waits, None, None, 0))

    def emit(self):
        nc = self.nc
        keys = list(self.ENG) + sorted(self.dma_cnt.keys())
        with ExitStack() as es:
            sems = {k: es.enter_context(nc.semaphore("s_" + k)) for k in keys}
            block = es.enter_context(nc.Block())
            decs = {"pe": block.tensor, "act": block.scalar, "dve": block.vector,
                    "pool": block.gpsimd, "sp": block.sync}
            final = {k: 16 * c for k, c in self.dma_cnt.items()}
            for e in self.ENG:
                def body(engine, ops=self.ops[e], e=e):
                    for waits, fn, key, inc in ops:
                        for k, v in waits:
                            engine.wait_ge(sems[k], v)
                        if fn is None:
                            continue
                        fn(engine).then_inc(sems[key], inc)
                    if e == "sp":
                        for k, v in final.items():
                            engine.wait_ge(sems[k], v)
                        for k in self.ENG:
                            if k != "sp" and self.cnt[k]:
                                engine.wait_ge(sems[k], self.cnt[k])
                decs[e](body)


class Alloc:
    def __init__(self, S, base):
        self.S = S
        self.off = base

    def t(self, shape, dtype, name):
        esz = 4 if dtype == F32 else 2
        nb = (int(np.prod(shape[1:])) * esz + 63) // 64 * 64
        r = self.S.tile(shape, dtype, name, self.off)
        self.off += nb
        return r


def _rel_bucket(dist):
    n = np.maximum(dist, 0)
    nf = np.maximum(n, 1).astype(np.float32)
    large = 16 + (np.log(nf / np.float32(16)) / np.float32(math.log(128 / 16)) * np.float32(16)).astype(np.int32)
    large = np.minimum(large, 31)
    return np.where(n < 16, n, large)


def make_consts():
    c = {}
    m = np.arange(384) - 128
    c["c_onehot"] = (_rel_bucket(m)[None, :] == np.arange(32)[:, None]).astype(np.float32)
    tt = np.arange(16)
    blk = np.arange(8)
    pb = np.where(blk[None, :] < (tt // 2)[:, None], 0.0, -30000.0).astype(np.float32)
    own = (blk[None, :] == (tt // 2)[:, None]).astype(np.float32)
    c["c_pb"] = np.ascontiguousarray(np.broadcast_to(pb[None, :, None, :], (128, 16, 4, 8))).astype(np.float32)
    c["c_own"] = np.ascontiguousarray(np.broadcast_to(own[None, :, None, :], (128, 16, 4, 8))).astype(np.float32)
    t = np.arange(SEQ)
    cur = t // 64
    b32 = np.arange(32)
    forced = (b32[None, :] == 0) | (b32[None, :] == cur[:, None]) | (b32[None, :] == cur[:, None] - 1)
    fb = np.where(b32[None, :] <= cur[:, None], np.where(forced, 1e4, 0.0), -30000.0).astype(np.float32)
    c["c_fb"] = np.ascontiguousarray(fb.reshape(16, 128, 32).transpose(1, 0, 2))
    starts = np.arange(127) * 16
    tok = np.arange(32 * 64)
    inside = (tok[None, :] >= starts[:, None]) & (tok[None, :] < starts[:, None] + 32)
    ovl = (inside.reshape(127, 32, 64).sum(-1) / 32).astype(np.float32)
    c["c_ovl"] = np.concatenate([ovl, np.zeros((1, 32), np.float32)], 0)
    j = np.arange(128)
    c["c_cmask"] = (j[None, :] >= j[:, None]).astype(np.float32)
    return c


W_SHAPES = {
    "rel_bias": (32, 8), "attn_norm": (2, 1024), "w_in": (2, 1024, 6044), "moba_q_norm": (2, 64),
    "moba_k_norm": (2, 64), "nsa_q_norm": (2, 64), "nsa_k_norm": (2, 3, 64), "cmp_pos_k": (2, 32, 64),
    "cmp_pos_v": (2, 32, 64), "cmp_k_w1": (2, 2048, 64), "cmp_k_w2": (2, 64, 64), "cmp_v_w1": (2, 2048, 64),
    "cmp_v_w2": (2, 64, 64), "gla_gate_w": (2, 16, 256), "gla_gate_b": (2, 256), "gla_out_norm": (2, 128),
    "w_branch_moba": (2, 256, 1024), "w_branch_nsa": (2, 256, 1024), "w_branch_gla": (2, 512, 1024),
    "w_out": (2, 1024, 1024), "ffn_norm": (2, 1024), "w_up": (2, 1024, 5632), "conv_w": (2, 3, 2816),
    "conv_b": (2, 2816), "w_down": (2, 2816, 1024),
}


def build_program(layers=(0, 1), seqs=(0, 1), dbg=False, phases="NABCMF"):
    nc = bass.Bass("TRN2", target_bir_lowering=False)
    consts = make_consts()
    I = {}
    I["x"] = nc.dram_tensor("x", [NTOK, DM], F32, kind="ExternalInput").ap()
    for k, shp in W_SHAPES.items():
        I[k] = nc.dram_tensor(k, list(shp), F32, kind="ExternalInput").ap()
    for k, v in consts.items():
        I[k] = nc.dram_tensor(k, list(v.shape), F32, kind="ExternalInput").ap()
    yout = nc.dram_tensor("y", [NTOK, DM], F32, kind="ExternalOutput").ap()
    skind = "ExternalOutput" if dbg else "Internal"
    xA = nc.dram_tensor("xA", [NTOK, DM], F32, kind=skind).ap()
    xB = nc.dram_tensor("xB", [NTOK, DM], F32, kind=skind).ap()
    oM = nc.dram_tensor("oM", [4, 64, NTOK], BF16, kind=skind).ap()
    oN = nc.dram_tensor("oN", [4, 64, NTOK], BF16, kind=skind).ap()
    oG = nc.dram_tensor("oG", [4, 128, NTOK], BF16, kind=skind).ap()
    Zh = nc.dram_tensor("Zsk", [8, 128, 384], F32)
    Zap = Zh.ap()

    S = Sched(nc)
    S.make_arena(207 * 1024)

    dres = {}

    def DR(name, idx):
        k = (name, idx)
        if k not in dres:
            dres[k] = Res("%s_%s" % (name, idx))
        return dres[k]

    def MM(out, lhsT, rhs, start, stop, rd, wr):
        S.add("pe", lambda e: e.matmul(out, lhsT=lhsT, rhs=rhs, start=start, stop=stop), reads=rd, writes=wr)

    def TR(out, in_, ident, rd, wr):
        S.add("pe", lambda e: e.transpose(out, in_, ident), reads=rd, writes=wr)

    def ACT(out, in_, func, rd, wr, bias=None, scale=None, accum=None):
        kw = {}
        if bias is not None:
            kw["bias"] = bias
        if scale is not None:
            kw["scale"] = scale
        if accum is not None:
            kw["accum_out"] = accum
        S.add("act", lambda e: e.activation(out, in_, func, **kw), reads=rd, writes=wr)

    def TT(eng, out, in0, in1, op, rd, wr):
        S.add(eng, lambda e: e.tensor_tensor(out=out, in0=in0, in1=in1, op=op), reads=rd, writes=wr)

    def TS(eng, out, in0, s1, s2, op0, op1, rd, wr):
        if s2 is None:
            S.add(eng, lambda e: e.tensor_scalar(out=out, in0=in0, scalar1=s1, scalar2=None, op0=op0), reads=rd, writes=wr)
        else:
            S.add(eng, lambda e: e.tensor_scalar(out=out, in0=in0, scalar1=s1, scalar2=s2, op0=op0, op1=op1), reads=rd, writes=wr)

    def STT(out, in0, scalar, in1, op0, op1, rd, wr):
        S.add("dve", lambda e: e.scalar_tensor_tensor(out=out, in0=in0, scalar=scalar, in1=in1, op0=op0, op1=op1),
              reads=rd, writes=wr)

    def CP(eng, out, in_, rd, wr):
        if eng == "act":
            S.add("act", lambda e: e.copy(out, in_), reads=rd, writes=wr)
        else:
            S.add(eng, lambda e: e.tensor_copy(out=out, in_=in_), reads=rd, writes=wr)

    def RECIP(out, in_, rd, wr):
        S.add("dve", lambda e: e.reciprocal(out, in_), reads=rd, writes=wr)

    def MEMSET(eng, out, val, wr):
        S.add(eng, lambda e: e.memset(out, val), writes=wr)

    def ASEL(out, in_, pattern, op, fill, base, cm, rd, wr):
        S.add("pool", lambda e: e.affine_select(out, in_, pattern, op, fill, base=base, channel_multiplier=cm),
              reads=rd, writes=wr)

    def DMA(q, out, in_, rd, wr):
        S.dma(q, lambda e: e.dma_start(out=out, in_=in_), reads=rd, writes=wr)

    P = [S.psum([128, 512], F32, "pb%d" % i) for i in range(7)]
    pst = S.psum([128, 8, 128], BF16, "pst")

    A0 = Alloc(S, 0)
    identb = A0.t([128, 128], BF16, "identb")
    identf = A0.t([128, 128], F32, "identf")
    onesb = A0.t([128, 128], BF16, "onesb")
    onesf = A0.t([128, 128], F32, "onesf")
    tb = A0.t([128, 8, 2, 128], F32, "tb")
    tw4 = A0.t([128, 128], F32, "tw4")
    cb = A0.t([128, 8], F32, "cb")
    cmk = A0.t([128, SEQ], BF16, "cmk")
    pbt = A0.t([128, 16, 4, 8], F32, "pbt")
    ownt = A0.t([128, 16, 4, 8], F32, "ownt")
    fbt = A0.t([128, 16, 32], F32, "fbt")
    ovl = A0.t([128, 32], F32, "ovl")
    cmask = A0.t([128, 128], F32, "cmask")
    egate = A0.t([12, 12, 64], BF16, "egate")
    CONST_END = A0.off
    hT = A0.t([128, 8, SEQ], BF16, "hT")
    hTf0 = A0.t([128, 8, 128], F32, "hTf0")
    PH_BASE = A0.off

    def prologue():
        A = Alloc(S, PH_BASE)
        tab = A.t([32, 8], F32, "tab")
        oneh = A.t([32, 384], F32, "oneh")
        tbc = A.t([32, 128], F32, "tbc")
        frs = [A.t([128, 384], F32, "frs%d" % i) for i in range(2)]
        e12 = A.t([12, 12, 64], F32, "e12")
        MEMSET("pool", identf[:], 0.0, [identf])
        ASEL(identf[:], identf[:], [[-1, 128]], ALU.not_equal, 1.0, 0, 1, [identf], [identf])
        CP("dve", identb[:], identf[:], [identf], [identb])
        MEMSET("dve", onesb[:], 1.0, [onesb])
        MEMSET("dve", onesf[:], 1.0, [onesf])
        DMA("sp", tab[:], I["rel_bias"], [], [tab])
        DMA("sp", oneh[:], I["c_onehot"], [], [oneh])
        DMA("sp", pbt[:], I["c_pb"], [], [pbt])
        DMA("sp", ownt[:], I["c_own"], [], [ownt])
        DMA("sp", fbt[:], I["c_fb"], [], [fbt])
        DMA("sp", ovl[:], I["c_ovl"], [], [ovl])
        DMA("sp", cmask[:], I["c_cmask"], [], [cmask])
        zres = Res("zsk")
        for h in range(8):
            fr = frs[h % 2]
            ACT(tbc[:], onesf[0:32, :], AF.Identity, [onesf, tab], [tbc], scale=tab[:, h:h + 1])
            MM(P[0][:, 0:384], tbc[:], oneh[:], True, True, [tbc, oneh], [P[0]])
            CP("act", cb[:, h:h + 1], P[0][:, 383:384], [P[0]], [cb])
            TS("dve", fr[:], P[0][:, 0:384], cb[:, h:h + 1], 8.0, ALU.subtract, ALU.mult, [P[0], cb], [fr])
            DMA("sp", Zap[h], fr[:], [fr], [zres])
        S.barrier()
        for h in range(8):
            for j in range(2):
                src = bass.AP(tensor=Zh, offset=h * 128 * 384 + 128 * (j + 1), ap=[[383, 128], [1, 128]])
                DMA("sp", tb[:, h, j, :], src, [zres], [tb])
        ASEL(tb[:, :, 0, :], tb[:, :, 0, :], [[0, 8], [1, 128]], ALU.is_ge, NEGM, 0, -1, [tb], [tb])
        MEMSET("pool", tw4[:], 0.0, [tw4])
        ASEL(tw4[:], tw4[:], [[-1, 128]], ALU.is_ge, NEGM, -1, 1, [tw4], [tw4])
        MEMSET("pool", cmk[:], 0.0, [cmk])
        ASEL(cmk[:], cmk[:], [[1, SEQ]], ALU.is_ge, NEGM, -31, -16, [cmk], [cmk])
        MEMSET("pool", e12[:], 1.0, [e12])
        ASEL(e12[:], e12[:], [[-1, 12], [0, 64]], ALU.is_equal, 0.0, 0, 1, [e12], [e12])
        CP("dve", egate[:], e12[:], [e12], [egate])
        S.barrier()

    def norm_T(A, xsrc, xname, row0, ntiles, gain_vec, dst, dcol0):
        h32 = A.t([128, DM], F32, "h32")
        grep = A.t([128, DM], F32, "grep")
        xin = [A.t([128, DM], F32, "xin%d" % i) for i in range(2)]
        sqj = A.t([128, DM], BF16, "sqj")
        hb = [A.t([128, DM], BF16, "hb%d" % i) for i in range(2)]
        st = [A.t([128, 4], F32, "st%d" % i) for i in range(2)]
        DMA("sp", grep[:], gain_vec.rearrange("(o d) -> o d", o=1).partition_broadcast(128), [], [grep])
        for t in range(ntiles):
            xi, h, s = xin[t % 2], hb[t % 2], st[t % 2]
            r0 = row0 + 128 * t
            DMA("sp", xi[:], xsrc[r0:r0 + 128, :], [DR(xname, r0 // 128)], [xi])
            ACT(sqj[:], xi[:], AF.Square, [xi], [sqj, s], accum=s[:, 0:1])
            ACT(s[:, 1:2], s[:, 0:1], AF.Sqrt, [s], [s], bias=EPS, scale=1.0 / DM)
            RECIP(s[:, 2:3], s[:, 1:2], [s], [s])
            STT(h[:], xi[:], s[:, 2:3], grep[:], ALU.mult, ALU.mult, [xi, s, grep], [h])
            for c in range(8):
                TR(pst[:, c, :], h[:, 128 * c:128 * (c + 1)], identb[:], [h, identb], [pst])
            CP("act", dst[:, :, dcol0 + 128 * t:dcol0 + 128 * (t + 1)], pst[:], [pst], [dst])
            if t == 0:
                STT(h32[:], xi[:], s[:, 2:3], grep[:], ALU.mult, ALU.mult, [xi, s, grep], [h32])
                for half in range(2):
                    pp = P[half]
                    for c in range(4):
                        TR(pp[:, 128 * c:128 * (c + 1)], h32[:, 128 * (4 * half + c):128 * (4 * half + c + 1)], identf[:],
                           [h32, identf], [pp])
                    CP("act", hTf0[:, 4 * half:4 * half + 4, :], pp[:, :].rearrange("p (c n) -> p c n", c=4), [pp], [hTf0])

    def fm_rmsnorm(ps, rows, n, gain, out_ap, out_res, sqb, rstd, ps2, ones_t):
        gain_col = gain[0:rows, 0:1]
        ACT(sqb[0:rows, 0:n], ps[0:rows, 0:n], AF.Square, [ps], [sqb])
        MM(ps2[0:rows, 0:n], ones_t[0:rows, 0:rows], sqb[0:rows, 0:n], True, True, [ones_t, sqb], [ps2])
        ACT(rstd[0:rows, 0:n], ps2[0:rows, 0:n], AF.Sqrt, [ps2], [rstd], bias=EPS, scale=1.0 / rows)
        RECIP(rstd[0:rows, 0:n], rstd[0:rows, 0:n], [rstd], [rstd])
        STT(out_ap, ps[0:rows, 0:n], gain_col, rstd[0:rows, 0:n], ALU.mult, ALU.mult, [ps, rstd, gain], [out_res])

    def attn_group(g, hb_idx, q_res, k_res, krows, v_res, v_ap_fn, kts, window, pnum, pden, pts, scnt):
        LOOK = 2
        items = []
        for kt in kts:
            r = kt - 4 * g
            lo = max(r, 0)
            hi = min(r + 4, 3) if window else 3
            biases = []
            if 0 <= r <= 3:
                biases.append((128 * r, tb[:, hb_idx, 0, :]))
            if 0 <= r + 1 <= 3:
                biases.append((128 * (r + 1), tb[:, hb_idx, 1, :]))
            if window and 0 <= r + 4 <= 3:
                biases.append((128 * (r + 4), tw4[:]))
            items.append((kt, 128 * lo, 128 * (hi + 1), biases))
        bufs = {}

        def qk(i):
            kt, c0, c1, biases = items[i]
            ps = P[scnt[0] % 3]
            pt = pts[scnt[0] % len(pts)]
            scnt[0] += 1
            bufs[i] = pt
            MM(ps[:, c0:c1], k_res[0:krows, 128 * kt:128 * (kt + 1)], q_res[0:krows, 512 * g + c0:512 * g + c1],
               True, len(biases) == 0, [k_res, q_res], [ps])
            for bi, (bc, bt) in enumerate(biases):
                MM(ps[:, bc:bc + 128], identf[:], bt, False, bi == len(biases) - 1, [identf, tb, tw4], [ps])
            ACT(pt[:, c0:c1], ps[:, c0:c1], AF.Exp, [ps, cb], [pt], bias=cb[:, hb_idx:hb_idx + 1], scale=0.125)

        def pv(i):
            kt, c0, c1, biases = items[i]
            pt = bufs[i]
            MM(pnum[0:64, c0:c1], v_ap_fn(kt), pt[:, c0:c1], i == 0, i == len(items) - 1, [v_res, pt], [pnum])
            MM(pden[0:64, c0:c1], onesb[:, 0:64], pt[:, c0:c1], i == 0, i == len(items) - 1, [onesb, pt], [pden])

        for i in range(len(items) + LOOK):
            if i < len(items):
                qk(i)
            if i - LOOK >= 0:
                pv(i - LOOK)

    def load_col(A, vec_ap, n, name):
        t = A.t([n, 1], F32, name)
        DMA("sp", t[:], vec_ap.rearrange("(d o) -> d o", o=1), [], [t])
        return t

    def sel_rows(kaug, nrows, blk):
        v = kaug[64:64 + nrows, :]
        MEMSET("pool", kaug[64:128, :], 0.0, [kaug])
        MEMSET("pool", v, 8192.0, [kaug])
        ASEL(v, v, [[1, SEQ]], ALU.is_ge, 0.0, 0, -blk, [kaug], [kaug])
        ASEL(v, v, [[-1, SEQ]], ALU.is_ge, 0.0, blk - 1, blk, [kaug], [kaug])

    def phase_moba(l, s):
        A = Alloc(S, PH_BASE)
        wA = A.t([128, 8, 768], BF16, "wA")
        for k_ in range(8):
            DMA("pool", wA[:, k_, :], I["w_in"][l, 128 * k_:128 * (k_ + 1), 0:768], [], [wA])
        gq = load_col(A, I["moba_q_norm"][l], 64, "gq")
        gk = load_col(A, I["moba_k_norm"][l], 64, "gk")
        qaug = [A.t([128, SEQ], BF16, "qaug%d" % h) for h in range(4)]
        kaug = [A.t([128, SEQ], BF16, "kaug%d" % h) for h in range(4)]
        V = A.t([128, 16, 256], BF16, "V")
        kmf = A.t([64, 4, 8], F32, "kmf")
        kmb = A.t([64, 4, 8], BF16, "kmb")
        sqb = [A.t([64, 512], BF16, "sqb%d" % i) for i in range(2)]
        rstd = [A.t([64, 512], F32, "rstd%d" % i) for i in range(2)]
        gsb = A.t([128, 4, 8], F32, "gsb")
        m8 = A.t([128, 4, 8], F32, "m8")
        selm = [A.t([128, 4, 8], F32, "selm%d" % i) for i in range(2)]
        mts = A.t([32, SEQ], BF16, "mts")
        pts = [A.t([128, 512], BF16, "pt%d" % i) for i in range(4)]
        rden = [A.t([64, 512], F32, "rden%d" % i) for i in range(2)]
        ob = [A.t([64, 512], BF16, "ob%d" % i) for i in range(2)]
        for h in range(4):
            sel_rows(kaug[h], 8, 256)
            MEMSET("pool", qaug[h][64:128, :], 0.0, [qaug[h]])
        n = 0
        for g in range(4):
            cols = slice(512 * g, 512 * (g + 1))
            for h in range(4):
                for (c0, dst, gain) in ((64 * h, qaug[h], gq), (256 + 64 * h, kaug[h], gk)):
                    ps = P[n % 2]
                    ps2 = P[2 + n % 2]
                    for k in range(8):
                        MM(ps[0:64, :], wA[:, k, c0:c0 + 64], hT[:, k, cols], k == 0, k == 7, [wA, hT], [ps])
                    fm_rmsnorm(ps, 64, 512, gain, dst[0:64, cols], dst, sqb[n % 2], rstd[n % 2], ps2, onesb)
                    n += 1
        for t in range(16):
            ps = P[4 + t % 2]
            for k in range(8):
                MM(ps[:, 0:256], hT[:, k, 128 * t:128 * (t + 1)], wA[:, k, 512:768], k == 0, k == 7, [hT, wA], [ps])
            CP("act", V[:, t, :], ps[:, 0:256], [ps], [V])
        for h in range(4):
            S.add("dve", lambda e, h=h: e.tensor_reduce(out=kmf[:, h, :], in_=kaug[h][0:64, :].rearrange("p (b j) -> p b j", j=256),
                                                        axis=AX.X, op=ALU.add), reads=[kaug[h]], writes=[kmf])
        CP("dve", kmb[:], kmf[:], [kmf], [kmb])
        for t in range(16):
            ps = P[t % 2]
            sm = selm[t % 2]
            for h in range(4):
                MM(ps[:, 8 * h:8 * h + 8], qaug[h][0:64, 128 * t:128 * (t + 1)], kmb[:, h, :], True, True, [qaug[h], kmb], [ps])
            TT("dve", gsb[:], ps[:, 0:32].rearrange("p (h b) -> p h b", h=4), pbt[:, t], ALU.add, [ps, pbt], [gsb])
            for h in range(4):
                S.add("dve", lambda e, h=h: e.max(out=m8[:, h, :], in_=gsb[:, h, :]), reads=[gsb], writes=[m8])
            for h in range(4):
                TS("dve", sm[:, h, :], gsb[:, h, :], m8[:, h, 2:3], None, ALU.is_ge, None, [gsb, m8], [sm])
            TT("dve", sm[:], sm[:], ownt[:, t], ALU.max, [sm, ownt], [sm])
            TS("dve", sm[:], sm[:], -1.0, None, ALU.add, None, [sm], [sm])
            pt_ = P[2 + t % 2]
            TR(pt_[0:32, 0:128], sm[:].rearrange("p h b -> p (h b)"), identf[:], [sm, identf], [pt_])
            CP("act", mts[:, 128 * t:128 * (t + 1)], pt_[0:32, 0:128], [pt_], [mts])
        for h in range(4):
            DMA("sp", qaug[h][64:72, :], mts[8 * h:8 * h + 8, :], [mts], [qaug[h]])
        scnt = [0]
        n = 0
        for h in range(4):
            for g in range(4):
                pnum, pden = P[3 + 2 * (n % 2)], P[4 + 2 * (n % 2)]
                attn_group(g, h, qaug[h], kaug[h], 128, V, lambda kt, h=h: V[:, kt, 64 * h:64 * h + 64],
                           list(range(4 * g + 4)), False, pnum, pden, pts, scnt)
                rd, o = rden[n % 2], ob[n % 2]
                RECIP(rd[:], pden[0:64, :], [pden], [rd])
                TT("dve", o[:], pnum[0:64, :], rd[:], ALU.mult, [pnum, rd], [o])
                c0 = NTOK // 2 * s + 512 * g
                DMA("sp", oM[h, :, c0:c0 + 512], o[:], [o], [DR("oM", (s, g))])
                n += 1

    def phase_nsa(l, s):
        A = Alloc(S, PH_BASE)
        wB = A.t([128, 8, 652], BF16, "wB")
        for k_ in range(8):
            DMA("pool", wB[:, k_, :], I["w_in"][l, 128 * k_:128 * (k_ + 1), 768:1420], [], [wB])
        gq = load_col(A, I["nsa_q_norm"][l], 64, "gq")
        gkc = load_col(A, I["nsa_k_norm"][l, 0], 64, "gkc")
        gks = load_col(A, I["nsa_k_norm"][l, 1], 64, "gks")
        gkw = load_col(A, I["nsa_k_norm"][l, 2], 64, "gkw")
        w1 = [A.t([64, 32, 64], BF16, "w1_%d" % i) for i in range(2)]
        w2 = [A.t([64, 64], BF16, "w2_%d" % i) for i in range(2)]
        posr = [A.t([32, 64], F32, "posr%d" % i) for i in range(2)]
        posT = [A.t([64, 32], F32, "posT%d" % i) for i in range(2)]
        for i, (a, b, c) in enumerate((("cmp_k_w1", "cmp_k_w2", "cmp_pos_k"), ("cmp_v_w1", "cmp_v_w2", "cmp_pos_v"))):
            DMA("pool", w1[i][:], I[a][l].rearrange("(l d) o -> d l o", d=64), [], [w1[i]])
            DMA("pool", w2[i][:], I[b][l], [], [w2[i]])
            DMA("sp", posr[i][:], I[c][l], [], [posr[i]])
        nqaug = [A.t([128, SEQ], BF16, "nqaug%d" % h) for h in range(4)]
        ksaug = A.t([128, SEQ], BF16, "ksaug")
        kwT = A.t([64, SEQ], BF16, "kwT")
        kcT = A.t([64, SEQ], BF16, "kcT")
        vcT = A.t([64, SEQ], BF16, "vcT")
        vs = A.t([128, 16, 64], BF16, "vs")
        vw = A.t([128, 16, 64], BF16, "vw")
        sgT = A.t([12, SEQ], BF16, "sgT")
        kcs = A.t([64, 32, 127], BF16, "kcs")
        gl = A.t([64, 128], BF16, "gl")
        kcn = A.t([64, 128], BF16, "kcn")
        vcm = A.t([128, 64], F32, "vcm")
        sqb = [A.t([64, 512], BF16, "sqb%d" % i) for i in range(2)]
        rstd = [A.t([64, 512], F32, "rstd%d" % i) for i in range(2)]
        ptf = [A.t([128, 512], F32, "ptf%d" % i) for i in range(2)]
        rdr = A.t([128, 512], F32, "rdr")
        impT = A.t([32, 512], F32, "impT")
        vv = A.t([128, 32], F32, "vv")
        vv2 = A.t([128, 32], F32, "vv2")
        m8a = A.t([128, 8], F32, "m8a")
        m8b = A.t([128, 8], F32, "m8b")
        selm = [A.t([128, 32], F32, "selm%d" % i) for i in range(2)]
        mts = A.t([32, 512], BF16, "mts")
        pts = [A.t([128, 512], BF16, "pt%d" % i) for i in range(4)]
        gr = [A.t([64, 512], F32, "gr%d" % i) for i in range(2)]
        rden = [A.t([64, 512], F32, "rden%d" % i) for i in range(2)]
        tmp = [A.t([64, 512], F32, "tmp%d" % i) for i in range(2)]
        oacc = [A.t([64, 512], F32, "oacc%d" % h) for h in range(4)]
        ob = [A.t([64, 512], BF16, "ob%d" % i) for i in range(2)]
        sel_rows(ksaug, 32, 64)
        for h in range(4):
            MEMSET("pool", nqaug[h][64:128, :], 0.0, [nqaug[h]])
        for i in range(2):
            TR(P[6][0:64, 0:32], posr[i][:], identf[0:32, 0:32], [posr[i], identf], [P[6]])
            CP("act", posT[i][:], P[6][0:64, 0:32], [P[6]], [posT[i]])
        n = 0
        for g in range(4):
            cols = slice(512 * g, 512 * (g + 1))
            outs = [(64 * h, nqaug[h], gq) for h in range(4)] + [(384, ksaug, gks), (512, kwT, gkw), (256, kcT, None), (320, vcT, None)]
            for (c0, dst, gain) in outs:
                ps = P[n % 2]
                ps2 = P[2 + n % 2]
                for k in range(8):
                    MM(ps[0:64, :], wB[:, k, c0:c0 + 64], hT[:, k, cols], k == 0, k == 7, [wB, hT], [ps])
                if gain is None:
                    CP("act", dst[0:64, cols], ps[0:64, :], [ps], [dst])
                else:
                    fm_rmsnorm(ps, 64, 512, gain, dst[0:64, cols], dst, sqb[n % 2], rstd[n % 2], ps2, onesb)
                n += 1
            ps = P[n % 2]
            n += 1
            for k in range(8):
                MM(ps[0:12, :], wB[:, k, 640:652], hT[:, k, cols], k == 0, k == 7, [wB, hT], [ps])
            ACT(sgT[:, cols], ps[0:12, :], AF.Sigmoid, [ps], [sgT])
        for t in range(16):
            ps = P[4 + t % 2]
            for k in range(8):
                MM(ps[:, 0:64], hT[:, k, 128 * t:128 * (t + 1)], wB[:, k, 448:512], k == 0, k == 7, [hT, wB], [ps])
            for k in range(8):
                MM(ps[:, 64:128], hT[:, k, 128 * t:128 * (t + 1)], wB[:, k, 576:640], k == 0, k == 7, [hT, wB], [ps])
            CP("act", vs[:, t, :], ps[:, 0:64], [ps], [vs])
            CP("act", vw[:, t, :], ps[:, 64:128], [ps], [vw])
        for i, src in enumerate((kcT, vcT)):
            for ll in range(32):
                TS("dve", kcs[:, ll, :], src[0:64, ll:ll + 2017:16], posT[i][:, ll:ll + 1], None, ALU.add, None,
                   [src, posT[i]], [kcs])
            ps = P[0]
            for ll in range(32):
                MM(ps[0:64, 0:127], w1[i][:, ll, :], kcs[:, ll, :], ll == 0, ll == 31, [w1[i], kcs], [ps])
            ACT(gl[:, 0:127], ps[0:64, 0:127], AF.Gelu_apprx_tanh, [ps], [gl])
            if i == 0:
                MM(P[1][0:64, 0:127], w2[0][:], gl[:, 0:127], True, True, [w2[0], gl], [P[1]])
                fm_rmsnorm(P[1], 64, 127, gkc, kcn[:, 0:127], kcn, sqb[0], rstd[0], P[2], onesb)
            else:
                MM(P[1][0:127, 0:64], gl[:, 0:127], w2[1][:], True, True, [w2[1], gl], [P[1]])
                CP("act", vcm[0:127, :], P[1][0:127, 0:64], [P[1]], [vcm])
        scnt = [0]
        nn = 0
        for g in range(4):
            cols = slice(512 * g, 512 * (g + 1))
            for h in range(4):
                ps = P[h % 2]
                pf = ptf[h % 2]
                MM(ps[0:127, :], kcn[:, 0:127], nqaug[h][0:64, cols], True, False, [kcn, nqaug[h]], [ps])
                MM(ps[0:127, :], identb[0:127, 0:127], cmk[0:127, cols], False, True, [identb, cmk], [ps])
                ACT(pf[0:127, :], ps[0:127, :], AF.Exp, [ps], [pf], scale=0.125)
                MM(P[2][:, :], onesf[0:127, :], pf[0:127, :], True, True, [onesf, pf], [P[2]])
                TS("dve", rdr[:], P[2][:, :], 1e-30, None, ALU.add, None, [P[2]], [rdr])
                RECIP(rdr[:], rdr[:], [rdr], [rdr])
                TT("dve", pf[0:127, :], pf[0:127, :], rdr[0:127, :], ALU.mult, [pf, rdr], [pf])
                MM(P[3][0:32, :], ovl[0:127, :], pf[0:127, :], h == 0, h == 3, [ovl, pf], [P[3]])
                MM(P[4][0:64, :], vcm[0:127, :], pf[0:127, :], True, True, [vcm, pf], [P[4]])
                gt = gr[h % 2]
                MM(P[5][0:64, :], egate[:, 3 * h + 0, :], sgT[:, cols], True, True, [egate, sgT], [P[5]])
                CP("act", gt[:], P[5][0:64, :], [P[5]], [gt])
                TT("dve", oacc[h][:], P[4][0:64, :], gt[:], ALU.mult, [P[4], gt], [oacc[h]])
            CP("act", impT[:], P[3][0:32, :], [P[3]], [impT])
            for tl in range(4):
                t = 4 * g + tl
                sm = selm[tl % 2]
                pp = P[tl % 2]
                TR(pp[:, 0:32], impT[:, 128 * tl:128 * (tl + 1)], identf[0:32, 0:32], [impT, identf], [pp])
                TT("dve", vv[:], pp[:, 0:32], fbt[:, t, :], ALU.add, [pp, fbt], [vv])
                S.add("dve", lambda e: e.max(out=m8a[:], in_=vv[:]), reads=[vv], writes=[m8a])
                S.add("dve", lambda e: e.match_replace(out=vv2[:], in_to_replace=m8a[:], in_values=vv[:], imm_value=-1e9),
                      reads=[vv, m8a], writes=[vv2])
                S.add("dve", lambda e: e.max(out=m8b[:], in_=vv2[:]), reads=[vv2], writes=[m8b])
                TS("dve", sm[:], vv[:], m8b[:, 7:8], -1.0, ALU.is_ge, ALU.add, [vv, m8b], [sm])
                pq = P[2 + tl % 2]
                TR(pq[0:32, 0:128], sm[:], identf[:], [sm, identf], [pq])
                CP("act", mts[:, 128 * tl:128 * (tl + 1)], pq[0:32, 0:128], [pq], [mts])
            for h in range(4):
                DMA("sp", nqaug[h][64:96, cols], mts[:, :], [mts], [nqaug[h]])
            for h in range(4):
                for br in (1, 2):
                    pnum, pden = P[3 + 2 * (nn % 2)], P[4 + 2 * (nn % 2)]
                    if br == 1:
                        attn_group(g, 4 + h, nqaug[h], ksaug, 128, vs, lambda kt: vs[:, kt, :],
                                   list(range(4 * g + 4)), False, pnum, pden, pts, scnt)
                    else:
                        attn_group(g, 4 + h, nqaug[h], kwT, 64, vw, lambda kt: vw[:, kt, :],
                                   list(range(max(0, 4 * g - 4), 4 * g + 4)), True, pnum, pden, pts, scnt)
                    rd, gt, tp = rden[nn % 2], gr[nn % 2], tmp[nn % 2]
                    MM(P[0][0:64, :], egate[:, 3 * h + br, :], sgT[:, cols], True, True, [egate, sgT], [P[0]])
                    RECIP(rd[:], pden[0:64, :], [pden], [rd])
                    TT("dve", rd[:], rd[:], P[0][0:64, :], ALU.mult, [rd, P[0]], [rd])
                    TT("dve", tp[:], pnum[0:64, :], rd[:], ALU.mult, [pnum, rd], [tp])
                    if br == 1:
                        TT("pool", oacc[h][:], oacc[h][:], tp[:], ALU.add, [oacc[h], tp], [oacc[h]])
                    else:
                        o = ob[h % 2]
                        TT("pool", o[:], oacc[h][:], tp[:], ALU.add, [oacc[h], tp], [o])
                        c0 = NTOK // 2 * s + 512 * g
                        DMA("sp", oN[h, :, c0:c0 + 512], o[:], [o], [DR("oN", (s, g))])
                    nn += 1

    def phase_gla(l, s):
        A = Alloc(S, PH_BASE)
        wC = A.t([128, 8, 1552], BF16, "wC")
        for k_ in range(8):
            DMA("pool", wC[:, k_, :], I["w_in"][l, 128 * k_:128 * (k_ + 1), 1420:2972], [], [wC])
        gwb = A.t([16, 256], BF16, "gwb")
        DMA("pool", gwb[:], I["gla_gate_w"][l], [], [gwb])
        gbias = A.t([128, 2], F32, "gbias")
        for c in range(2):
            DMA("sp", gbias[:, c:c + 1], I["gla_gate_b"][l, 128 * c:128 * (c + 1)].rearrange("(d o) -> d o", o=1), [], [gbias])
        ngb = A.t([128, 2], F32, "ngb")
        gon = load_col(A, I["gla_out_norm"][l], 128, "gon")
        gqT = A.t([128, 2, SEQ], BF16, "gqT")
        gkT = A.t([128, 2, SEQ], BF16, "gkT")
        gv = A.t([128, 16, 512], BF16, "gv")
        glrT = A.t([16, SEQ], BF16, "glrT")
        so = A.t([128, 4, SEQ], BF16, "so")
        csp = A.t([128, 2, SEQ], F32, "csp")
        ex = [A.t([128, 512], F32, "ex%d" % i) for i in range(2)]
        eb = [A.t([128, 128], F32, "eb%d" % i) for i in range(2)]
        ek = [A.t([128, 128], F32, "ek%d" % i) for i in range(2)]
        qt = [A.t([128, 128], BF16, "qt%d" % i) for i in range(2)]
        kt_ = [A.t([128, 128], BF16, "kt%d" % i) for i in range(2)]
        ktm = [A.t([128, 128], BF16, "ktm%d" % i) for i in range(2)]
        am = [A.t([128, 128], BF16, "am%d" % i) for i in range(4)]
        Sf = [A.t([128, 128], F32, "Sf%d" % i) for i in range(2)]
        Sb = [A.t([128, 128], BF16, "Sb%d" % i) for i in range(2)]
        sqb = [A.t([128, 512], BF16, "sqb%d" % i) for i in range(2)]
        rstd = [A.t([128, 512], F32, "rstd%d" % i) for i in range(2)]
        og = [A.t([128, 512], BF16, "og%d" % i) for i in range(2)]
        S.add("act", lambda e: e.mul(ngb[:], gbias[:], -1.0), reads=[gbias], writes=[ngb])
        wqk = A.t([128, 8, 512], F32, "wqk")
        for k_ in range(8):
            DMA("sp", wqk[:, k_, :], I["w_in"][l, 128 * k_:128 * (k_ + 1), 1420:1932], [], [wqk])
        q0f = A.t([128, 2, 128], F32, "q0f")
        k0f = A.t([128, 2, 128], F32, "k0f")
        qf = [A.t([128, 128], F32, "qf%d" % i) for i in range(2)]
        kf = [A.t([128, 128], F32, "kf%d" % i) for i in range(2)]
        n = 0
        for c in range(2):
            ps = P[n % 2]; n += 1
            for k in range(8):
                MM(ps[:, 0:128], wqk[:, k, 128 * c:128 * (c + 1)], hTf0[:, k, :], k == 0, k == 7, [wqk, hTf0], [ps])
            ACT(q0f[:, c, :], ps[:, 0:128], AF.Identity, [ps], [q0f], scale=0.125)
            ps = P[n % 2]; n += 1
            for k in range(8):
                MM(ps[:, 0:128], wqk[:, k, 256 + 128 * c:256 + 128 * (c + 1)], hTf0[:, k, :], k == 0, k == 7, [wqk, hTf0], [ps])
            CP("act", k0f[:, c, :], ps[:, 0:128], [ps], [k0f])
        for g in range(4):
            cols = slice(512 * g, 512 * (g + 1))
            for c in range(2):
                ps = P[n % 2]; n += 1
                for k in range(8):
                    MM(ps[:, :], wC[:, k, 128 * c:128 * (c + 1)], hT[:, k, cols], k == 0, k == 7, [wC, hT], [ps])
                ACT(gqT[:, c, cols], ps[:, :], AF.Identity, [ps], [gqT], scale=0.125)
                ps = P[n % 2]; n += 1
                for k in range(8):
                    MM(ps[:, :], wC[:, k, 256 + 128 * c:256 + 128 * (c + 1)], hT[:, k, cols], k == 0, k == 7, [wC, hT], [ps])
                CP("act", gkT[:, c, cols], ps[:, :], [ps], [gkT])
            ps = P[n % 2]; n += 1
            for k in range(8):
                MM(ps[0:16, :], wC[:, k, 1024:1040], hT[:, k, cols], k == 0, k == 7, [wC, hT], [ps])
            CP("act", glrT[:, cols], ps[0:16, :], [ps], [glrT])
            for c in range(4):
                ps = P[n % 2]; n += 1
                for k in range(8):
                    MM(ps[:, :], wC[:, k, 1040 + 128 * c:1040 + 128 * (c + 1)], hT[:, k, cols], k == 0, k == 7, [wC, hT], [ps])
                ACT(so[:, c, cols], ps[:, :], AF.Silu, [ps], [so])
            for c in range(2):
                ps = P[2 + c]
                e_ = ex[c]
                MM(ps[:, :], gwb[:, 128 * c:128 * (c + 1)], glrT[:, cols], True, True, [gwb, glrT], [ps])
                ACT(e_[:], ps[:, :], AF.Exp, [ps, ngb], [e_], bias=ngb[:, c:c + 1], scale=-1.0)
                ACT(e_[:], e_[:], AF.Ln, [e_], [e_], bias=1.0)
                for j in range(4):
                    cc = slice(512 * g + 128 * j, 512 * g + 128 * (j + 1))
                    S.add("dve", lambda e, c=c, cc=cc, j=j, e_=e_: e.tensor_tensor_scan(
                        out=csp[:, c, cc], data0=onesf[:, :], data1=e_[:, 128 * j:128 * (j + 1)], initial=0.0,
                        op0=ALU.mult, op1=ALU.add), reads=[e_, onesf], writes=[csp])
        for t in range(16):
            ps = P[4 + t % 2]
            for k in range(8):
                MM(ps[:, :], hT[:, k, 128 * t:128 * (t + 1)], wC[:, k, 512:1024], k == 0, k == 7, [hT, wC], [ps])
            CP("act", gv[:, t, :], ps[:, :], [ps], [gv])
        for c in range(2):
            MEMSET("pool", Sf[c][:], 0.0, [Sf[c]])
            MEMSET("pool", Sb[c][:], 0.0, [Sb[c]])
        S.barrier()
        po = [P[0], P[1], P[2], P[3]]
        pat = [Res("patA", P[4][:, 0:128]), Res("patB", P[4][:, 128:256])]
        pds = P[5]
        pss = P[6]
        for j in range(16):
            cc = slice(128 * j, 128 * (j + 1))
            jl = j % 4
            for c in range(2):
                b_, k_, q_, kk, km = eb[c], ek[c], qt[c], kt_[c], ktm[c]
                ACT(b_[:], csp[:, c, cc], AF.Exp, [csp], [b_], scale=-1.0 / 16)
                ACT(k_[:], csp[:, c, cc], AF.Exp, [csp], [k_], scale=1.0 / 16)
                TT("dve", q_[:], gqT[:, c, cc], b_[:], ALU.mult, [gqT, b_], [q_])
                TT("dve", kk[:], gkT[:, c, cc], k_[:], ALU.mult, [gkT, k_], [kk])
                TR(pst[:, c, :], kk[:], identb[:], [kk, identb], [pst])
                CP("act", km[:], pst[:, c, :], [pst], [km])
                if j == 0:
                    TT("dve", qf[c][:], q0f[:, c, :], b_[:], ALU.mult, [q0f, b_], [qf[c]])
                    TT("dve", kf[c][:], k0f[:, c, :], k_[:], ALU.mult, [k0f, k_], [kf[c]])
                for hh in range(2):
                    h = 2 * c + hh
                    rows = slice(64 * hh, 64 * (hh + 1))
                    a_ = am[2 * c + hh]
                    if j == 0:
                        MM(pat[hh][:, :], kf[c][rows, :], qf[c][rows, :], True, True, [kf[c], qf[c]], [pat[hh]])
                    else:
                        MM(pat[hh][:, :], kk[rows, :], q_[rows, :], True, True, [kk, q_], [pat[hh]])
                    TT("dve", a_[:], pat[hh][:, :], cmask[:], ALU.mult, [pat[hh], cmask], [a_])
                    MM(po[h][:, 128 * jl:128 * (jl + 1)], gv[:, j, 128 * h:128 * (h + 1)], a_[:], True, False, [gv, a_], [po[h]])
                    MM(po[h][:, 128 * jl:128 * (jl + 1)], Sb[c][rows, :], q_[rows, :], False, True, [Sb[c], q_], [po[h]])
                    MM(pds[rows, 0:128], km[:, rows], gv[:, j, 128 * h:128 * (h + 1)], True, True, [km, gv], [pds])
                TS("dve", Sf[c][:], Sf[c][:], b_[:, 127:128], None, ALU.mult, None, [Sf[c], b_], [Sf[c]])
                STT(Sf[c][:], pds[:, 0:128], b_[:, 127:128], Sf[c][:], ALU.mult, ALU.add, [pds, b_, Sf[c]], [Sf[c]])
                CP("act", Sb[c][:], Sf[c][:], [Sf[c]], [Sb[c]])
            if jl == 3:
                g = j // 4
                cols = slice(512 * g, 512 * (g + 1))
                for h in range(4):
                    sq_, rs_, o_ = sqb[h % 2], rstd[h % 2], og[h % 2]
                    ACT(sq_[:], po[h][:, :], AF.Square, [po[h]], [sq_])
                    MM(pss[:, :], onesb[:, :], sq_[:], True, True, [onesb, sq_], [pss])
                    ACT(rs_[:], pss[:, :], AF.Sqrt, [pss], [rs_], bias=EPS, scale=1.0 / 128)
                    RECIP(rs_[:], rs_[:], [rs_], [rs_])
                    STT(rs_[:], po[h][:, :], gon[:, 0:1], rs_[:], ALU.mult, ALU.mult, [po[h], gon, rs_], [rs_])
                    TT("dve", o_[:], rs_[:], so[:, h, cols], ALU.mult, [rs_, so], [o_])
                    c0 = NTOK // 2 * s + 512 * g
                    DMA("sp", oG[h, :, c0:c0 + 512], o_[:], [o_], [DR("oG", (s, g))])

    def phase_merge(l, s, xsrc, xsname, xdst, xdname):
        A = Alloc(S, PH_BASE)
        wM = A.t([128, 8, 3072], BF16, "wM")
        for k_ in range(8):
            DMA("pool", wM[:, k_, :], I["w_in"][l, 128 * k_:128 * (k_ + 1), 2972:6044], [], [wM])
        wbm = A.t([64, 4, DM], BF16, "wbm")
        wbn = A.t([64, 4, DM], BF16, "wbn")
        wbg = A.t([128, 4, DM], BF16, "wbg")
        wo = A.t([128, 8, DM], BF16, "wo")
        DMA("pool", wbm[:], I["w_branch_moba"][l].rearrange("(h d) n -> d h n", d=64), [], [wbm])
        DMA("pool", wbn[:], I["w_branch_nsa"][l].rearrange("(h d) n -> d h n", d=64), [], [wbn])
        DMA("pool", wbg[:], I["w_branch_gla"][l].rearrange("(h d) n -> d h n", d=128), [], [wbg])
        for k_ in range(8):
            DMA("pool", wo[:, k_, :], I["w_out"][l, 128 * k_:128 * (k_ + 1), :], [], [wo])
        om = [A.t([64, 4, 512], BF16, "om%d" % i) for i in range(2)]
        on = [A.t([64, 4, 512], BF16, "on%d" % i) for i in range(2)]
        ogt = [A.t([128, 4, 512], BF16, "ogt%d" % i) for i in range(2)]
        gt = [A.t([128, 512], F32, "gt%d" % i) for i in range(3)]
        t3 = [A.t([128, 512], F32, "t3_%d" % i) for i in range(3)]
        zT = A.t([128, 8, 512], BF16, "zT")
        xin = [A.t([128, DM], F32, "xin%d" % i) for i in range(2)]
        for g in range(4):
            cols = slice(512 * g, 512 * (g + 1))
            c0 = NTOK // 2 * s + 512 * g
            o1, o2, o3 = om[g % 2], on[g % 2], ogt[g % 2]
            DMA("sp", o1[:], oM[:, :, c0:c0 + 512].rearrange("h d n -> d h n"), [DR("oM", (s, g))], [o1])
            DMA("sp", o2[:], oN[:, :, c0:c0 + 512].rearrange("h d n -> d h n"), [DR("oN", (s, g))], [o2])
            DMA("sp", o3[:], oG[:, :, c0:c0 + 512].rearrange("h d n -> d h n"), [DR("oG", (s, g))], [o3])
            for c in range(8):
                for b in range(3):
                    ps = P[b]
                    for k in range(8):
                        MM(ps[:, :], wM[:, k, 1024 * b + 128 * c:1024 * b + 128 * (c + 1)], hT[:, k, cols], k == 0, k == 7,
                           [wM, hT], [ps])
                    ACT(gt[b][:], ps[:, :], AF.Sigmoid, [ps], [gt[b]])
                for b, (wb, ot) in enumerate(((wbm, o1), (wbn, o2), (wbg, o3))):
                    ps = P[3 + b]
                    rows = 64 if b < 2 else 128
                    for h in range(4):
                        MM(ps[:, :], wb[0:rows, h, 128 * c:128 * (c + 1)], ot[0:rows, h, :], h == 0, h == 3, [wb, ot], [ps])
                    TT("dve", t3[b][:], ps[:, :], gt[b][:], ALU.mult, [ps, gt[b]], [t3[b]])
                TT("pool", t3[0][:], t3[0][:], t3[1][:], ALU.add, [t3[0], t3[1]], [t3[0]])
                TT("pool", zT[:, c, :], t3[0][:], t3[2][:], ALU.add, [t3[0], t3[2]], [zT])
            for tl in range(4):
                r0 = NTOK // 2 * s + 512 * g + 128 * tl
                xi = xin[tl % 2]
                DMA("sp", xi[:], xsrc[r0:r0 + 128, :], [DR(xsname, r0 // 128)], [xi])
                for hf in range(2):
                    ps = P[6] if hf == 0 else P[0]
                    for c in range(8):
                        MM(ps[:, :], zT[:, c, 128 * tl:128 * (tl + 1)], wo[:, c, 512 * hf:512 * (hf + 1)], c == 0, c == 7,
                           [zT, wo], [ps])
                    TT("dve", xi[:, 512 * hf:512 * (hf + 1)], xi[:, 512 * hf:512 * (hf + 1)], ps[:, :], ALU.add, [xi, ps], [xi])
                DMA("sp", xdst[r0:r0 + 128, :], xi[:], [xi], [DR(xdname, r0 // 128)])

    def phase_ffn(l, xsrc, xsname, xdst, xdname, seq_list):
        A = Alloc(S, CONST_END)
        wu = A.t([128, 8, 2 * DFF], BF16, "wu")
        wd = A.t([128, 22, DM], BF16, "wd")
        for k in range(8):
            DMA("pool", wu[:, k, :], I["w_up"][l, 128 * k:128 * (k + 1), :], [], [wu])
        for c in range(0, 22, 2):
            DMA("pool", wd[:, c:c + 2, :], I["w_down"][l, 128 * c:128 * (c + 2), :].rearrange("(c p) n -> p c n", p=128), [], [wd])
        cwr = A.t([22, 4, 128], F32, "cwr")
        cw = A.t([128, 4, 22], F32, "cw")
        for j in range(3):
            DMA("sp", cwr[:, j, :], I["conv_w"][l, j].rearrange("(c p) -> c p", p=128), [], [cwr])
        DMA("sp", cwr[:, 3, :], I["conv_b"][l].rearrange("(c p) -> c p", p=128), [], [cwr])
        for j in range(4):
            TR(P[6][:, 22 * j:22 * (j + 1)], cwr[:, j, :], identf[0:22, 0:22], [cwr, identf], [P[6]])
        CP("act", cw[:].rearrange("p j c -> p (j c)"), P[6][:, 0:88], [P[6]], [cw])
        h2T = A.t([128, 8, 256], BF16, "h2T")
        uT = A.t([128, 22, 256], BF16, "uT")
        NB = 3
        ab = [A.t([128, 258], F32, "ab%d" % i) for i in range(NB)]
        t1 = [A.t([128, 256], F32, "t1_%d" % i) for i in range(NB)]
        ge = [A.t([128, 256], F32, "ge%d" % i) for i in range(NB)]
        pas = [Res("pa%d" % i, P[2 * i][:, 0:256]) for i in range(NB)]
        pgs = [Res("pg%d" % i, P[2 * i + 1][:, 0:256]) for i in range(NB)]
        halo = A.t([128, 22, 2], F32, "halo")
        grep = A.t([128, DM], F32, "grep")
        xin = [A.t([128, DM], F32, "xin%d" % i) for i in range(2)]
        sqj = A.t([128, DM], BF16, "sqj")
        hb = [A.t([128, DM], BF16, "hb%d" % i) for i in range(2)]
        st = [A.t([128, 4], F32, "st%d" % i) for i in range(2)]
        DMA("sp", grep[:], I["ffn_norm"][l].rearrange("(o d) -> o d", o=1).partition_broadcast(128), [], [grep])
        n = 0
        for s in seq_list:
            MEMSET("pool", halo[:], 0.0, [halo])
            for gi in range(8):
                row0 = NTOK // 2 * s + 256 * gi
                for t in range(2):
                    xi, h, s_ = xin[t], hb[t], st[t]
                    r0 = row0 + 128 * t
                    DMA("sp", xi[:], xsrc[r0:r0 + 128, :], [DR(xsname, r0 // 128)], [xi])
                    ACT(sqj[:], xi[:], AF.Square, [xi], [sqj, s_], accum=s_[:, 0:1])
                    ACT(s_[:, 1:2], s_[:, 0:1], AF.Sqrt, [s_], [s_], bias=EPS, scale=1.0 / DM)
                    RECIP(s_[:, 2:3], s_[:, 1:2], [s_], [s_])
                    STT(h[:], xi[:], s_[:, 2:3], grep[:], ALU.mult, ALU.mult, [xi, s_, grep], [h])
                    for c in range(8):
                        TR(pst[:, c, :], h[:, 128 * c:128 * (c + 1)], identb[:], [h, identb], [pst])
                    CP("act", h2T[:, :, 128 * t:128 * (t + 1)], pst[:], [pst], [h2T])
                for c in range(22):
                    pa, pg = pas[n % NB], pgs[n % NB]
                    a_, t_, g_ = ab[n % NB], t1[n % NB], ge[n % NB]
                    n += 1
                    for k in range(8):
                        MM(pa[:, :], wu[:, k, 128 * c:128 * (c + 1)], h2T[:, k, :], k == 0, k == 7, [wu, h2T], [pa])
                    for k in range(8):
                        MM(pg[:, :], wu[:, k, DFF + 128 * c:DFF + 128 * (c + 1)], h2T[:, k, :], k == 0, k == 7, [wu, h2T], [pg])
                    CP("pool", a_[:, 0:2], halo[:, c, :], [halo], [a_])
                    CP("act", a_[:, 2:258], pa[:, :], [pa], [a_])
                    CP("pool", halo[:, c, :], a_[:, 256:258], [a_], [halo])
                    TS("dve", t_[:], a_[:, 2:258], cw[:, 2, c:c + 1], cw[:, 3, c:c + 1], ALU.mult, ALU.add, [a_, cw], [t_])
                    STT(t_[:], a_[:, 1:257], cw[:, 1, c:c + 1], t_[:], ALU.mult, ALU.add, [a_, cw, t_], [t_])
                    STT(t_[:], a_[:, 0:256], cw[:, 0, c:c + 1], t_[:], ALU.mult, ALU.add, [a_, cw, t_], [t_])
                    ACT(g_[:], t_[:], AF.Gelu_apprx_tanh, [t_], [g_])
                    TT("dve", uT[:, c, :], g_[:], pg[:, :], ALU.mult, [g_, pg], [uT])
                for t in range(2):
                    xi = xin[t]
                    r0 = row0 + 128 * t
                    for hf in range(2):
                        ps = P[6]
                        for c in range(22):
                            MM(ps[:, :], uT[:, c, 128 * t:128 * (t + 1)], wd[:, c, 512 * hf:512 * (hf + 1)], c == 0, c == 21,
                               [uT, wd], [ps])
                        TT("dve", xi[:, 512 * hf:512 * (hf + 1)], xi[:, 512 * hf:512 * (hf + 1)], ps[:, :], ALU.add, [xi, ps], [xi])
                    DMA("sp", xdst[r0:r0 + 128, :], xi[:], [xi], [DR(xdname, r0 // 128)])

    prologue()
    cur, cname = I["x"], "x"
    for l in layers:
        mid, mname = (xA, "xA")
        for s in seqs:
            if "N" in phases:
                A = Alloc(S, PH_BASE)
                norm_T(A, cur, cname, NTOK // 2 * s, 16, I["attn_norm"][l], hT, 0)
                S.barrier()
            if "A" in phases:
                phase_moba(l, s)
                S.barrier()
            if "B" in phases:
                phase_nsa(l, s)
                S.barrier()
            if "C" in phases:
                phase_gla(l, s)
                S.barrier()
            if "M" in phases:
                phase_merge(l, s, cur, cname, mid, mname)
                S.barrier()
        last = (l == layers[-1])
        dst, dname = (yout, "y") if (last and not dbg) else (xB, "xB")
        if "F" in phases:
            phase_ffn(l, mid, mname, dst, dname, seqs)
            S.barrier()
        cur, cname = dst, dname
    S.emit()
    return nc, consts


_CACHE = {}


def kernel(**inputs):
    n = 8
    x = np.ascontiguousarray(np.asarray(inputs["x"], dtype=np.float32))
    if "prog" not in _CACHE:
        _CACHE["prog"] = build_program()
    nc, consts = _CACHE["prog"]
    base = {k: np.ascontiguousarray(np.asarray(inputs[k], dtype=np.float32)) for k in W_SHAPES}
    base.update(consts)
    in_maps = []
    for i in range(n):
        m = dict(base)
        m["x"] = x[2 * i:2 * i + 2].reshape(NTOK, DM)
        in_maps.append(m)
    res = run_bass_kernel_spmd(nc, in_maps, core_ids=list(range(n)))
    out = np.stack([np.asarray(r["y"], dtype=np.float32).reshape(2, SEQ, DM) for r in res.results], 0)
    return out.reshape(16, SEQ, DM)
```

```python
import math
from contextlib import ExitStack
import numpy as np
import concourse.bass as bass
import concourse.mybir as mybir
from concourse.bass_utils import run_bass_kernel_spmd

F32 = mybir.dt.float32
BF16 = mybir.dt.bfloat16
AF = mybir.ActivationFunctionType
ALU = mybir.AluOpType
AX = mybir.AxisListType

SEQ = 2048
DM = 1024
NTOK = 4096
DFF = 2816
NEGM = -8192.0
EPS = 1e-6


class Res:
    __slots__ = ("name", "w", "rs", "t")

    def __init__(self, name, t=None):
        self.name = name
        self.w = None
        self.rs = {}
        self.t = t

    def __getitem__(self, k):
        return self.t[k]


class Sched:
    ENG = ("pe", "act", "dve", "pool", "sp")

    def __init__(self, nc, n_dma_sems=20):
        self.nc = nc
        self.ops = {e: [] for e in self.ENG}
        self.cnt = {e: 0 for e in self.ENG}
        self.waited = {e: {} for e in self.ENG}
        self.n_dma_sems = n_dma_sems
        self.dma_cnt = {}
        self.dma_rr = {"sp": 0, "pool": 0, "act": 0}
        self.ntile = 0

    def make_arena(self, nbytes):
        self.arena_t = self.nc.alloc_sbuf_tensor("arena", [128, nbytes], mybir.dt.uint8)
        self.arena_n = nbytes

    def tile(self, shape, dtype, name, offset):
        self.ntile += 1
        name = "%s_%d" % (name, self.ntile)
        esz = 4 if dtype == F32 else 2
        n = int(np.prod(shape[1:]))
        assert offset % 4 == 0 and offset + n * esz <= self.arena_n, (name, offset, n * esz, self.arena_n)
        v = self.arena_t[0:shape[0], offset:offset + n * esz].bitcast(dtype)
        if len(shape) == 3:
            v = v.rearrange("p (a b) -> p a b", a=shape[1])
        elif len(shape) == 4:
            v = v.rearrange("p (a b c) -> p a b c", a=shape[1], b=shape[2])
        return Res(name, v)

    def psum(self, shape, dtype, name):
        self.ntile += 1
        return Res(name, self.nc.alloc_psum_tensor("%s_%d" % (name, self.ntile), list(shape), dtype))

    def _deps(self, reads, writes):
        deps = {}
        for r in reads:
            if r.w is not None:
                k, v = r.w
                if deps.get(k, 0) < v:
                    deps[k] = v
        for w in writes:
            if w.w is not None:
                k, v = w.w
                if deps.get(k, 0) < v:
                    deps[k] = v
            for k, v in w.rs.items():
                if deps.get(k, 0) < v:
                    deps[k] = v
        return deps

    def add(self, eng, fn, reads=(), writes=()):
        deps = self._deps(reads, writes)
        waits = []
        wd = self.waited[eng]
        for k, v in deps.items():
            if wd.get(k, 0) >= v:
                continue
            if eng == "pe" and k == "pe":
                continue
            waits.append((k, v))
            wd[k] = v
        self.cnt[eng] += 1
        key, val = eng, self.cnt[eng]
        self.ops[eng].append((waits, fn, key, 1))
        for r in reads:
            if r.rs.get(key, 0) < val:
                r.rs[key] = val
        for w in writes:
            w.w = (key, val)
            w.rs = {}

    def dma(self, q, fn, reads=(), writes=()):
        deps = self._deps(reads, writes)
        i = self.dma_rr[q]
        self.dma_rr[q] = (i + 1) % self.n_dma_sems
        key = "d_%s_%d" % (q, i)
        prev = self.dma_cnt.get(key, 0)
        if prev:
            deps[key] = max(deps.get(key, 0), 16 * prev)
        waits = []
        wd = self.waited[q]
        for k, v in deps.items():
            if wd.get(k, 0) >= v:
                continue
            waits.append((k, v))
            wd[k] = v
        self.dma_cnt[key] = prev + 1
        val = 16 * (prev + 1)
        self.ops[q].append((waits, fn, key, 16))
        for r in reads:
            if r.rs.get(key, 0) < val:
                r.rs[key] = val
        for w in writes:
            w.w = (key, val)
            w.rs = {}

    def barrier(self):
        tot = {e: self.cnt[e] for e in self.ENG if self.cnt[e]}
        for k, c in self.dma_cnt.items():
            tot[k] = 16 * c
        for e in self.ENG:
            waits = []
            wd = self.waited[e]
            for k, v in tot.items():
                if k == e or wd.get(k, 0) >= v:
                    continue
                waits.append((k, v))
                wd[k] = v
            if waits:
                self.ops[e].append((waits, None, None, 0))

    def emit(self):
        nc = self.nc
        keys = list(self.ENG) + sorted(self.dma_cnt.keys())
        with ExitStack() as es:
            sems = {k: es.enter_context(nc.semaphore("s_" + k)) for k in keys}
            block = es.enter_context(nc.Block())
            decs = {"pe": block.tensor, "act": block.scalar, "dve": block.vector,
                    "pool": block.gpsimd, "sp": block.sync}
            final = {k: 16 * c for k, c in self.dma_cnt.items()}
            for e in self.ENG:
                def body(engine, ops=self.ops[e], e=e):
                    for waits, fn, key, inc in ops:
                        for k, v in waits:
                            engine.wait_ge(sems[k], v)
                        if fn is None:
                            continue
                        fn(engine).then_inc(sems[key], inc)
                    if e == "sp":
                        for k, v in final.items():
                            engine.wait_ge(sems[k], v)
                        for k in self.ENG:
                            if k != "sp" and self.cnt[k]:
                                engine.wait_ge(sems[k], self.cnt[k])
                decs[e](body)


class Alloc:
    def __init__(self, S, base):
        self.S = S
        self.off = base

    def t(self, shape, dtype, name):
        esz = 4 if dtype == F32 else 2
        nb = (int(np.prod(shape[1:])) * esz + 63) // 64 * 64
        r = self.S.tile(shape, dtype, name, self.off)
        self.off += nb
        return r


def _rel_bucket(dist):
    n = np.maximum(dist, 0)
    nf = np.maximum(n, 1).astype(np.float32)
    large = 16 + (np.log(nf / np.float32(16)) / np.float32(math.log(128 / 16)) * np.float32(16)).astype(np.int32)
    large = np.minimum(large, 31)
    return np.where(n < 16, n, large)


def make_consts():
    c = {}
    m = np.arange(384) - 128
    c["c_onehot"] = (_rel_bucket(m)[None, :] == np.arange(32)[:, None]).astype(np.float32)
    tt = np.arange(16)
    blk = np.arange(8)
    pb = np.where(blk[None, :] < (tt // 2)[:, None], 0.0, -30000.0).astype(np.float32)
    own = (blk[None, :] == (tt // 2)[:, None]).astype(np.float32)
    c["c_pb"] = np.ascontiguousarray(np.broadcast_to(pb[None, :, None, :], (128, 16, 4, 8))).astype(np.float32)
    c["c_own"] = np.ascontiguousarray(np.broadcast_to(own[None, :, None, :], (128, 16, 4, 8))).astype(np.float32)
    t = np.arange(SEQ)
    cur = t // 64
    b32 = np.arange(32)
    forced = (b32[None, :] == 0) | (b32[None, :] == cur[:, None]) | (b32[None, :] == cur[:, None] - 1)
    fb = np.where(b32[None, :] <= cur[:, None], np.where(forced, 1e4, 0.0), -30000.0).astype(np.float32)
    c["c_fb"] = np.ascontiguousarray(fb.reshape(16, 128, 32).transpose(1, 0, 2))
    starts = np.arange(127) * 16
    tok = np.arange(32 * 64)
    inside = (tok[None, :] >= starts[:, None]) & (tok[None, :] < starts[:, None] + 32)
    ovl = (inside.reshape(127, 32, 64).sum(-1) / 32).astype(np.float32)
    c["c_ovl"] = np.concatenate([ovl, np.zeros((1, 32), np.float32)], 0)
    j = np.arange(128)
    c["c_cmask"] = (j[None, :] >= j[:, None]).astype(np.float32)
    return c


W_SHAPES = {
    "rel_bias": (32, 8), "attn_norm": (2, 1024), "w_in": (2, 1024, 6044), "moba_q_norm": (2, 64),
    "moba_k_norm": (2, 64), "nsa_q_norm": (2, 64), "nsa_k_norm": (2, 3, 64), "cmp_pos_k": (2, 32, 64),
    "cmp_pos_v": (2, 32, 64), "cmp_k_w1": (2, 2048, 64), "cmp_k_w2": (2, 64, 64), "cmp_v_w1": (2, 2048, 64),
    "cmp_v_w2": (2, 64, 64), "gla_gate_w": (2, 16, 256), "gla_gate_b": (2, 256), "gla_out_norm": (2, 128),
    "w_branch_moba": (2, 256, 1024), "w_branch_nsa": (2, 256, 1024), "w_branch_gla": (2, 512, 1024),
    "w_out": (2, 1024, 1024), "ffn_norm": (2, 1024), "w_up": (2, 1024, 5632), "conv_w": (2, 3, 2816),
    "conv_b": (2, 2816), "w_down": (2, 2816, 1024),
}


def build_program(layers=(0, 1), seqs=(0, 1), dbg=False, phases="NABCMF"):
    nc = bass.Bass("TRN2", target_bir_lowering=False)
    consts = make_consts()
    I = {}
    I["x"] = nc.dram_tensor("x", [NTOK, DM], F32, kind="ExternalInput").ap()
    for k, shp in W_SHAPES.items():
        I[k] = nc.dram_tensor(k, list(shp), F32, kind="ExternalInput").ap()
    for k, v in consts.items():
        I[k] = nc.dram_tensor(k, list(v.shape), F32, kind="ExternalInput").ap()
    yout = nc.dram_tensor("y", [NTOK, DM], F32, kind="ExternalOutput").ap()
    skind = "ExternalOutput" if dbg else "Internal"
    xA = nc.dram_tensor("xA", [NTOK, DM], F32, kind=skind).ap()
    xB = nc.dram_tensor("xB", [NTOK, DM], F32, kind=skind).ap()
    oM = nc.dram_tensor("oM", [4, 64, NTOK], BF16, kind=skind).ap()
    oN = nc.dram_tensor("oN", [4, 64, NTOK], BF16, kind=skind).ap()
    oG = nc.dram_tensor("oG", [4, 128, NTOK], BF16, kind=skind).ap()
    Zh = nc.dram_tensor("Zsk", [8, 128, 384], F32)
    Zap = Zh.ap()

    S = Sched(nc)
    S.make_arena(207 * 1024)

    dres = {}

    def DR(name, idx):
        k = (name, idx)
        if k not in dres:
            dres[k] = Res("%s_%s" % (name, idx))
        return dres[k]

    def MM(out, lhsT, rhs, start, stop, rd, wr):
        S.add("pe", lambda e: e.matmul(out, lhsT=lhsT, rhs=rhs, start=start, stop=stop), reads=rd, writes=wr)

    def TR(out, in_, ident, rd, wr):
        S.add("pe", lambda e: e.transpose(out, in_, ident), reads=rd, writes=wr)

    def ACT(out, in_, func, rd, wr, bias=None, scale=None, accum=None):
        kw = {}
        if bias is not None:
            kw["bias"] = bias
        if scale is not None:
            kw["scale"] = scale
        if accum is not None:
            kw["accum_out"] = accum
        S.add("act", lambda e: e.activation(out, in_, func, **kw), reads=rd, writes=wr)

    def TT(eng, out, in0, in1, op, rd, wr):
        S.add(eng, lambda e: e.tensor_tensor(out=out, in0=in0, in1=in1, op=op), reads=rd, writes=wr)

    def TS(eng, out, in0, s1, s2, op0, op1, rd, wr):
        if s2 is None:
            S.add(eng, lambda e: e.tensor_scalar(out=out, in0=in0, scalar1=s1, scalar2=None, op0=op0), reads=rd, writes=wr)
        else:
            S.add(eng, lambda e: e.tensor_scalar(out=out, in0=in0, scalar1=s1, scalar2=s2, op0=op0, op1=op1), reads=rd, writes=wr)

    def STT(out, in0, scalar, in1, op0, op1, rd, wr):
        S.add("dve", lambda e: e.scalar_tensor_tensor(out=out, in0=in0, scalar=scalar, in1=in1, op0=op0, op1=op1),
              reads=rd, writes=wr)

    def CP(eng, out, in_, rd, wr):
        if eng == "act":
            S.add("act", lambda e: e.copy(out, in_), reads=rd, writes=wr)
        else:
            S.add(eng, lambda e: e.tensor_copy(out=out, in_=in_), reads=rd, writes=wr)

    def RECIP(out, in_, rd, wr):
        S.add("dve", lambda e: e.reciprocal(out, in_), reads=rd, writes=wr)

    def MEMSET(eng, out, val, wr):
        S.add(eng, lambda e: e.memset(out, val), writes=wr)

    def ASEL(out, in_, pattern, op, fill, base, cm, rd, wr):
        S.add("pool", lambda e: e.affine_select(out, in_, pattern, op, fill, base=base, channel_multiplier=cm),
              reads=rd, writes=wr)

    def DMA(q, out, in_, rd, wr):
        S.dma(q, lambda e: e.dma_start(out=out, in_=in_), reads=rd, writes=wr)

    P = [S.psum([128, 512], F32, "pb%d" % i) for i in range(7)]
    pst = S.psum([128, 8, 128], BF16, "pst")

    A0 = Alloc(S, 0)
    identb = A0.t([128, 128], BF16, "identb")
    identf = A0.t([128, 128], F32, "identf")
    onesb = A0.t([128, 128], BF16, "onesb")
    onesf = A0.t([128, 128], F32, "onesf")
    tb = A0.t([128, 8, 2, 128], F32, "tb")
    tw4 = A0.t([128, 128], F32, "tw4")
    cb = A0.t([128, 8], F32, "cb")
    cmk = A0.t([128, SEQ], BF16, "cmk")
    pbt = A0.t([128, 16, 4, 8], F32, "pbt")
    ownt = A0.t([128, 16, 4, 8], F32, "ownt")
    fbt = A0.t([128, 16, 32], F32, "fbt")
    ovl = A0.t([128, 32], F32, "ovl")
    cmask = A0.t([128, 128], F32, "cmask")
    egate = A0.t([12, 12, 64], BF16, "egate")
    CONST_END = A0.off
    hT = A0.t([128, 8, SEQ], BF16, "hT")
    hTf0 = A0.t([128, 8, 128], F32, "hTf0")
    PH_BASE = A0.off

    def prologue():
        A = Alloc(S, PH_BASE)
        tab = A.t([32, 8], F32, "tab")
        oneh = A.t([32, 384], F32, "oneh")
        tbc = A.t([32, 128], F32, "tbc")
        frs = [A.t([128, 384], F32, "frs%d" % i) for i in range(2)]
        e12 = A.t([12, 12, 64], F32, "e12")
        MEMSET("pool", identf[:], 0.0, [identf])
        ASEL(identf[:], identf[:], [[-1, 128]], ALU.not_equal, 1.0, 0, 1, [identf], [identf])
        CP("dve", identb[:], identf[:], [identf], [identb])
        MEMSET("dve", onesb[:], 1.0, [onesb])
        MEMSET("dve", onesf[:], 1.0, [onesf])
        DMA("sp", tab[:], I["rel_bias"], [], [tab])
        DMA("sp", oneh[:], I["c_onehot"], [], [oneh])
        DMA("sp", pbt[:], I["c_pb"], [], [pbt])
        DMA("sp", ownt[:], I["c_own"], [], [ownt])
        DMA("sp", fbt[:], I["c_fb"], [], [fbt])
        DMA("sp", ovl[:], I["c_ovl"], [], [ovl])
        DMA("sp", cmask[:], I["c_cmask"], [], [cmask])
        zres = Res("zsk")
        for h in range(8):
            fr = frs[h % 2]
            ACT(tbc[:], onesf[0:32, :], AF.Identity, [onesf, tab], [tbc], scale=tab[:, h:h + 1])
            MM(P[0][:, 0:384], tbc[:], oneh[:], True, True, [tbc, oneh], [P[0]])
            CP("act", cb[:, h:h + 1], P[0][:, 383:384], [P[0]], [cb])
            TS("dve", fr[:], P[0][:, 0:384], cb[:, h:h + 1], 8.0, ALU.subtract, ALU.mult, [P[0], cb], [fr])
            DMA("sp", Zap[h], fr[:], [fr], [zres])
        S.barrier()
        for h in range(8):
            for j in range(2):
                src = bass.AP(tensor=Zh, offset=h * 128 * 384 + 128 * (j + 1), ap=[[383, 128], [1, 128]])
                DMA("sp", tb[:, h, j, :], src, [zres], [tb])
        ASEL(tb[:, :, 0, :], tb[:, :, 0, :], [[0, 8], [1, 128]], ALU.is_ge, NEGM, 0, -1, [tb], [tb])
        MEMSET("pool", tw4[:], 0.0, [tw4])
        ASEL(tw4[:], tw4[:], [[-1, 128]], ALU.is_ge, NEGM, -1, 1, [tw4], [tw4])
        MEMSET("pool", cmk[:], 0.0, [cmk])
        ASEL(cmk[:], cmk[:], [[1, SEQ]], ALU.is_ge, NEGM, -31, -16, [cmk], [cmk])
        MEMSET("pool", e12[:], 1.0, [e12])
        ASEL(e12[:], e12[:], [[-1, 12], [0, 64]], ALU.is_equal, 0.0, 0, 1, [e12], [e12])
        CP("dve", egate[:], e12[:], [e12], [egate])
        S.barrier()

    def norm_T(A, xsrc, xname, row0, ntiles, gain_vec, dst, dcol0):
        h32 = A.t([128, DM], F32, "h32")
        grep = A.t([128, DM], F32, "grep")
        xin = [A.t([128, DM], F32, "xin%d" % i) for i in range(2)]
        sqj = A.t([128, DM], BF16, "sqj")
        hb = [A.t([128, DM], BF16, "hb%d" % i) for i in range(2)]
        st = [A.t([128, 4], F32, "st%d" % i) for i in range(2)]
        DMA("sp", grep[:], gain_vec.rearrange("(o d) -> o d", o=1).partition_broadcast(128), [], [grep])
        for t in range(ntiles):
            xi, h, s = xin[t % 2], hb[t % 2], st[t % 2]
            r0 = row0 + 128 * t
            DMA("sp", xi[:], xsrc[r0:r0 + 128, :], [DR(xname, r0 // 128)], [xi])
            ACT(sqj[:], xi[:], AF.Square, [xi], [sqj, s], accum=s[:, 0:1])
            ACT(s[:, 1:2], s[:, 0:1], AF.Sqrt, [s], [s], bias=EPS, scale=1.0 / DM)
            RECIP(s[:, 2:3], s[:, 1:2], [s], [s])
            STT(h[:], xi[:], s[:, 2:3], grep[:], ALU.mult, ALU.mult, [xi, s, grep], [h])
            for c in range(8):
                TR(pst[:, c, :], h[:, 128 * c:128 * (c + 1)], identb[:], [h, identb], [pst])
            CP("act", dst[:, :, dcol0 + 128 * t:dcol0 + 128 * (t + 1)], pst[:], [pst], [dst])
            if t == 0:
                STT(h32[:], xi[:], s[:, 2:3], grep[:], ALU.mult, ALU.mult, [xi, s, grep], [h32])
                for half in range(2):
                    pp = P[half]
                    for c in range(4):
                        TR(pp[:, 128 * c:128 * (c + 1)], h32[:, 128 * (4 * half + c):128 * (4 * half + c + 1)], identf[:],
                           [h32, identf], [pp])
                    CP("act", hTf0[:, 4 * half:4 * half + 4, :], pp[:, :].rearrange("p (c n) -> p c n", c=4), [pp], [hTf0])

    def fm_rmsnorm(ps, rows, n, gain, out_ap, out_res, sqb, rstd, ps2, ones_t):
        gain_col = gain[0:rows, 0:1]
        ACT(sqb[0:rows, 0:n], ps[0:rows, 0:n], AF.Square, [ps], [sqb])
        MM(ps2[0:rows, 0:n], ones_t[0:rows, 0:rows], sqb[0:rows, 0:n], True, True, [ones_t, sqb], [ps2])
        ACT(rstd[0:rows, 0:n], ps2[0:rows, 0:n], AF.Sqrt, [ps2], [rstd], bias=EPS, scale=1.0 / rows)
        RECIP(rstd[0:rows, 0:n], rstd[0:rows, 0:n], [rstd], [rstd])
        STT(out_ap, ps[0:rows, 0:n], gain_col, rstd[0:rows, 0:n], ALU.mult, ALU.mult, [ps, rstd, gain], [out_res])

    def attn_group(g, hb_idx, q_res, k_res, krows, v_res, v_ap_fn, kts, window, pnum, pden, pts, scnt):
        LOOK = 2
        items = []
        for kt in kts:
            r = kt - 4 * g
            lo = max(r, 0)
            hi = min(r + 4, 3) if window else 3
            biases = []
            if 0 <= r <= 3:
                biases.append((128 * r, tb[:, hb_idx, 0, :]))
            if 0 <= r + 1 <= 3:
                biases.append((128 * (r + 1), tb[:, hb_idx, 1, :]))
            if window and 0 <= r + 4 <= 3:
                biases.append((128 * (r + 4), tw4[:]))
            items.append((kt, 128 * lo, 128 * (hi + 1), biases))
        bufs = {}

        def qk(i):
            kt, c0, c1, biases = items[i]
            ps = P[scnt[0] % 3]
            pt = pts[scnt[0] % len(pts)]
            scnt[0] += 1
            bufs[i] = pt
            MM(ps[:, c0:c1], k_res[0:krows, 128 * kt:128 * (kt + 1)], q_res[0:krows, 512 * g + c0:512 * g + c1],
               True, len(biases) == 0, [k_res, q_res], [ps])
            for bi, (bc, bt) in enumerate(biases):
                MM(ps[:, bc:bc + 128], identf[:], bt, False, bi == len(biases) - 1, [identf, tb, tw4], [ps])
            ACT(pt[:, c0:c1], ps[:, c0:c1], AF.Exp, [ps, cb], [pt], bias=cb[:, hb_idx:hb_idx + 1], scale=0.125)

        def pv(i):
            kt, c0, c1, biases = items[i]
            pt = bufs[i]
            MM(pnum[0:64, c0:c1], v_ap_fn(kt), pt[:, c0:c1], i == 0, i == len(items) - 1, [v_res, pt], [pnum])
            MM(pden[0:64, c0:c1], onesb[:, 0:64], pt[:, c0:c1], i == 0, i == len(items) - 1, [onesb, pt], [pden])

        for i in range(len(items) + LOOK):
            if i < len(items):
                qk(i)
            if i - LOOK >= 0:
                pv(i - LOOK)

    def load_col(A, vec_ap, n, name):
        t = A.t([n, 1], F32, name)
        DMA("sp", t[:], vec_ap.rearrange("(d o) -> d o", o=1), [], [t])
        return t

    def sel_rows(kaug, nrows, blk):
        v = kaug[64:64 + nrows, :]
        MEMSET("pool", kaug[64:128, :], 0.0, [kaug])
        MEMSET("pool", v, 8192.0, [kaug])
        ASEL(v, v, [[1, SEQ]], ALU.is_ge, 0.0, 0, -blk, [kaug], [kaug])
        ASEL(v, v, [[-1, SEQ]], ALU.is_ge, 0.0, blk - 1, blk, [kaug], [kaug])

    def phase_moba(l, s):
        A = Alloc(S, PH_BASE)
        wA = A.t([128, 8, 768], BF16, "wA")
        for k_ in range(8):
            DMA("pool", wA[:, k_, :], I["w_in"][l, 128 * k_:128 * (k_ + 1), 0:768], [], [wA])
        gq = load_col(A, I["moba_q_norm"][l], 64, "gq")
        gk = load_col(A, I["moba_k_norm"][l], 64, "gk")
        qaug = [A.t([128, SEQ], BF16, "qaug%d" % h) for h in range(4)]
        kaug = [A.t([128, SEQ], BF16, "kaug%d" % h) for h in range(4)]
        V = A.t([128, 16, 256], BF16, "V")
        kmf = A.t([64, 4, 8], F32, "kmf")
        kmb = A.t([64, 4, 8], BF16, "kmb")
        sqb = [A.t([64, 512], BF16, "sqb%d" % i) for i in range(2)]
        rstd = [A.t([64, 512], F32, "rstd%d" % i) for i in range(2)]
        gsb = A.t([128, 4, 8], F32, "gsb")
        m8 = A.t([128, 4, 8], F32, "m8")
        selm = [A.t([128, 4, 8], F32, "selm%d" % i) for i in range(2)]
        mts = A.t([32, SEQ], BF16, "mts")
        pts = [A.t([128, 512], BF16, "pt%d" % i) for i in range(4)]
        rden = [A.t([64, 512], F32, "rden%d" % i) for i in range(2)]
        ob = [A.t([64, 512], BF16, "ob%d" % i) for i in range(2)]
        for h in range(4):
            sel_rows(kaug[h], 8, 256)
            MEMSET("pool", qaug[h][64:128, :], 0.0, [qaug[h]])
        n = 0
        for g in range(4):
            cols = slice(512 * g, 512 * (g + 1))
            for h in range(4):
                for (c0, dst, gain) in ((64 * h, qaug[h], gq), (256 + 64 * h, kaug[h], gk)):
                    ps = P[n % 2]
                    ps2 = P[2 + n % 2]
                    for k in range(8):
                        MM(ps[0:64, :], wA[:, k, c0:c0 + 64], hT[:, k, cols], k == 0, k == 7, [wA, hT], [ps])
                    fm_rmsnorm(ps, 64, 512, gain, dst[0:64, cols], dst, sqb[n % 2], rstd[n % 2], ps2, onesb)
                    n += 1
        for t in range(16):
            ps = P[4 + t % 2]
            for k in range(8):
                MM(ps[:, 0:256], hT[:, k, 128 * t:128 * (t + 1)], wA[:, k, 512:768], k == 0, k == 7, [hT, wA], [ps])
            CP("act", V[:, t, :], ps[:, 0:256], [ps], [V])
        for h in range(4):
            S.add("dve", lambda e, h=h: e.tensor_reduce(out=kmf[:, h, :], in_=kaug[h][0:64, :].rearrange("p (b j) -> p b j", j=256),
                                                        axis=AX.X, op=ALU.add), reads=[kaug[h]], writes=[kmf])
        CP("dve", kmb[:], kmf[:], [kmf], [kmb])
        for t in range(16):
            ps = P[t % 2]
            sm = selm[t % 2]
            for h in range(4):
                MM(ps[:, 8 * h:8 * h + 8], qaug[h][0:64, 128 * t:128 * (t + 1)], kmb[:, h, :], True, True, [qaug[h], kmb], [ps])
            TT("dve", gsb[:], ps[:, 0:32].rearrange("p (h b) -> p h b", h=4), pbt[:, t], ALU.add, [ps, pbt], [gsb])
            for h in range(4):
                S.add("dve", lambda e, h=h: e.max(out=m8[:, h, :], in_=gsb[:, h, :]), reads=[gsb], writes=[m8])
            for h in range(4):
                TS("dve", sm[:, h, :], gsb[:, h, :], m8[:, h, 2:3], None, ALU.is_ge, None, [gsb, m8], [sm])
            TT("dve", sm[:], sm[:], ownt[:, t], ALU.max, [sm, ownt], [sm])
            TS("dve", sm[:], sm[:], -1.0, None, ALU.add, None, [sm], [sm])
            pt_ = P[2 + t % 2]
            TR(pt_[0:32, 0:128], sm[:].rearrange("p h b -> p (h b)"), identf[:], [sm, identf], [pt_])
            CP("act", mts[:, 128 * t:128 * (t + 1)], pt_[0:32, 0:128], [pt_], [mts])
        for h in range(4):
            DMA("sp", qaug[h][64:72, :], mts[8 * h:8 * h + 8, :], [mts], [qaug[h]])
        scnt = [0]
        n = 0
        for h in range(4):
            for g in range(4):
                pnum, pden = P[3 + 2 * (n % 2)], P[4 + 2 * (n % 2)]
                attn_group(g, h, qaug[h], kaug[h], 128, V, lambda kt, h=h: V[:, kt, 64 * h:64 * h + 64],
                           list(range(4 * g + 4)), False, pnum, pden, pts, scnt)
                rd, o = rden[n % 2], ob[n % 2]
                RECIP(rd[:], pden[0:64, :], [pden], [rd])
                TT("dve", o[:], pnum[0:64, :], rd[:], ALU.mult, [pnum, rd], [o])
                c0 = NTOK // 2 * s + 512 * g
                DMA("sp", oM[h, :, c0:c0 + 512], o[:], [o], [DR("oM", (s, g))])
                n += 1

    def phase_nsa(l, s):
        A = Alloc(S, PH_BASE)
        wB = A.t([128, 8, 652], BF16, "wB")
        for k_ in range(8):
            DMA("pool", wB[:, k_, :], I["w_in"][l, 128 * k_:128 * (k_ + 1), 768:1420], [], [wB])
        gq = load_col(A, I["nsa_q_norm"][l], 64, "gq")
        gkc = load_col(A, I["nsa_k_norm"][l, 0], 64, "gkc")
        gks = load_col(A, I["nsa_k_norm"][l, 1], 64, "gks")
        gkw = load_col(A, I["nsa_k_norm"][l, 2], 64, "gkw")
        w1 = [A.t([64, 32, 64], BF16, "w1_%d" % i) for i in range(2)]
        w2 = [A.t([64, 64], BF16, "w2_%d" % i) for i in range(2)]
        posr = [A.t([32, 64], F32, "posr%d" % i) for i in range(2)]
        posT = [A.t([64, 32], F32, "posT%d" % i) for i in range(2)]
        for i, (a, b, c) in enumerate((("cmp_k_w1", "cmp_k_w2", "cmp_pos_k"), ("cmp_v_w1", "cmp_v_w2", "cmp_pos_v"))):
            DMA("pool", w1[i][:], I[a][l].rearrange("(l d) o -> d l o", d=64), [], [w1[i]])
            DMA("pool", w2[i][:], I[b][l], [], [w2[i]])
            DMA("sp", posr[i][:], I[c][l], [], [posr[i]])
        nqaug = [A.t([128, SEQ], BF16, "nqaug%d" % h) for h in range(4)]
        ksaug = A.t([128, SEQ], BF16, "ksaug")
        kwT = A.t([64, SEQ], BF16, "kwT")
        kcT = A.t([64, SEQ], BF16, "kcT")
        vcT = A.t([64, SEQ], BF16, "vcT")
        vs = A.t([128, 16, 64], BF16, "vs")
        vw = A.t([128, 16, 64], BF16, "vw")
        sgT = A.t([12, SEQ], BF16, "sgT")
        kcs = A.t([64, 32, 127], BF16, "kcs")
        gl = A.t([64, 128], BF16, "gl")
        kcn = A.t([64, 128], BF16, "kcn")
        vcm = A.t([128, 64], F32, "vcm")
        sqb = [A.t([64, 512], BF16, "sqb%d" % i) for i in range(2)]
        rstd = [A.t([64, 512], F32, "rstd%d" % i) for i in range(2)]
        ptf = [A.t([128, 512], F32, "ptf%d" % i) for i in range(2)]
        rdr = A.t([128, 512], F32, "rdr")
        impT = A.t([32, 512], F32, "impT")
        vv = A.t([128, 32], F32, "vv")
        vv2 = A.t([128, 32], F32, "vv2")
        m8a = A.t([128, 8], F32, "m8a")
        m8b = A.t([128, 8], F32, "m8b")
        selm = [A.t([128, 32], F32, "selm%d" % i) for i in range(2)]
        mts = A.t([32, 512], BF16, "mts")
        pts = [A.t([128, 512], BF16, "pt%d" % i) for i in range(4)]
        gr = [A.t([64, 512], F32, "gr%d" % i) for i in range(2)]
        rden = [A.t([64, 512], F32, "rden%d" % i) for i in range(2)]
        tmp = [A.t([64, 512], F32, "tmp%d" % i) for i in range(2)]
        oacc = [A.t([64, 512], F32, "oacc%d" % h) for h in range(4)]
        ob = [A.t([64, 512], BF16, "ob%d" % i) for i in range(2)]
        sel_rows(ksaug, 32, 64)
        for h in range(4):
            MEMSET("pool", nqaug[h][64:128, :], 0.0, [nqaug[h]])
        for i in range(2):
            TR(P[6][0:64, 0:32], posr[i][:], identf[0:32, 0:32], [posr[i], identf], [P[6]])
            CP("act", posT[i][:], P[6][0:64, 0:32], [P[6]], [posT[i]])
        n = 0
        for g in range(4):
            cols = slice(512 * g, 512 * (g + 1))
            outs = [(64 * h, nqaug[h], gq) for h in range(4)] + [(384, ksaug, gks), (512, kwT, gkw), (256, kcT, None), (320, vcT, None)]
            for (c0, dst, gain) in outs:
                ps = P[n % 2]
                ps2 = P[2 + n % 2]
                for k in range(8):
                    MM(ps[0:64, :], wB[:, k, c0:c0 + 64], hT[:, k, cols], k == 0, k == 7, [wB, hT], [ps])
                if gain is None:
                    CP("act", dst[0:64, cols], ps[0:64, :], [ps], [dst])
                else:
                    fm_rmsnorm(ps, 64, 512, gain, dst[0:64, cols], dst, sqb[n % 2], rstd[n % 2], ps2, onesb)
                n += 1
            ps = P[n % 2]
            n += 1
            for k in range(8):
                MM(ps[0:12, :], wB[:, k, 640:652], hT[:, k, cols], k == 0, k == 7, [wB, hT], [ps])
            ACT(sgT[:, cols], ps[0:12, :], AF.Sigmoid, [ps], [sgT])
        for t in range(16):
            ps = P[4 + t % 2]
            for k in range(8):
                MM(ps[:, 0:64], hT[:, k, 128 * t:128 * (t + 1)], wB[:, k, 448:512], k == 0, k == 7, [hT, wB], [ps])
            for k in range(8):
                MM(ps[:, 64:128], hT[:, k, 128 * t:128 * (t + 1)], wB[:, k, 576:640], k == 0, k == 7, [hT, wB], [ps])
            CP("act", vs[:, t, :], ps[:, 0:64], [ps], [vs])
            CP("act", vw[:, t, :], ps[:, 64:128], [ps], [vw])
        for i, src in enumerate((kcT, vcT)):
            for ll in range(32):
                TS("dve", kcs[:, ll, :], src[0:64, ll:ll + 2017:16], posT[i][:, ll:ll + 1], None, ALU.add, None,
                   [src, posT[i]], [kcs])
            ps = P[0]
            for ll in range(32):
                MM(ps[0:64, 0:127], w1[i][:, ll, :], kcs[:, ll, :], ll == 0, ll == 31, [w1[i], kcs], [ps])
            ACT(gl[:, 0:127], ps[0:64, 0:127], AF.Gelu_apprx_tanh, [ps], [gl])
            if i == 0:
                MM(P[1][0:64, 0:127], w2[0][:], gl[:, 0:127], True, True, [w2[0], gl], [P[1]])
                fm_rmsnorm(P[1], 64, 127, gkc, kcn[:, 0:127], kcn, sqb[0], rstd[0], P[2], onesb)
            else:
                MM(P[1][0:127, 0:64], gl[:, 0:127], w2[1][:], True, True, [w2[1], gl], [P[1]])
                CP("act", vcm[0:127, :], P[1][0:127, 0:64], [P[1]], [vcm])
        scnt = [0]
        nn = 0
        for g in range(4):
            cols = slice(512 * g, 512 * (g + 1))
            for h in range(4):
                ps = P[h % 2]
                pf = ptf[h % 2]
                MM(ps[0:127, :], kcn[:, 0:127], nqaug[h][0:64, cols], True, False, [kcn, nqaug[h]], [ps])
                MM(ps[0:127, :], identb[0:127, 0:127], cmk[0:127, cols], False, True, [identb, cmk], [ps])
                ACT(pf[0:127, :], ps[0:127, :], AF.Exp, [ps], [pf], scale=0.125)
                MM(P[2][:, :], onesf[0:127, :], pf[0:127, :], True, True, [onesf, pf], [P[2]])
                TS("dve", rdr[:], P[2][:, :], 1e-30, None, ALU.add, None, [P[2]], [rdr])
                RECIP(rdr[:], rdr[:], [rdr], [rdr])
                TT("dve", pf[0:127, :], pf[0:127, :], rdr[0:127, :], ALU.mult, [pf, rdr], [pf])
                MM(P[3][0:32, :], ovl[0:127, :], pf[0:127, :], h == 0, h == 3, [ovl, pf], [P[3]])
                MM(P[4][0:64, :], vcm[0:127, :], pf[0:127, :], True, True, [vcm, pf], [P[4]])
                gt = gr[h % 2]
                MM(P[5][0:64, :], egate[:, 3 * h + 0, :], sgT[:, cols], True, True, [egate, sgT], [P[5]])
                CP("act", gt[:], P[5][0:64, :], [P[5]], [gt])
                TT("dve", oacc[h][:], P[4][0:64, :], gt[:], ALU.mult, [P[4], gt], [oacc[h]])
            CP("act", impT[:], P[3][0:32, :], [P[3]], [impT])
            for tl in range(4):
                t = 4 * g + tl
                sm = selm[tl % 2]
                pp = P[tl % 2]
                TR(pp[:, 0:32], impT[:, 128 * tl:128 * (tl + 1)], identf[0:32, 0:32], [impT, identf], [pp])
                TT("dve", vv[:], pp[:, 0:32], fbt[:, t, :], ALU.add, [pp, fbt], [vv])
                S.add("dve", lambda e: e.max(out=m8a[:], in_=vv[:]), reads=[vv], writes=[m8a])
                S.add("dve", lambda e: e.match_replace(out=vv2[:], in_to_replace=m8a[:], in_values=vv[:], imm_value=-1e9),
                      reads=[vv, m8a], writes=[vv2])
                S.add("dve", lambda e: e.max(out=m8b[:], in_=vv2[:]), reads=[vv2], writes=[m8b])
                TS("dve", sm[:], vv[:], m8b[:, 7:8], -1.0, ALU.is_ge, ALU.add, [vv, m8b], [sm])
                pq = P[2 + tl % 2]
                TR(pq[0:32, 0:128], sm[:], identf[:], [sm, identf], [pq])
                CP("act", mts[:, 128 * tl:128 * (tl + 1)], pq[0:32, 0:128], [pq], [mts])
            for h in range(4):
                DMA("sp", nqaug[h][64:96, cols], mts[:, :], [mts], [nqaug[h]])
            for h in range(4):
                for br in (1, 2):
                    pnum, pden = P[3 + 2 * (nn % 2)], P[4 + 2 * (nn % 2)]
                    if br == 1:
                        attn_group(g, 4 + h, nqaug[h], ksaug, 128, vs, lambda kt: vs[:, kt, :],
                                   list(range(4 * g + 4)), False, pnum, pden, pts, scnt)
                    else:
                        attn_group(g, 4 + h, nqaug[h], kwT, 64, vw, lambda kt: vw[:, kt, :],
                                   list(range(max(0, 4 * g - 4), 4 * g + 4)), True, pnum, pden, pts, scnt)
                    rd, gt, tp = rden[nn % 2], gr[nn % 2], tmp[nn % 2]
                    MM(P[0][0:64, :], egate[:, 3 * h + br, :], sgT[:, cols], True, True, [egate, sgT], [P[0]])
                    RECIP(rd[:], pden[0:64, :], [pden], [rd])
                    TT("dve", rd[:], rd[:], P[0][0:64, :], ALU.mult, [rd, P[0]], [rd])
                    TT("dve", tp[:], pnum[0:64, :], rd[:], ALU.mult, [pnum, rd], [tp])
                    if br == 1:
                        TT("pool", oacc[h][:], oacc[h][:], tp[:], ALU.add, [oacc[h], tp], [oacc[h]])
                    else:
                        o = ob[h % 2]
                        TT("pool", o[:], oacc[h][:], tp[:], ALU.add, [oacc[h], tp], [o])
                        c0 = NTOK // 2 * s + 512 * g
                        DMA("sp", oN[h, :, c0:c0 + 512], o[:], [o], [DR("oN", (s, g))])
                    nn += 1

    def phase_gla(l, s):
        A = Alloc(S, PH_BASE)
        wC = A.t([128, 8, 1552], BF16, "wC")
        for k_ in range(8):
            DMA("pool", wC[:, k_, :], I["w_in"][l, 128 * k_:128 * (k_ + 1), 1420:2972], [], [wC])
        gwb = A.t([16, 256], BF16, "gwb")
        DMA("pool", gwb[:], I["gla_gate_w"][l], [], [gwb])
        gbias = A.t([128, 2], F32, "gbias")
        for c in range(2):
            DMA("sp", gbias[:, c:c + 1], I["gla_gate_b"][l, 128 * c:128 * (c + 1)].rearrange("(d o) -> d o", o=1), [], [gbias])
        ngb = A.t([128, 2], F32, "ngb")
        gon = load_col(A, I["gla_out_norm"][l], 128, "gon")
        gqT = A.t([128, 2, SEQ], BF16, "gqT")
        gkT = A.t([128, 2, SEQ], BF16, "gkT")
        gv = A.t([128, 16, 512], BF16, "gv")
        glrT = A.t([16, SEQ], BF16, "glrT")
        so = A.t([128, 4, SEQ], BF16, "so")
        csp = A.t([128, 2, SEQ], F32, "csp")
        ex = [A.t([128, 512], F32, "ex%d" % i) for i in range(2)]
        eb = [A.t([128, 128], F32, "eb%d" % i) for i in range(2)]
        ek = [A.t([128, 128], F32, "ek%d" % i) for i in range(2)]
        qt = [A.t([128, 128], BF16, "qt%d" % i) for i in range(2)]
        kt_ = [A.t([128, 128], BF16, "kt%d" % i) for i in range(2)]
        ktm = [A.t([128, 128], BF16, "ktm%d" % i) for i in range(2)]
        am = [A.t([128, 128], BF16, "am%d" % i) for i in range(4)]
        Sf = [A.t([128, 128], F32, "Sf%d" % i) for i in range(2)]
        Sb = [A.t([128, 128], BF16, "Sb%d" % i) for i in range(2)]
        sqb = [A.t([128, 512], BF16, "sqb%d" % i) for i in range(2)]
        rstd = [A.t([128, 512], F32, "rstd%d" % i) for i in range(2)]
        og = [A.t([128, 512], BF16, "og%d" % i) for i in range(2)]
        S.add("act", lambda e: e.mul(ngb[:], gbias[:], -1.0), reads=[gbias], writes=[ngb])
        wqk = A.t([128, 8, 512], F32, "wqk")
        for k_ in range(8):
            DMA("sp", wqk[:, k_, :], I["w_in"][l, 128 * k_:128 * (k_ + 1), 1420:1932], [], [wqk])
        q0f = A.t([128, 2, 128], F32, "q0f")
        k0f = A.t([128, 2, 128], F32, "k0f")
        qf = [A.t([128, 128], F32, "qf%d" % i) for i in range(2)]
        kf = [A.t([128, 128], F32, "kf%d" % i) for i in range(2)]
        n = 0
        for c in range(2):
            ps = P[n % 2]; n += 1
            for k in range(8):
                MM(ps[:, 0:128], wqk[:, k, 128 * c:128 * (c + 1)], hTf0[:, k, :], k == 0, k == 7, [wqk, hTf0], [ps])
            ACT(q0f[:, c, :], ps[:, 0:128], AF.Identity, [ps], [q0f], scale=0.125)
            ps = P[n % 2]; n += 1
            for k in range(8):
                MM(ps[:, 0:128], wqk[:, k, 256 + 128 * c:256 + 128 * (c + 1)], hTf0[:, k, :], k == 0, k == 7, [wqk, hTf0], [ps])
            CP("act", k0f[:, c, :], ps[:, 0:128], [ps], [k0f])
        for g in range(4):
            cols = slice(512 * g, 512 * (g + 1))
            for c in range(2):
                ps = P[n % 2]; n += 1
                for k in range(8):
                    MM(ps[:, :], wC[:, k, 128 * c:128 * (c + 1)], hT[:, k, cols], k == 0, k == 7, [wC, hT], [ps])
                ACT(gqT[:, c, cols], ps[:, :], AF.Identity, [ps], [gqT], scale=0.125)
                ps = P[n % 2]; n += 1
                for k in range(8):
                    MM(ps[:, :], wC[:, k, 256 + 128 * c:256 + 128 * (c + 1)], hT[:, k, cols], k == 0, k == 7, [wC, hT], [ps])
                CP("act", gkT[:, c, cols], ps[:, :], [ps], [gkT])
            ps = P[n % 2]; n += 1
            for k in range(8):
                MM(ps[0:16, :], wC[:, k, 1024:1040], hT[:, k, cols], k == 0, k == 7, [wC, hT], [ps])
            CP("act", glrT[:, cols], ps[0:16, :], [ps], [glrT])
            for c in range(4):
                ps = P[n % 2]; n += 1
                for k in range(8):
                    MM(ps[:, :], wC[:, k, 1040 + 128 * c:1040 + 128 * (c + 1)], hT[:, k, cols], k == 0, k == 7, [wC, hT], [ps])
                ACT(so[:, c, cols], ps[:, :], AF.Silu, [ps], [so])
            for c in range(2):
                ps = P[2 + c]
                e_ = ex[c]
                MM(ps[:, :], gwb[:, 128 * c:128 * (c + 1)], glrT[:, cols], True, True, [gwb, glrT], [ps])
                ACT(e_[:], ps[:, :], AF.Exp, [ps, ngb], [e_], bias=ngb[:, c:c + 1], scale=-1.0)
                ACT(e_[:], e_[:], AF.Ln, [e_], [e_], bias=1.0)
                for j in range(4):
                    cc = slice(512 * g + 128 * j, 512 * g + 128 * (j + 1))
                    S.add("dve", lambda e, c=c, cc=cc, j=j, e_=e_: e.tensor_tensor_scan(
                        out=csp[:, c, cc], data0=onesf[:, :], data1=e_[:, 128 * j:128 * (j + 1)], initial=0.0,
                        op0=ALU.mult, op1=ALU.add), reads=[e_, onesf], writes=[csp])
        for t in range(16):
            ps = P[4 + t % 2]
            for k in range(8):
                MM(ps[:, :], hT[:, k, 128 * t:128 * (t + 1)], wC[:, k, 512:1024], k == 0, k == 7, [hT, wC], [ps])
            CP("act", gv[:, t, :], ps[:, :], [ps], [gv])
        for c in range(2):
            MEMSET("pool", Sf[c][:], 0.0, [Sf[c]])
            MEMSET("pool", Sb[c][:], 0.0, [Sb[c]])
        S.barrier()
        po = [P[0], P[1], P[2], P[3]]
        pat = [P[4], P[4]]
        pds = P[5]
        pss = P[6]
        for j in range(16):
            cc = slice(128 * j, 128 * (j + 1))
            jl = j % 4
            for c in range(2):
                b_, k_, q_, kk, km = eb[c], ek[c], qt[c], kt_[c], ktm[c]
                ACT(b_[:], csp[:, c, cc], AF.Exp, [csp], [b_], scale=-1.0 / 16)
                ACT(k_[:], csp[:, c, cc], AF.Exp, [csp], [k_], scale=1.0 / 16)
                TT("dve", q_[:], gqT[:, c, cc], b_[:], ALU.mult, [gqT, b_], [q_])
                TT("dve", kk[:], gkT[:, c, cc], k_[:], ALU.mult, [gkT, k_], [kk])
                TR(pst[:, c, :], kk[:], identb[:], [kk, identb], [pst])
                CP("act", km[:], pst[:, c, :], [pst], [km])
                if j == 0:
                    TT("dve", qf[c][:], q0f[:, c, :], b_[:], ALU.mult, [q0f, b_], [qf[c]])
                    TT("dve", kf[c][:], k0f[:, c, :], k_[:], ALU.mult, [k0f, k_], [kf[c]])
                for hh in range(2):
                    h = 2 * c + hh
                    rows = slice(64 * hh, 64 * (hh + 1))
                    a_ = am[2 * c + hh]
                    if j == 0:
                        MM(pat[hh][:, 128 * hh:128 * (hh + 1)], kf[c][rows, :], qf[c][rows, :], True, True, [kf[c], qf[c]], [pat[hh]])
                    else:
                        MM(pat[hh][:, 128 * hh:128 * (hh + 1)], kk[rows, :], q_[rows, :], True, True, [kk, q_], [pat[hh]])
                    TT("dve", a_[:], pat[hh][:, 128 * hh:128 * (hh + 1)], cmask[:], ALU.mult, [pat[hh], cmask], [a_])
                    MM(po[h][:, 128 * jl:128 * (jl + 1)], gv[:, j, 128 * h:128 * (h + 1)], a_[:], True, False, [gv, a_], [po[h]])
                    MM(po[h][:, 128 * jl:128 * (jl + 1)], Sb[c][rows, :], q_[rows, :], False, True, [Sb[c], q_], [po[h]])
                    MM(pds[rows, 0:128], km[:, rows], gv[:, j, 128 * h:128 * (h + 1)], True, True, [km, gv], [pds])
                TS("dve", Sf[c][:], Sf[c][:], b_[:, 127:128], None, ALU.mult, None, [Sf[c], b_], [Sf[c]])
                STT(Sf[c][:], pds[:, 0:128], b_[:, 127:128], Sf[c][:], ALU.mult, ALU.add, [pds, b_, Sf[c]], [Sf[c]])
                CP("act", Sb[c][:], Sf[c][:], [Sf[c]], [Sb[c]])
            if jl == 3:
                g = j // 4
                cols = slice(512 * g, 512 * (g + 1))
                for h in range(4):
                    sq_, rs_, o_ = sqb[h % 2], rstd[h % 2], og[h % 2]
                    ACT(sq_[:], po[h][:, :], AF.Square, [po[h]], [sq_])
                    MM(pss[:, :], onesb[:, :], sq_[:], True, True, [onesb, sq_], [pss])
                    ACT(rs_[:], pss[:, :], AF.Sqrt, [pss], [rs_], bias=EPS, scale=1.0 / 128)
                    RECIP(rs_[:], rs_[:], [rs_], [rs_])
                    STT(rs_[:], po[h][:, :], gon[:, 0:1], rs_[:], ALU.mult, ALU.mult, [po[h], gon, rs_], [rs_])
                    TT("dve", o_[:], rs_[:], so[:, h, cols], ALU.mult, [rs_, so], [o_])
                    c0 = NTOK // 2 * s + 512 * g
                    DMA("sp", oG[h, :, c0:c0 + 512], o_[:], [o_], [DR("oG", (s, g))])

    def phase_merge(l, s, xsrc, xsname, xdst, xdname):
        A = Alloc(S, PH_BASE)
        wM = A.t([128, 8, 3072], BF16, "wM")
        for k_ in range(8):
            DMA("pool", wM[:, k_, :], I["w_in"][l, 128 * k_:128 * (k_ + 1), 2972:6044], [], [wM])
        wbm = A.t([64, 4, DM], BF16, "wbm")
        wbn = A.t([64, 4, DM], BF16, "wbn")
        wbg = A.t([128, 4, DM], BF16, "wbg")
        wo = A.t([128, 8, DM], BF16, "wo")
        DMA("pool", wbm[:], I["w_branch_moba"][l].rearrange("(h d) n -> d h n", d=64), [], [wbm])
        DMA("pool", wbn[:], I["w_branch_nsa"][l].rearrange("(h d) n -> d h n", d=64), [], [wbn])
        DMA("pool", wbg[:], I["w_branch_gla"][l].rearrange("(h d) n -> d h n", d=128), [], [wbg])
        for k_ in range(8):
            DMA("pool", wo[:, k_, :], I["w_out"][l, 128 * k_:128 * (k_ + 1), :], [], [wo])
        om = [A.t([64, 4, 512], BF16, "om%d" % i) for i in range(2)]
        on = [A.t([64, 4, 512], BF16, "on%d" % i) for i in range(2)]
        ogt = [A.t([128, 4, 512], BF16, "ogt%d" % i) for i in range(2)]
        gt = [A.t([128, 512], F32, "gt%d" % i) for i in range(3)]
        t3 = [A.t([128, 512], F32, "t3_%d" % i) for i in range(3)]
        zT = A.t([128, 8, 512], BF16, "zT")
        xin = [A.t([128, DM], F32, "xin%d" % i) for i in range(2)]
        for g in range(4):
            cols = slice(512 * g, 512 * (g + 1))
            c0 = NTOK // 2 * s + 512 * g
            o1, o2, o3 = om[g % 2], on[g % 2], ogt[g % 2]
            DMA("sp", o1[:], oM[:, :, c0:c0 + 512].rearrange("h d n -> d h n"), [DR("oM", (s, g))], [o1])
            DMA("sp", o2[:], oN[:, :, c0:c0 + 512].rearrange("h d n -> d h n"), [DR("oN", (s, g))], [o2])
            DMA("sp", o3[:], oG[:, :, c0:c0 + 512].rearrange("h d n -> d h n"), [DR("oG", (s, g))], [o3])
            for c in range(8):
                for b in range(3):
                    ps = P[b]
                    for k in range(8):
                        MM(ps[:, :], wM[:, k, 1024 * b + 128 * c:1024 * b + 128 * (c + 1)], hT[:, k, cols], k == 0, k == 7,
                           [wM, hT], [ps])
                    ACT(gt[b][:], ps[:, :], AF.Sigmoid, [ps], [gt[b]])
                for b, (wb, ot) in enumerate(((wbm, o1), (wbn, o2), (wbg, o3))):
                    ps = P[3 + b]
                    rows = 64 if b < 2 else 128
                    for h in range(4):
                        MM(ps[:, :], wb[0:rows, h, 128 * c:128 * (c + 1)], ot[0:rows, h, :], h == 0, h == 3, [wb, ot], [ps])
                    TT("dve", t3[b][:], ps[:, :], gt[b][:], ALU.mult, [ps, gt[b]], [t3[b]])
                TT("pool", t3[0][:], t3[0][:], t3[1][:], ALU.add, [t3[0], t3[1]], [t3[0]])
                TT("pool", zT[:, c, :], t3[0][:], t3[2][:], ALU.add, [t3[0], t3[2]], [zT])
            for tl in range(4):
                r0 = NTOK // 2 * s + 512 * g + 128 * tl
                xi = xin[tl % 2]
                DMA("sp", xi[:], xsrc[r0:r0 + 128, :], [DR(xsname, r0 // 128)], [xi])
                for hf in range(2):
                    ps = P[6] if hf == 0 else P[0]
                    for c in range(8):
                        MM(ps[:, :], zT[:, c, 128 * tl:128 * (tl + 1)], wo[:, c, 512 * hf:512 * (hf + 1)], c == 0, c == 7,
                           [zT, wo], [ps])
                    TT("dve", xi[:, 512 * hf:512 * (hf + 1)], xi[:, 512 * hf:512 * (hf + 1)], ps[:, :], ALU.add, [xi, ps], [xi])
                DMA("sp", xdst[r0:r0 + 128, :], xi[:], [xi], [DR(xdname, r0 // 128)])

    def phase_ffn(l, xsrc, xsname, xdst, xdname, seq_list):
        A = Alloc(S, CONST_END)
        wu = A.t([128, 8, 2 * DFF], BF16, "wu")
        wd = A.t([128, 22, DM], BF16, "wd")
        for k in range(8):
            DMA("pool", wu[:, k, :], I["w_up"][l, 128 * k:128 * (k + 1), :], [], [wu])
        for c in range(0, 22, 2):
            DMA("pool", wd[:, c:c + 2, :], I["w_down"][l, 128 * c:128 * (c + 2), :].rearrange("(c p) n -> p c n", p=128), [], [wd])
        cwr = A.t([22, 4, 128], F32, "cwr")
        cw = A.t([128, 4, 22], F32, "cw")
        for j in range(3):
            DMA("sp", cwr[:, j, :], I["conv_w"][l, j].rearrange("(c p) -> c p", p=128), [], [cwr])
        DMA("sp", cwr[:, 3, :], I["conv_b"][l].rearrange("(c p) -> c p", p=128), [], [cwr])
        for j in range(4):
            TR(P[6][:, 22 * j:22 * (j + 1)], cwr[:, j, :], identf[0:22, 0:22], [cwr, identf], [P[6]])
        CP("act", cw[:].rearrange("p j c -> p (j c)"), P[6][:, 0:88], [P[6]], [cw])
        h2T = A.t([128, 8, 256], BF16, "h2T")
        uT = A.t([128, 22, 256], BF16, "uT")
        NB = 3
        ab = [A.t([128, 258], F32, "ab%d" % i) for i in range(NB)]
        t1 = [A.t([128, 256], F32, "t1_%d" % i) for i in range(NB)]
        ge = [A.t([128, 256], F32, "ge%d" % i) for i in range(NB)]
        pas = [Res("pa%d" % i, P[2 * i][:, 0:256]) for i in range(NB)]
        pgs = [Res("pg%d" % i, P[2 * i + 1][:, 0:256]) for i in range(NB)]
        halo = A.t([128, 22, 2], F32, "halo")
        grep = A.t([128, DM], F32, "grep")
        xin = [A.t([128, DM], F32, "xin%d" % i) for i in range(2)]
        sqj = A.t([128, DM], BF16, "sqj")
        hb = [A.t([128, DM], BF16, "hb%d" % i) for i in range(2)]
        st = [A.t([128, 4], F32, "st%d" % i) for i in range(2)]
        DMA("sp", grep[:], I["ffn_norm"][l].rearrange("(o d) -> o d", o=1).partition_broadcast(128), [], [grep])
        n = 0
        for s in seq_list:
            MEMSET("pool", halo[:], 0.0, [halo])
            for gi in range(8):
                row0 = NTOK // 2 * s + 256 * gi
                for t in range(2):
                    xi, h, s_ = xin[t], hb[t], st[t]
                    r0 = row0 + 128 * t
                    DMA("sp", xi[:], xsrc[r0:r0 + 128, :], [DR(xsname, r0 // 128)], [xi])
                    ACT(sqj[:], xi[:], AF.Square, [xi], [sqj, s_], accum=s_[:, 0:1])
                    ACT(s_[:, 1:2], s_[:, 0:1], AF.Sqrt, [s_], [s_], bias=EPS, scale=1.0 / DM)
                    RECIP(s_[:, 2:3], s_[:, 1:2], [s_], [s_])
                    STT(h[:], xi[:], s_[:, 2:3], grep[:], ALU.mult, ALU.mult, [xi, s_, grep], [h])
                    for c in range(8):
                        TR(pst[:, c, :], h[:, 128 * c:128 * (c + 1)], identb[:], [h, identb], [pst])
                    CP("act", h2T[:, :, 128 * t:128 * (t + 1)], pst[:], [pst], [h2T])
                for c in range(22):
                    pa, pg = pas[n % NB], pgs[n % NB]
                    a_, t_, g_ = ab[n % NB], t1[n % NB], ge[n % NB]
                    n += 1
                    for k in range(8):
                        MM(pa[:, :], wu[:, k, 128 * c:128 * (c + 1)], h2T[:, k, :], k == 0, k == 7, [wu, h2T], [pa])
                    for k in range(8):
                        MM(pg[:, :], wu[:, k, DFF + 128 * c:DFF + 128 * (c + 1)], h2T[:, k, :], k == 0, k == 7, [wu, h2T], [pg])
                    CP("pool", a_[:, 0:2], halo[:, c, :], [halo], [a_])
                    CP("act", a_[:, 2:258], pa[:, :], [pa], [a_])
                    CP("pool", halo[:, c, :], a_[:, 256:258], [a_], [halo])
                    TS("dve", t_[:], a_[:, 2:258], cw[:, 2, c:c + 1], cw[:, 3, c:c + 1], ALU.mult, ALU.add, [a_, cw], [t_])
                    STT(t_[:], a_[:, 1:257], cw[:, 1, c:c + 1], t_[:], ALU.mult, ALU.add, [a_, cw, t_], [t_])
                    STT(t_[:], a_[:, 0:256], cw[:, 0, c:c + 1], t_[:], ALU.mult, ALU.add, [a_, cw, t_], [t_])
                    ACT(g_[:], t_[:], AF.Gelu_apprx_tanh, [t_], [g_])
                    TT("dve", uT[:, c, :], g_[:], pg[:, :], ALU.mult, [g_, pg], [uT])
                for t in range(2):
                    xi = xin[t]
                    r0 = row0 + 128 * t
                    for hf in range(2):
                        ps = P[6]
                        for c in range(22):
                            MM(ps[:, :], uT[:, c, 128 * t:128 * (t + 1)], wd[:, c, 512 * hf:512 * (hf + 1)], c == 0, c == 21,
                               [uT, wd], [ps])
                        TT("dve", xi[:, 512 * hf:512 * (hf + 1)], xi[:, 512 * hf:512 * (hf + 1)], ps[:, :], ALU.add, [xi, ps], [xi])
                    DMA("sp", xdst[r0:r0 + 128, :], xi[:], [xi], [DR(xdname, r0 // 128)])

    prologue()
    cur, cname = I["x"], "x"
    for l in layers:
        mid, mname = (xA, "xA")
        for s in seqs:
            if "N" in phases:
                A = Alloc(S, PH_BASE)
                norm_T(A, cur, cname, NTOK // 2 * s, 16, I["attn_norm"][l], hT, 0)
                S.barrier()
            if "A" in phases:
                phase_moba(l, s)
                S.barrier()
            if "B" in phases:
                phase_nsa(l, s)
                S.barrier()
            if "C" in phases:
                phase_gla(l, s)
                S.barrier()
            if "M" in phases:
                phase_merge(l, s, cur, cname, mid, mname)
                S.barrier()
        last = (l == layers[-1])
        dst, dname = (yout, "y") if (last and not dbg) else (xB, "xB")
        if "F" in phases:
            phase_ffn(l, mid, mname, dst, dname, seqs)
            S.barrier()
        cur, cname = dst, dname
    S.emit()
    return nc, consts


_CACHE = {}


def kernel(**inputs):
    n = 8
    x = np.ascontiguousarray(np.asarray(inputs["x"], dtype=np.float32))
    if "prog" not in _CACHE:
        _CACHE["prog"] = build_program()
    nc, consts = _CACHE["prog"]
    base = {k: np.ascontiguousarray(np.asarray(inputs[k], dtype=np.float32)) for k in W_SHAPES}
    base.update(consts)
    in_maps = []
    for i in range(n):
        m = dict(base)
        m["x"] = x[2 * i:2 * i + 2].reshape(NTOK, DM)
        in_maps.append(m)
    res = run_bass_kernel_spmd(nc, in_maps, core_ids=list(range(n)))
    out = np.stack([np.asarray(r["y"], dtype=np.float32).reshape(2, SEQ, DM) for r in res.results], 0)
    return out.reshape(16, SEQ, DM)
```

```python
import math
from contextlib import ExitStack
import numpy as np
import concourse.bass as bass
import concourse.mybir as mybir
from concourse.bass_utils import run_bass_kernel_spmd

F32 = mybir.dt.float32
BF16 = mybir.dt.bfloat16
AF = mybir.ActivationFunctionType
ALU = mybir.AluOpType
AX = mybir.AxisListType

SEQ = 2048
DM = 1024
NTOK = 4096
DFF = 2816
NEGM = -8192.0
EPS = 1e-6


class Res:
    __slots__ = ("name", "w", "rs", "t")

    def __init__(self, name, t=None):
        self.name = name
        self.w = None
        self.rs = {}
        self.t = t

    def __getitem__(self, k):
        return self.t[k]


class Sched:
    ENG = ("pe", "act", "dve", "pool", "sp")

    def __init__(self, nc, n_dma_sems=20):
        self.nc = nc
        self.ops = {e: [] for e in self.ENG}
        self.cnt = {e: 0 for e in self.ENG}
        self.waited = {e: {} for e in self.ENG}
        self.n_dma_sems = n_dma_sems
        self.dma_cnt = {}
        self.dma_rr = {"sp": 0, "pool": 0, "act": 0}
        self.ntile = 0

    def make_arena(self, nbytes):
        self.arena_t = self.nc.alloc_sbuf_tensor("arena", [128, nbytes], mybir.dt.uint8)
        self.arena_n = nbytes

    def tile(self, shape, dtype, name, offset):
        self.ntile += 1
        name = "%s_%d" % (name, self.ntile)
        esz = 4 if dtype == F32 else 2
        n = int(np.prod(shape[1:]))
        assert offset % 4 == 0 and offset + n * esz <= self.arena_n, (name, offset, n * esz, self.arena_n)
        v = self.arena_t[0:shape[0], offset:offset + n * esz].bitcast(dtype)
        if len(shape) == 3:
            v = v.rearrange("p (a b) -> p a b", a=shape[1])
        elif len(shape) == 4:
            v = v.rearrange("p (a b c) -> p a b c", a=shape[1], b=shape[2])
        return Res(name, v)

    def psum(self, shape, dtype, name):
        self.ntile += 1
        return Res(name, self.nc.alloc_psum_tensor("%s_%d" % (name, self.ntile), list(shape), dtype))

    def _deps(self, reads, writes):
        deps = {}
        for r in reads:
            if r.w is not None:
                k, v = r.w
                if deps.get(k, 0) < v:
                    deps[k] = v
        for w in writes:
            if w.w is not None:
                k, v = w.w
                if deps.get(k, 0) < v:
                    deps[k] = v
            for k, v in w.rs.items():
                if deps.get(k, 0) < v:
                    deps[k] = v
        return deps

    def add(self, eng, fn, reads=(), writes=()):
        deps = self._deps(reads, writes)
        waits = []
        wd = self.waited[eng]
        for k, v in deps.items():
            if wd.get(k, 0) >= v:
                continue
            if eng == "pe" and k == "pe":
                continue
            waits.append((k, v))
            wd[k] = v
        self.cnt[eng] += 1
        key, val = eng, self.cnt[eng]
        self.ops[eng].append((waits, fn, key, 1))
        for r in reads:
            if r.rs.get(key, 0) < val:
                r.rs[key] = val
        for w in writes:
            w.w = (key, val)
            w.rs = {}

    def dma(self, q, fn, reads=(), writes=()):
        deps = self._deps(reads, writes)
        i = self.dma_rr[q]
        self.dma_rr[q] = (i + 1) % self.n_dma_sems
        key = "d_%s_%d" % (q, i)
        prev = self.dma_cnt.get(key, 0)
        if prev:
            deps[key] = max(deps.get(key, 0), 16 * prev)
        waits = []
        wd = self.waited[q]
        for k, v in deps.items():
            if wd.get(k, 0) >= v:
                continue
            waits.append((k, v))
            wd[k] = v
        self.dma_cnt[key] = prev + 1
        val = 16 * (prev + 1)
        self.ops[q].append((waits, fn, key, 16))
        for r in reads:
            if r.rs.get(key, 0) < val:
                r.rs[key] = val
        for w in writes:
            w.w = (key, val)
            w.rs = {}

    def barrier(self):
        tot = {e: self.cnt[e] for e in self.ENG if self.cnt[e]}
        for k, c in self.dma_cnt.items():
            tot[k] = 16 * c
        for e in self.ENG:
            waits = []
            wd = self.waited[e]
            for k, v in tot.items():
                if k == e or wd.get(k, 0) >= v:
                    continue
                waits.append((k, v))
                wd[k] = v
            if waits:
                self.ops[e].append((waits, None, None, 0))

    def emit(self):
        nc = self.nc
        keys = list(self.ENG) + sorted(self.dma_cnt.keys())
        with ExitStack() as es:
            sems = {k: es.enter_context(nc.semaphore("s_" + k)) for k in keys}
            block = es.enter_context(nc.Block())
            decs = {"pe": block.tensor, "act": block.scalar, "dve": block.vector,
                    "pool": block.gpsimd, "sp": block.sync}
            final = {k: 16 * c for k, c in self.dma_cnt.items()}
            for e in self.ENG:
                def body(engine, ops=self.ops[e], e=e):
                    for waits, fn, key, inc in ops:
                        for k, v in waits:
                            engine.wait_ge(sems[k], v)
                        if fn is None:
                            continue
                        fn(engine).then_inc(sems[key], inc)
                    if e == "sp":
                        for k, v in final.items():
                            engine.wait_ge(sems[k], v)
                        for k in self.ENG:
                            if k != "sp" and self.cnt[k]:
                                engine.wait_ge(sems[k], self.cnt[k])
                decs[e](body)


class Alloc:
    def __init__(self, S, base):
        self.S = S
        self.off = base

    def t(self, shape, dtype, name):
        esz = 4 if dtype == F32 else 2
        nb = (int(np.prod(shape[1:])) * esz + 63) // 64 * 64
        r = self.S.tile(shape, dtype, name, self.off)
        self.off += nb
        return r


def _rel_bucket(dist):
    n = np.maximum(dist, 0)
    nf = np.maximum(n, 1).astype(np.float32)
    large = 16 + (np.log(nf / np.float32(16)) / np.float32(math.log(128 / 16)) * np.float32(16)).astype(np.int32)
    large = np.minimum(large, 31)
    return np.where(n < 16, n, large)


def make_consts():
    c = {}
    m = np.arange(384) - 128
    c["c_onehot"] = (_rel_bucket(m)[None, :] == np.arange(32)[:, None]).astype(np.float32)
    tt = np.arange(16)
    blk = np.arange(8)
    pb = np.where(blk[None, :] < (tt // 2)[:, None], 0.0, -30000.0).astype(np.float32)
    own = (blk[None, :] == (tt // 2)[:, None]).astype(np.float32)
    c["c_pb"] = np.ascontiguousarray(np.broadcast_to(pb[None, :, None, :], (128, 16, 4, 8))).astype(np.float32)
    c["c_own"] = np.ascontiguousarray(np.broadcast_to(own[None, :, None, :], (128, 16, 4, 8))).astype(np.float32)
    t = np.arange(SEQ)
    cur = t // 64
    b32 = np.arange(32)
    forced = (b32[None, :] == 0) | (b32[None, :] == cur[:, None]) | (b32[None, :] == cur[:, None] - 1)
    fb = np.where(b32[None, :] <= cur[:, None], np.where(forced, 1e4, 0.0), -30000.0).astype(np.float32)
    c["c_fb"] = np.ascontiguousarray(fb.reshape(16, 128, 32).transpose(1, 0, 2))
    starts = np.arange(127) * 16
    tok = np.arange(32 * 64)
    inside = (tok[None, :] >= starts[:, None]) & (tok[None, :] < starts[:, None] + 32)
    ovl = (inside.reshape(127, 32, 64).sum(-1) / 32).astype(np.float32)
    c["c_ovl"] = np.concatenate([ovl, np.zeros((1, 32), np.float32)], 0)
    j = np.arange(128)
    c["c_cmask"] = (j[None, :] >= j[:, None]).astype(np.float32)
    return c


W_SHAPES = {
    "rel_bias": (32, 8), "attn_norm": (2, 1024), "w_in": (2, 1024, 6044), "moba_q_norm": (2, 64),
    "moba_k_norm": (2, 64), "nsa_q_norm": (2, 64), "nsa_k_norm": (2, 3, 64), "cmp_pos_k": (2, 32, 64),
    "cmp_pos_v": (2, 32, 64), "cmp_k_w1": (2, 2048, 64), "cmp_k_w2": (2, 64, 64), "cmp_v_w1": (2, 2048, 64),
    "cmp_v_w2": (2, 64, 64), "gla_gate_w": (2, 16, 256), "gla_gate_b": (2, 256), "gla_out_norm": (2, 128),
    "w_branch_moba": (2, 256, 1024), "w_branch_nsa": (2, 256, 1024), "w_branch_gla": (2, 512, 1024),
    "w_out": (2, 1024, 1024), "ffn_norm": (2, 1024), "w_up": (2, 1024, 5632), "conv_w": (2, 3, 2816),
    "conv_b": (2, 2816), "w_down": (2, 2816, 1024),
}


def build_program(layers=(0, 1), seqs=(0, 1), dbg=False, phases="NABCMF"):
    nc = bass.Bass("TRN2", target_bir_lowering=False)
    consts = make_consts()
    I = {}
    I["x"] = nc.dram_tensor("x", [NTOK, DM], F32, kind="ExternalInput").ap()
    for k, shp in W_SHAPES.items():
        I[k] = nc.dram_tensor(k, list(shp), F32, kind="ExternalInput").ap()
    for k, v in consts.items():
        I[k] = nc.dram_tensor(k, list(v.shape), F32, kind="ExternalInput").ap()
    yout = nc.dram_tensor("y", [NTOK, DM], F32, kind="ExternalOutput").ap()
    skind = "ExternalOutput" if dbg else "Internal"
    xA = nc.dram_tensor("xA", [NTOK, DM], F32, kind=skind).ap()
    xB = nc.dram_tensor("xB", [NTOK, DM], F32, kind=skind).ap()
    oM = nc.dram_tensor("oM", [4, 64, NTOK], BF16, kind=skind).ap()
    oN = nc.dram_tensor("oN", [4, 64, NTOK], BF16, kind=skind).ap()
    oG = nc.dram_tensor("oG", [4, 128, NTOK], BF16, kind=skind).ap()
    Zh = nc.dram_tensor("Zsk", [8, 128, 384], F32)
    Zap = Zh.ap()

    S = Sched(nc)
    S.make_arena(207 * 1024)

    dres = {}

    def DR(name, idx):
        k = (name, idx)
        if k not in dres:
            dres[k] = Res("%s_%s" % (name, idx))
        return dres[k]

    def MM(out, lhsT, rhs, start, stop, rd, wr):
        S.add("pe", lambda e: e.matmul(out, lhsT=lhsT, rhs=rhs, start=start, stop=stop), reads=rd, writes=wr)

    def TR(out, in_, ident, rd, wr):
        S.add("pe", lambda e: e.transpose(out, in_, ident), reads=rd, writes=wr)

    def ACT(out, in_, func, rd, wr, bias=None, scale=None, accum=None):
        kw = {}
        if bias is not None:
            kw["bias"] = bias
        if scale is not None:
            kw["scale"] = scale
        if accum is not None:
            kw["accum_out"] = accum
        S.add("act", lambda e: e.activation(out, in_, func, **kw), reads=rd, writes=wr)

    def TT(eng, out, in0, in1, op, rd, wr):
        S.add(eng, lambda e: e.tensor_tensor(out=out, in0=in0, in1=in1, op=op), reads=rd, writes=wr)

    def TS(eng, out, in0, s1, s2, op0, op1, rd, wr):
        if s2 is None:
            S.add(eng, lambda e: e.tensor_scalar(out=out, in0=in0, scalar1=s1, scalar2=None, op0=op0), reads=rd, writes=wr)
        else:
            S.add(eng, lambda e: e.tensor_scalar(out=out, in0=in0, scalar1=s1, scalar2=s2, op0=op0, op1=op1), reads=rd, writes=wr)

    def STT(out, in0, scalar, in1, op0, op1, rd, wr):
        S.add("dve", lambda e: e.scalar_tensor_tensor(out=out, in0=in0, scalar=scalar, in1=in1, op0=op0, op1=op1),
              reads=rd, writes=wr)

    def CP(eng, out, in_, rd, wr):
        if eng == "act":
            S.add("act", lambda e: e.copy(out, in_), reads=rd, writes=wr)
        else:
            S.add(eng, lambda e: e.tensor_copy(out=out, in_=in_), reads=rd, writes=wr)

    def RECIP(out, in_, rd, wr):
        S.add("dve", lambda e: e.reciprocal(out, in_), reads=rd, writes=wr)

    def MEMSET(eng, out, val, wr):
        S.add(eng, lambda e: e.memset(out, val), writes=wr)

    def ASEL(out, in_, pattern, op, fill, base, cm, rd, wr):
        S.add("pool", lambda e: e.affine_select(out, in_, pattern, op, fill, base=base, channel_multiplier=cm),
              reads=rd, writes=wr)

    def DMA(q, out, in_, rd, wr):
        S.dma(q, lambda e: e.dma_start(out=out, in_=in_), reads=rd, writes=wr)

    P = [S.psum([128, 512], F32, "pb%d" % i) for i in range(7)]
    pst = S.psum([128, 8, 128], BF16, "pst")

    A0 = Alloc(S, 0)
    identb = A0.t([128, 128], BF16, "identb")
    identf = A0.t([128, 128], F32, "identf")
    onesb = A0.t([128, 128], BF16, "onesb")
    onesf = A0.t([128, 128], F32, "onesf")
    tb = A0.t([128, 8, 2, 128], F32, "tb")
    tw4 = A0.t([128, 128], F32, "tw4")
    cb = A0.t([128, 8], F32, "cb")
    cmk = A0.t([128, SEQ], BF16, "cmk")
    pbt = A0.t([128, 16, 4, 8], F32, "pbt")
    ownt = A0.t([128, 16, 4, 8], F32, "ownt")
    fbt = A0.t([128, 16, 32], F32, "fbt")
    ovl = A0.t([128, 32], F32, "ovl")
    cmask = A0.t([128, 128], F32, "cmask")
    egate = A0.t([12, 12, 64], BF16, "egate")
    zerob = A0.t([128, 64], BF16, "zerob")
    CONST_END = A0.off
    hT = A0.t([128, 8, SEQ], BF16, "hT")
    hTf0 = A0.t([128, 8, 128], F32, "hTf0")
    PH_BASE = A0.off

    def prologue():
        A = Alloc(S, PH_BASE)
        tab = A.t([32, 8], F32, "tab")
        oneh = A.t([32, 384], F32, "oneh")
        tbc = A.t([32, 128], F32, "tbc")
        frs = [A.t([128, 384], F32, "frs%d" % i) for i in range(2)]
        e12 = A.t([12, 12, 64], F32, "e12")
        MEMSET("pool", identf[:], 0.0, [identf])
        ASEL(identf[:], identf[:], [[-1, 128]], ALU.not_equal, 1.0, 0, 1, [identf], [identf])
        CP("dve", identb[:], identf[:], [identf], [identb])
        MEMSET("dve", onesb[:], 1.0, [onesb])
        MEMSET("dve", onesf[:], 1.0, [onesf])
        MEMSET("dve", zerob[:], 0.0, [zerob])
        DMA("sp", tab[:], I["rel_bias"], [], [tab])
        DMA("sp", oneh[:], I["c_onehot"], [], [oneh])
        DMA("sp", pbt[:], I["c_pb"], [], [pbt])
        DMA("sp", ownt[:], I["c_own"], [], [ownt])
        DMA("sp", fbt[:], I["c_fb"], [], [fbt])
        DMA("sp", ovl[:], I["c_ovl"], [], [ovl])
        DMA("sp", cmask[:], I["c_cmask"], [], [cmask])
        zres = Res("zsk")
        for h in range(8):
            fr = frs[h % 2]
            ACT(tbc[:], onesf[0:32, :], AF.Identity, [onesf, tab], [tbc], scale=tab[:, h:h + 1])
            MM(P[0][:, 0:384], tbc[:], oneh[:], True, True, [tbc, oneh], [P[0]])
            CP("act", cb[:, h:h + 1], P[0][:, 383:384], [P[0]], [cb])
            TS("dve", fr[:], P[0][:, 0:384], cb[:, h:h + 1], 8.0, ALU.subtract, ALU.mult, [P[0], cb], [fr])
            DMA("sp", Zap[h], fr[:], [fr], [zres])
        S.barrier()
        for h in range(8):
            for j in range(2):
                src = bass.AP(tensor=Zh, offset=h * 128 * 384 + 128 * (j + 1), ap=[[383, 128], [1, 128]])
                DMA("sp", tb[:, h, j, :], src, [zres], [tb])
        ASEL(tb[:, :, 0, :], tb[:, :, 0, :], [[0, 8], [1, 128]], ALU.is_ge, NEGM, 0, -1, [tb], [tb])
        MEMSET("pool", tw4[:], 0.0, [tw4])
        ASEL(tw4[:], tw4[:], [[-1, 128]], ALU.is_ge, NEGM, -1, 1, [tw4], [tw4])
        MEMSET("pool", cmk[:], 0.0, [cmk])
        ASEL(cmk[:], cmk[:], [[1, SEQ]], ALU.is_ge, NEGM, -31, -16, [cmk], [cmk])
        MEMSET("pool", e12[:], 1.0, [e12])
        ASEL(e12[:], e12[:], [[-1, 12], [0, 64]], ALU.is_equal, 0.0, 0, 1, [e12], [e12])
        CP("dve", egate[:], e12[:], [e12], [egate])
        S.barrier()

    def norm_T(A, xsrc, xname, row0, ntiles, gain_vec, dst, dcol0):
        h32 = A.t([128, DM], F32, "h32")
        grep = A.t([128, DM], F32, "grep")
        xin = [A.t([128, DM], F32, "xin%d" % i) for i in range(2)]
        sqj = A.t([128, DM], BF16, "sqj")
        hb = [A.t([128, DM], BF16, "hb%d" % i) for i in range(2)]
        st = [A.t([128, 4], F32, "st%d" % i) for i in range(2)]
        DMA("sp", grep[:], gain_vec.rearrange("(o d) -> o d", o=1).partition_broadcast(128), [], [grep])
        for t in range(ntiles):
            xi, h, s = xin[t % 2], hb[t % 2], st[t % 2]
            r0 = row0 + 128 * t
            DMA("sp", xi[:], xsrc[r0:r0 + 128, :], [DR(xname, r0 // 128)], [xi])
            ACT(sqj[:], xi[:], AF.Square, [xi], [sqj, s], accum=s[:, 0:1])
            ACT(s[:, 1:2], s[:, 0:1], AF.Sqrt, [s], [s], bias=EPS, scale=1.0 / DM)
            RECIP(s[:, 2:3], s[:, 1:2], [s], [s])
            STT(h[:], xi[:], s[:, 2:3], grep[:], ALU.mult, ALU.mult, [xi, s, grep], [h])
            for c in range(8):
                TR(pst[:, c, :], h[:, 128 * c:128 * (c + 1)], identb[:], [h, identb], [pst])
            CP("act", dst[:, :, dcol0 + 128 * t:dcol0 + 128 * (t + 1)], pst[:], [pst], [dst])
            if t == 0:
                STT(h32[:], xi[:], s[:, 2:3], grep[:], ALU.mult, ALU.mult, [xi, s, grep], [h32])
                for half in range(2):
                    pp = P[half]
                    for c in range(4):
                        TR(pp[:, 128 * c:128 * (c + 1)], h32[:, 128 * (4 * half + c):128 * (4 * half + c + 1)], identf[:],
                           [h32, identf], [pp])
                    CP("act", hTf0[:, 4 * half:4 * half + 4, :], pp[:, :].rearrange("p (c n) -> p c n", c=4), [pp], [hTf0])

    def fm_rmsnorm(ps, rows, n, gain, out_ap, out_res, sqb, rstd, ps2, ones_t):
        gain_col = gain[0:rows, 0:1]
        ACT(sqb[0:rows, 0:n], ps[0:rows, 0:n], AF.Square, [ps], [sqb])
        MM(ps2[0:rows, 0:n], ones_t[0:rows, 0:rows], sqb[0:rows, 0:n], True, True, [ones_t, sqb], [ps2])
        ACT(rstd[0:rows, 0:n], ps2[0:rows, 0:n], AF.Sqrt, [ps2], [rstd], bias=EPS, scale=1.0 / rows)
        RECIP(rstd[0:rows, 0:n], rstd[0:rows, 0:n], [rstd], [rstd])
        STT(out_ap, ps[0:rows, 0:n], gain_col, rstd[0:rows, 0:n], ALU.mult, ALU.mult, [ps, rstd, gain], [out_res])

    def attn_group(g, hb_idx, q_res, k_res, krows, v_res, v_ap_fn, kts, window, pnum, pden, pts, scnt):
        LOOK = 2
        items = []
        for kt in kts:
            r = kt - 4 * g
            lo = max(r, 0)
            hi = min(r + 4, 3) if window else 3
            biases = []
            if 0 <= r <= 3:
                biases.append((128 * r, tb[:, hb_idx, 0, :]))
            if 0 <= r + 1 <= 3:
                biases.append((128 * (r + 1), tb[:, hb_idx, 1, :]))
            if window and 0 <= r + 4 <= 3:
                biases.append((128 * (r + 4), tw4[:]))
            items.append((kt, 128 * lo, 128 * (hi + 1), biases))
        bufs = {}

        def qk(i):
            kt, c0, c1, biases = items[i]
            ps = P[scnt[0] % 3]
            pt = pts[scnt[0] % len(pts)]
            scnt[0] += 1
            bufs[i] = pt
            MM(ps[:, c0:c1], k_res[0:krows, 128 * kt:128 * (kt + 1)], q_res[0:krows, 512 * g + c0:512 * g + c1],
               True, len(biases) == 0, [k_res, q_res], [ps])
            for bi, (bc, bt) in enumerate(biases):
                MM(ps[:, bc:bc + 128], identf[:], bt, False, bi == len(biases) - 1, [identf, tb, tw4], [ps])
            ACT(pt[:, c0:c1], ps[:, c0:c1], AF.Exp, [ps, cb], [pt], bias=cb[:, hb_idx:hb_idx + 1], scale=0.125)

        full0 = items[0][1] == 0 and items[0][2] == 512
        if not full0:
            MM(pnum[0:64, :], zerob[:, 0:64], cmk[:, 0:512], True, False, [zerob, cmk], [pnum])
            MM(pden[0:64, :], zerob[:, 0:64], cmk[:, 0:512], True, False, [zerob, cmk], [pden])

        def pv(i):
            kt, c0, c1, biases = items[i]
            pt = bufs[i]
            st_ = (i == 0) and full0
            MM(pnum[0:64, c0:c1], v_ap_fn(kt), pt[:, c0:c1], st_, i == len(items) - 1, [v_res, pt], [pnum])
            MM(pden[0:64, c0:c1], onesb[:, 0:64], pt[:, c0:c1], st_, i == len(items) - 1, [onesb, pt], [pden])

        for i in range(len(items) + LOOK):
            if i < len(items):
                qk(i)
            if i - LOOK >= 0:
                pv(i - LOOK)

    def load_col(A, vec_ap, n, name):
        t = A.t([n, 1], F32, name)
        DMA("sp", t[:], vec_ap.rearrange("(d o) -> d o", o=1), [], [t])
        return t

    def sel_rows(kaug, nrows, blk):
        v = kaug[64:64 + nrows, :]
        MEMSET("pool", kaug[64:128, :], 0.0, [kaug])
        MEMSET("pool", v, 8192.0, [kaug])
        ASEL(v, v, [[1, SEQ]], ALU.is_ge, 0.0, 0, -blk, [kaug], [kaug])
        ASEL(v, v, [[-1, SEQ]], ALU.is_ge, 0.0, blk - 1, blk, [kaug], [kaug])

    def phase_moba(l, s):
        A = Alloc(S, PH_BASE)
        wA = A.t([128, 8, 768], BF16, "wA")
        for k_ in range(8):
            DMA("pool", wA[:, k_, :], I["w_in"][l, 128 * k_:128 * (k_ + 1), 0:768], [], [wA])
        gq = load_col(A, I["moba_q_norm"][l], 64, "gq")
        gk = load_col(A, I["moba_k_norm"][l], 64, "gk")
        qaug = [A.t([128, SEQ], BF16, "qaug%d" % h) for h in range(4)]
        kaug = [A.t([128, SEQ], BF16, "kaug%d" % h) for h in range(4)]
        V = A.t([128, 16, 256], BF16, "V")
        kmf = A.t([64, 4, 8], F32, "kmf")
        kmb = A.t([64, 4, 8], BF16, "kmb")
        sqb = [A.t([64, 512], BF16, "sqb%d" % i) for i in range(2)]
        rstd = [A.t([64, 512], F32, "rstd%d" % i) for i in range(2)]
        gsb = A.t([128, 4, 8], F32, "gsb")
        m8 = A.t([128, 4, 8], F32, "m8")
        selm = [A.t([128, 4, 8], F32, "selm%d" % i) for i in range(2)]
        mts = A.t([32, SEQ], BF16, "mts")
        pts = [A.t([128, 512], BF16, "pt%d" % i) for i in range(4)]
        rden = [A.t([64, 512], F32, "rden%d" % i) for i in range(2)]
        ob = [A.t([64, 512], BF16, "ob%d" % i) for i in range(2)]
        for h in range(4):
            sel_rows(kaug[h], 8, 256)
            MEMSET("pool", qaug[h][64:128, :], 0.0, [qaug[h]])
        n = 0
        for g in range(4):
            cols = slice(512 * g, 512 * (g + 1))
            for h in range(4):
                for (c0, dst, gain) in ((64 * h, qaug[h], gq), (256 + 64 * h, kaug[h], gk)):
                    ps = P[n % 2]
                    ps2 = P[2 + n % 2]
                    for k in range(8):
                        MM(ps[0:64, :], wA[:, k, c0:c0 + 64], hT[:, k, cols], k == 0, k == 7, [wA, hT], [ps])
                    fm_rmsnorm(ps, 64, 512, gain, dst[0:64, cols], dst, sqb[n % 2], rstd[n % 2], ps2, onesb)
                    n += 1
        for t in range(16):
            ps = P[4 + t % 2]
            for k in range(8):
                MM(ps[:, 0:256], hT[:, k, 128 * t:128 * (t + 1)], wA[:, k, 512:768], k == 0, k == 7, [hT, wA], [ps])
            CP("act", V[:, t, :], ps[:, 0:256], [ps], [V])
        for h in range(4):
            S.add("dve", lambda e, h=h: e.tensor_reduce(out=kmf[:, h, :], in_=kaug[h][0:64, :].rearrange("p (b j) -> p b j", j=256),
                                                        axis=AX.X, op=ALU.add), reads=[kaug[h]], writes=[kmf])
        CP("dve", kmb[:], kmf[:], [kmf], [kmb])
        for t in range(16):
            ps = P[t % 2]
            sm = selm[t % 2]
            for h in range(4):
                MM(ps[:, 8 * h:8 * h + 8], qaug[h][0:64, 128 * t:128 * (t + 1)], kmb[:, h, :], True, True, [qaug[h], kmb], [ps])
            TT("dve", gsb[:], ps[:, 0:32].rearrange("p (h b) -> p h b", h=4), pbt[:, t], ALU.add, [ps, pbt], [gsb])
            for h in range(4):
                S.add("dve", lambda e, h=h: e.max(out=m8[:, h, :], in_=gsb[:, h, :]), reads=[gsb], writes=[m8])
            for h in range(4):
                TS("dve", sm[:, h, :], gsb[:, h, :], m8[:, h, 2:3], None, ALU.is_ge, None, [gsb, m8], [sm])
            TT("dve", sm[:], sm[:], ownt[:, t], ALU.max, [sm, ownt], [sm])
            TS("dve", sm[:], sm[:], -1.0, None, ALU.add, None, [sm], [sm])
            pt_ = P[2 + t % 2]
            TR(pt_[0:32, 0:128], sm[:].rearrange("p h b -> p (h b)"), identf[:], [sm, identf], [pt_])
            CP("act", mts[:, 128 * t:128 * (t + 1)], pt_[0:32, 0:128], [pt_], [mts])
        for h in range(4):
            DMA("sp", qaug[h][64:72, :], mts[8 * h:8 * h + 8, :], [mts], [qaug[h]])
        scnt = [0]
        n = 0
        for h in range(4):
            for g in range(4):
                pnum, pden = P[3 + 2 * (n % 2)], P[4 + 2 * (n % 2)]
                attn_group(g, h, qaug[h], kaug[h], 128, V, lambda kt, h=h: V[:, kt, 64 * h:64 * h + 64],
                           list(range(4 * g + 4)), False, pnum, pden, pts, scnt)
                rd, o = rden[n % 2], ob[n % 2]
                RECIP(rd[:], pden[0:64, :], [pden], [rd])
                TT("dve", o[:], pnum[0:64, :], rd[:], ALU.mult, [pnum, rd], [o])
                c0 = NTOK // 2 * s + 512 * g
                DMA("sp", oM[h, :, c0:c0 + 512], o[:], [o], [DR("oM", (s, g))])
                n += 1

    def phase_nsa(l, s):
        A = Alloc(S, PH_BASE)
        wB = A.t([128, 8, 652], BF16, "wB")
        for k_ in range(8):
            DMA("pool", wB[:, k_, :], I["w_in"][l, 128 * k_:128 * (k_ + 1), 768:1420], [], [wB])
        gq = load_col(A, I["nsa_q_norm"][l], 64, "gq")
        gkc = load_col(A, I["nsa_k_norm"][l, 0], 64, "gkc")
        gks = load_col(A, I["nsa_k_norm"][l, 1], 64, "gks")
        gkw = load_col(A, I["nsa_k_norm"][l, 2], 64, "gkw")
        w1 = [A.t([64, 32, 64], BF16, "w1_%d" % i) for i in range(2)]
        w2 = [A.t([64, 64], BF16, "w2_%d" % i) for i in range(2)]
        posr = [A.t([32, 64], F32, "posr%d" % i) for i in range(2)]
        posT = [A.t([64, 32], F32, "posT%d" % i) for i in range(2)]
        for i, (a, b, c) in enumerate((("cmp_k_w1", "cmp_k_w2", "cmp_pos_k"), ("cmp_v_w1", "cmp_v_w2", "cmp_pos_v"))):
            DMA("pool", w1[i][:], I[a][l].rearrange("(l d) o -> d l o", d=64), [], [w1[i]])
            DMA("pool", w2[i][:], I[b][l], [], [w2[i]])
            DMA("sp", posr[i][:], I[c][l], [], [posr[i]])
        nqaug = [A.t([128, SEQ], BF16, "nqaug%d" % h) for h in range(4)]
        ksaug = A.t([128, SEQ], BF16, "ksaug")
        kwT = A.t([64, SEQ], BF16, "kwT")
        kcT = A.t([64, SEQ], BF16, "kcT")
        vcT = A.t([64, SEQ], BF16, "vcT")
        vs = A.t([128, 16, 64], BF16, "vs")
        vw = A.t([128, 16, 64], BF16, "vw")
        sgT = A.t([12, SEQ], BF16, "sgT")
        kcs = A.t([64, 32, 127], BF16, "kcs")
        gl = A.t([64, 128], BF16, "gl")
        kcn = A.t([64, 128], BF16, "kcn")
        vcm = A.t([128, 64], F32, "vcm")
        sqb = [A.t([64, 512], BF16, "sqb%d" % i) for i in range(2)]
        rstd = [A.t([64, 512], F32, "rstd%d" % i) for i in range(2)]
        ptf = [A.t([128, 512], F32, "ptf%d" % i) for i in range(2)]
        rdr = A.t([128, 512], F32, "rdr")
        impT = A.t([32, 512], F32, "impT")
        vv = A.t([128, 32], F32, "vv")
        vv2 = A.t([128, 32], F32, "vv2")
        m8a = A.t([128, 8], F32, "m8a")
        m8b = A.t([128, 8], F32, "m8b")
        selm = [A.t([128, 32], F32, "selm%d" % i) for i in range(2)]
        mts = A.t([32, 512], BF16, "mts")
        pts = [A.t([128, 512], BF16, "pt%d" % i) for i in range(4)]
        gr = [A.t([64, 512], F32, "gr%d" % i) for i in range(2)]
        rden = [A.t([64, 512], F32, "rden%d" % i) for i in range(2)]
        tmp = [A.t([64, 512], F32, "tmp%d" % i) for i in range(2)]
        oacc = [A.t([64, 512], F32, "oacc%d" % h) for h in range(4)]
        ob = [A.t([64, 512], BF16, "ob%d" % i) for i in range(2)]
        sel_rows(ksaug, 32, 64)
        for h in range(4):
            MEMSET("pool", nqaug[h][64:128, :], 0.0, [nqaug[h]])
        for i in range(2):
            TR(P[6][0:64, 0:32], posr[i][:], identf[0:32, 0:32], [posr[i], identf], [P[6]])
            CP("act", posT[i][:], P[6][0:64, 0:32], [P[6]], [posT[i]])
        n = 0
        for g in range(4):
            cols = slice(512 * g, 512 * (g + 1))
            outs = [(64 * h, nqaug[h], gq) for h in range(4)] + [(384, ksaug, gks), (512, kwT, gkw), (256, kcT, None), (320, vcT, None)]
            for (c0, dst, gain) in outs:
                ps = P[n % 2]
                ps2 = P[2 + n % 2]
                for k in range(8):
                    MM(ps[0:64, :], wB[:, k, c0:c0 + 64], hT[:, k, cols], k == 0, k == 7, [wB, hT], [ps])
                if gain is None:
                    CP("act", dst[0:64, cols], ps[0:64, :], [ps], [dst])
                else:
                    fm_rmsnorm(ps, 64, 512, gain, dst[0:64, cols], dst, sqb[n % 2], rstd[n % 2], ps2, onesb)
                n += 1
            ps = P[n % 2]
            n += 1
            for k in range(8):
                MM(ps[0:12, :], wB[:, k, 640:652], hT[:, k, cols], k == 0, k == 7, [wB, hT], [ps])
            ACT(sgT[:, cols], ps[0:12, :], AF.Sigmoid, [ps], [sgT])
        for t in range(16):
            ps = P[4 + t % 2]
            for k in range(8):
                MM(ps[:, 0:64], hT[:, k, 128 * t:128 * (t + 1)], wB[:, k, 448:512], k == 0, k == 7, [hT, wB], [ps])
            for k in range(8):
                MM(ps[:, 64:128], hT[:, k, 128 * t:128 * (t + 1)], wB[:, k, 576:640], k == 0, k == 7, [hT, wB], [ps])
            CP("act", vs[:, t, :], ps[:, 0:64], [ps], [vs])
            CP("act", vw[:, t, :], ps[:, 64:128], [ps], [vw])
        for i, src in enumerate((kcT, vcT)):
            for ll in range(32):
                TS("dve", kcs[:, ll, :], src[0:64, ll:ll + 2017:16], posT[i][:, ll:ll + 1], None, ALU.add, None,
                   [src, posT[i]], [kcs])
            ps = P[0]
            for ll in range(32):
                MM(ps[0:64, 0:127], w1[i][:, ll, :], kcs[:, ll, :], ll == 0, ll == 31, [w1[i], kcs], [ps])
            ACT(gl[:, 0:127], ps[0:64, 0:127], AF.Gelu_apprx_tanh, [ps], [gl])
            if i == 0:
                MM(P[1][0:64, 0:127], w2[0][:], gl[:, 0:127], True, True, [w2[0], gl], [P[1]])
                fm_rmsnorm(P[1], 64, 127, gkc, kcn[:, 0:127], kcn, sqb[0], rstd[0], P[2], onesb)
            else:
                MM(P[1][0:127, 0:64], gl[:, 0:127], w2[1][:], True, True, [w2[1], gl], [P[1]])
                CP("act", vcm[0:127, :], P[1][0:127, 0:64], [P[1]], [vcm])
        scnt = [0]
        nn = 0
        for g in range(4):
            cols = slice(512 * g, 512 * (g + 1))
            for h in range(4):
                ps = P[h % 2]
                pf = ptf[h % 2]
                MM(ps[0:127, :], kcn[:, 0:127], nqaug[h][0:64, cols], True, False, [kcn, nqaug[h]], [ps])
                MM(ps[0:127, :], identb[0:127, 0:127], cmk[0:127, cols], False, True, [identb, cmk], [ps])
                ACT(pf[0:127, :], ps[0:127, :], AF.Exp, [ps], [pf], scale=0.125)
                MM(P[2][:, :], onesf[0:127, :], pf[0:127, :], True, True, [onesf, pf], [P[2]])
                TS("dve", rdr[:], P[2][:, :], 1e-30, None, ALU.add, None, [P[2]], [rdr])
                RECIP(rdr[:], rdr[:], [rdr], [rdr])
                TT("dve", pf[0:127, :], pf[0:127, :], rdr[0:127, :], ALU.mult, [pf, rdr], [pf])
                MM(P[3][0:32, :], ovl[0:127, :], pf[0:127, :], h == 0, h == 3, [ovl, pf], [P[3]])
                MM(P[4][0:64, :], vcm[0:127, :], pf[0:127, :], True, True, [vcm, pf], [P[4]])
                gt = gr[h % 2]
                MM(P[5][0:64, :], egate[:, 3 * h + 0, :], sgT[:, cols], True, True, [egate, sgT], [P[5]])
                CP("act", gt[:], P[5][0:64, :], [P[5]], [gt])
                TT("dve", oacc[h][:], P[4][0:64, :], gt[:], ALU.mult, [P[4], gt], [oacc[h]])
            CP("act", impT[:], P[3][0:32, :], [P[3]], [impT])
            for tl in range(4):
                t = 4 * g + tl
                sm = selm[tl % 2]
                pp = P[tl % 2]
                TR(pp[:, 0:32], impT[:, 128 * tl:128 * (tl + 1)], identf[0:32, 0:32], [impT, identf], [pp])
                TT("dve", vv[:], pp[:, 0:32], fbt[:, t, :], ALU.add, [pp, fbt], [vv])
                S.add("dve", lambda e: e.max(out=m8a[:], in_=vv[:]), reads=[vv], writes=[m8a])
                S.add("dve", lambda e: e.match_replace(out=vv2[:], in_to_replace=m8a[:], in_values=vv[:], imm_value=-1e9),
                      reads=[vv, m8a], writes=[vv2])
                S.add("dve", lambda e: e.max(out=m8b[:], in_=vv2[:]), reads=[vv2], writes=[m8b])
                TS("dve", sm[:], vv[:], m8b[:, 7:8], -1.0, ALU.is_ge, ALU.add, [vv, m8b], [sm])
                pq = P[2 + tl % 2]
                TR(pq[0:32, 0:128], sm[:], identf[:], [sm, identf], [pq])
                CP("act", mts[:, 128 * tl:128 * (tl + 1)], pq[0:32, 0:128], [pq], [mts])
            for h in range(4):
                DMA("sp", nqaug[h][64:96, cols], mts[:, :], [mts], [nqaug[h]])
            for h in range(4):
                for br in (1, 2):
                    pnum, pden = P[3 + 2 * (nn % 2)], P[4 + 2 * (nn % 2)]
                    if br == 1:
                        attn_group(g, 4 + h, nqaug[h], ksaug, 128, vs, lambda kt: vs[:, kt, :],
                                   list(range(4 * g + 4)), False, pnum, pden, pts, scnt)
                    else:
                        attn_group(g, 4 + h, nqaug[h], kwT, 64, vw, lambda kt: vw[:, kt, :],
                                   list(range(max(0, 4 * g - 4), 4 * g + 4)), True, pnum, pden, pts, scnt)
                    rd, gt, tp = rden[nn % 2], gr[nn % 2], tmp[nn % 2]
                    MM(P[0][0:64, :], egate[:, 3 * h + br, :], sgT[:, cols], True, True, [egate, sgT], [P[0]])
                    RECIP(rd[:], pden[0:64, :], [pden], [rd])
                    TT("dve", rd[:], rd[:], P[0][0:64, :], ALU.mult, [rd, P[0]], [rd])
                    TT("dve", tp[:], pnum[0:64, :], rd[:], ALU.mult, [pnum, rd], [tp])
                    if br == 1:
                        TT("pool", oacc[h][:], oacc[h][:], tp[:], ALU.add, [oacc[h], tp], [oacc[h]])
                    else:
                        o = ob[h % 2]
                        TT("pool", o[:], oacc[h][:], tp[:], ALU.add, [oacc[h], tp], [o])
                        c0 = NTOK // 2 * s + 512 * g
                        DMA("sp", oN[h, :, c0:c0 + 512], o[:], [o], [DR("oN", (s, g))])
                    nn += 1

    def phase_gla(l, s):
        A = Alloc(S, PH_BASE)
        wC = A.t([128, 8, 1552], BF16, "wC")
        for k_ in range(8):
            DMA("pool", wC[:, k_, :], I["w_in"][l, 128 * k_:128 * (k_ + 1), 1420:2972], [], [wC])
        gwb = A.t([16, 256], BF16, "gwb")
        DMA("pool", gwb[:], I["gla_gate_w"][l], [], [gwb])
        gbias = A.t([128, 2], F32, "gbias")
        for c in range(2):
            DMA("sp", gbias[:, c:c + 1], I["gla_gate_b"][l, 128 * c:128 * (c + 1)].rearrange("(d o) -> d o", o=1), [], [gbias])
        ngb = A.t([128, 2], F32, "ngb")
        gon = load_col(A, I["gla_out_norm"][l], 128, "gon")
        gqT = A.t([128, 2, SEQ], BF16, "gqT")
        gkT = A.t([128, 2, SEQ], BF16, "gkT")
        gv = A.t([128, 16, 512], BF16, "gv")
        glrT = A.t([16, SEQ], BF16, "glrT")
        so = A.t([128, 4, SEQ], BF16, "so")
        csp = A.t([128, 2, SEQ], F32, "csp")
        ex = [A.t([128, 512], F32, "ex%d" % i) for i in range(2)]
        eb = [A.t([128, 128], F32, "eb%d" % i) for i in range(2)]
        ek = [A.t([128, 128], F32, "ek%d" % i) for i in range(2)]
        qt = [A.t([128, 128], BF16, "qt%d" % i) for i in range(2)]
        kt_ = [A.t([128, 128], BF16, "kt%d" % i) for i in range(2)]
        ktm = [A.t([128, 128], BF16, "ktm%d" % i) for i in range(2)]
        am = [A.t([128, 128], BF16, "am%d" % i) for i in range(4)]
        Sf = [A.t([128, 128], F32, "Sf%d" % i) for i in range(2)]
        Sb = [A.t([128, 128], BF16, "Sb%d" % i) for i in range(2)]
        sqb = [A.t([128, 512], BF16, "sqb%d" % i) for i in range(2)]
        rstd = [A.t([128, 512], F32, "rstd%d" % i) for i in range(2)]
        og = [A.t([128, 512], BF16, "og%d" % i) for i in range(2)]
        S.add("act", lambda e: e.mul(ngb[:], gbias[:], -1.0), reads=[gbias], writes=[ngb])
        wqk = A.t([128, 8, 512], F32, "wqk")
        for k_ in range(8):
            DMA("sp", wqk[:, k_, :], I["w_in"][l, 128 * k_:128 * (k_ + 1), 1420:1932], [], [wqk])
        q0f = A.t([128, 2, 128], F32, "q0f")
        k0f = A.t([128, 2, 128], F32, "k0f")
        qf = [A.t([128, 128], F32, "qf%d" % i) for i in range(2)]
        kf = [A.t([128, 128], F32, "kf%d" % i) for i in range(2)]
        n = 0
        for c in range(2):
            ps = P[n % 2]; n += 1
            for k in range(8):
                MM(ps[:, 0:128], wqk[:, k, 128 * c:128 * (c + 1)], hTf0[:, k, :], k == 0, k == 7, [wqk, hTf0], [ps])
            ACT(q0f[:, c, :], ps[:, 0:128], AF.Identity, [ps], [q0f], scale=0.125)
            ps = P[n % 2]; n += 1
            for k in range(8):
                MM(ps[:, 0:128], wqk[:, k, 256 + 128 * c:256 + 128 * (c + 1)], hTf0[:, k, :], k == 0, k == 7, [wqk, hTf0], [ps])
            CP("act", k0f[:, c, :], ps[:, 0:128], [ps], [k0f])
        for g in range(4):
            cols = slice(512 * g, 512 * (g + 1))
            for c in range(2):
                ps = P[n % 2]; n += 1
                for k in range(8):
                    MM(ps[:, :], wC[:, k, 128 * c:128 * (c + 1)], hT[:, k, cols], k == 0, k == 7, [wC, hT], [ps])
                ACT(gqT[:, c, cols], ps[:, :], AF.Identity, [ps], [gqT], scale=0.125)
                ps = P[n % 2]; n += 1
                for k in range(8):
                    MM(ps[:, :], wC[:, k, 256 + 128 * c:256 + 128 * (c + 1)], hT[:, k, cols], k == 0, k == 7, [wC, hT], [ps])
                CP("act", gkT[:, c, cols], ps[:, :], [ps], [gkT])
            ps = P[n % 2]; n += 1
            for k in range(8):
                MM(ps[0:16, :], wC[:, k, 1024:1040], hT[:, k, cols], k == 0, k == 7, [wC, hT], [ps])
            CP("act", glrT[:, cols], ps[0:16, :], [ps], [glrT])
            for c in range(4):
                ps = P[n % 2]; n += 1
                for k in range(8):
                    MM(ps[:, :], wC[:, k, 1040 + 128 * c:1040 + 128 * (c + 1)], hT[:, k, cols], k == 0, k == 7, [wC, hT], [ps])
                ACT(so[:, c, cols], ps[:, :], AF.Silu, [ps], [so])
            for c in range(2):
                ps = P[2 + c]
                e_ = ex[c]
                MM(ps[:, :], gwb[:, 128 * c:128 * (c + 1)], glrT[:, cols], True, True, [gwb, glrT], [ps])
                ACT(e_[:], ps[:, :], AF.Exp, [ps, ngb], [e_], bias=ngb[:, c:c + 1], scale=-1.0)
                ACT(e_[:], e_[:], AF.Ln, [e_], [e_], bias=1.0)
                for j in range(4):
                    cc = slice(512 * g + 128 * j, 512 * g + 128 * (j + 1))
                    S.add("dve", lambda e, c=c, cc=cc, j=j, e_=e_: e.tensor_tensor_scan(
                        out=csp[:, c, cc], data0=onesf[:, :], data1=e_[:, 128 * j:128 * (j + 1)], initial=0.0,
                        op0=ALU.mult, op1=ALU.add), reads=[e_, onesf], writes=[csp])
        for t in range(16):
            ps = P[4 + t % 2]
            for k in range(8):
                MM(ps[:, :], hT[:, k, 128 * t:128 * (t + 1)], wC[:, k, 512:1024], k == 0, k == 7, [hT, wC], [ps])
            CP("act", gv[:, t, :], ps[:, :], [ps], [gv])
        for c in range(2):
            MEMSET("pool", Sf[c][:], 0.0, [Sf[c]])
            MEMSET("pool", Sb[c][:], 0.0, [Sb[c]])
        S.barrier()
        po = [P[0], P[1], P[2], P[3]]
        pat = [P[4], P[4]]
        pds = P[5]
        pss = P[6]
        for j in range(16):
            cc = slice(128 * j, 128 * (j + 1))
            jl = j % 4
            for c in range(2):
                b_, k_, q_, kk, km = eb[c], ek[c], qt[c], kt_[c], ktm[c]
                ACT(b_[:], csp[:, c, cc], AF.Exp, [csp], [b_], scale=-1.0 / 16)
                ACT(k_[:], csp[:, c, cc], AF.Exp, [csp], [k_], scale=1.0 / 16)
                TT("dve", q_[:], gqT[:, c, cc], b_[:], ALU.mult, [gqT, b_], [q_])
                TT("dve", kk[:], gkT[:, c, cc], k_[:], ALU.mult, [gkT, k_], [kk])
                TR(pst[:, c, :], kk[:], identb[:], [kk, identb], [pst])
                CP("act", km[:], pst[:, c, :], [pst], [km])
                if j == 0:
                    TT("dve", qf[c][:], q0f[:, c, :], b_[:], ALU.mult, [q0f, b_], [qf[c]])
                    TT("dve", kf[c][:], k0f[:, c, :], k_[:], ALU.mult, [k0f, k_], [kf[c]])
                for hh in range(2):
                    h = 2 * c + hh
                    rows = slice(64 * hh, 64 * (hh + 1))
                    a_ = am[2 * c + hh]
                    if j == 0:
                        MM(pat[hh][:, 128 * hh:128 * (hh + 1)], kf[c][rows, :], qf[c][rows, :], True, True, [kf[c], qf[c]], [pat[hh]])
                    else:
                        MM(pat[hh][:, 128 * hh:128 * (hh + 1)], kk[rows, :], q_[rows, :], True, True, [kk, q_], [pat[hh]])
                    TT("dve", a_[:], pat[hh][:, 128 * hh:128 * (hh + 1)], cmask[:], ALU.mult, [pat[hh], cmask], [a_])
                    MM(po[h][:, 128 * jl:128 * (jl + 1)], gv[:, j, 128 * h:128 * (h + 1)], a_[:], True, False, [gv, a_], [po[h]])
                    MM(po[h][:, 128 * jl:128 * (jl + 1)], Sb[c][rows, :], q_[rows, :], False, True, [Sb[c], q_], [po[h]])
                    MM(pds[rows, 0:128], km[:, rows], gv[:, j, 128 * h:128 * (h + 1)], True, True, [km, gv], [pds])
                TS("dve", Sf[c][:], Sf[c][:], b_[:, 127:128], None, ALU.mult, None, [Sf[c], b_], [Sf[c]])
                STT(Sf[c][:], pds[:, 0:128], b_[:, 127:128], Sf[c][:], ALU.mult, ALU.add, [pds, b_, Sf[c]], [Sf[c]])
                CP("act", Sb[c][:], Sf[c][:], [Sf[c]], [Sb[c]])
            if jl == 3:
                g = j // 4
                cols = slice(512 * g, 512 * (g + 1))
                for h in range(4):
                    sq_, rs_, o_ = sqb[h % 2], rstd[h % 2], og[h % 2]
                    ACT(sq_[:], po[h][:, :], AF.Square, [po[h]], [sq_])
                    MM(pss[:, :], onesb[:, :], sq_[:], True, True, [onesb, sq_], [pss])
                    ACT(rs_[:], pss[:, :], AF.Sqrt, [pss], [rs_], bias=EPS, scale=1.0 / 128)
                    RECIP(rs_[:], rs_[:], [rs_], [rs_])
                    STT(rs_[:], po[h][:, :], gon[:, 0:1], rs_[:], ALU.mult, ALU.mult, [po[h], gon, rs_], [rs_])
                    TT("dve", o_[:], rs_[:], so[:, h, cols], ALU.mult, [rs_, so], [o_])
                    c0 = NTOK // 2 * s + 512 * g
                    DMA("sp", oG[h, :, c0:c0 + 512], o_[:], [o_], [DR("oG", (s, g))])

    def phase_merge(l, s, xsrc, xsname, xdst, xdname):
        A = Alloc(S, PH_BASE)
        wM = A.t([128, 8, 3072], BF16, "wM")
        for k_ in range(8):
            DMA("pool", wM[:, k_, :], I["w_in"][l, 128 * k_:128 * (k_ + 1), 2972:6044], [], [wM])
        wbm = A.t([64, 4, DM], BF16, "wbm")
        wbn = A.t([64, 4, DM], BF16, "wbn")
        wbg = A.t([128, 4, DM], BF16, "wbg")
        wo = A.t([128, 8, DM], BF16, "wo")
        DMA("pool", wbm[:], I["w_branch_moba"][l].rearrange("(h d) n -> d h n", d=64), [], [wbm])
        DMA("pool", wbn[:], I["w_branch_nsa"][l].rearrange("(h d) n -> d h n", d=64), [], [wbn])
        DMA("pool", wbg[:], I["w_branch_gla"][l].rearrange("(h d) n -> d h n", d=128), [], [wbg])
        for k_ in range(8):
            DMA("pool", wo[:, k_, :], I["w_out"][l, 128 * k_:128 * (k_ + 1), :], [], [wo])
        om = [A.t([64, 4, 512], BF16, "om%d" % i) for i in range(2)]
        on = [A.t([64, 4, 512], BF16, "on%d" % i) for i in range(2)]
        ogt = [A.t([128, 4, 512], BF16, "ogt%d" % i) for i in range(2)]
        gt = [A.t([128, 512], F32, "gt%d" % i) for i in range(3)]
        t3 = [A.t([128, 512], F32, "t3_%d" % i) for i in range(3)]
        zT = A.t([128, 8, 512], BF16, "zT")
        xin = [A.t([128, DM], F32, "xin%d" % i) for i in range(2)]
        for g in range(4):
            cols = slice(512 * g, 512 * (g + 1))
            c0 = NTOK // 2 * s + 512 * g
            o1, o2, o3 = om[g % 2], on[g % 2], ogt[g % 2]
            DMA("sp", o1[:], oM[:, :, c0:c0 + 512].rearrange("h d n -> d h n"), [DR("oM", (s, g))], [o1])
            DMA("sp", o2[:], oN[:, :, c0:c0 + 512].rearrange("h d n -> d h n"), [DR("oN", (s, g))], [o2])
            DMA("sp", o3[:], oG[:, :, c0:c0 + 512].rearrange("h d n -> d h n"), [DR("oG", (s, g))], [o3])
            for c in range(8):
                for b in range(3):
                    ps = P[b]
                    for k in range(8):
                        MM(ps[:, :], wM[:, k, 1024 * b + 128 * c:1024 * b + 128 * (c + 1)], hT[:, k, cols], k == 0, k == 7,
                           [wM, hT], [ps])
                    ACT(gt[b][:], ps[:, :], AF.Sigmoid, [ps], [gt[b]])
                for b, (wb, ot) in enumerate(((wbm, o1), (wbn, o2), (wbg, o3))):
                    ps = P[3 + b]
                    rows = 64 if b < 2 else 128
                    for h in range(4):
                        MM(ps[:, :], wb[0:rows, h, 128 * c:128 * (c + 1)], ot[0:rows, h, :], h == 0, h == 3, [wb, ot], [ps])
                    TT("dve", t3[b][:], ps[:, :], gt[b][:], ALU.mult, [ps, gt[b]], [t3[b]])
                TT("pool", t3[0][:], t3[0][:], t3[1][:], ALU.add, [t3[0], t3[1]], [t3[0]])
                TT("pool", zT[:, c, :], t3[0][:], t3[2][:], ALU.add, [t3[0], t3[2]], [zT])
            for tl in range(4):
                r0 = NTOK // 2 * s + 512 * g + 128 * tl
                xi = xin[tl % 2]
                DMA("sp", xi[:], xsrc[r0:r0 + 128, :], [DR(xsname, r0 // 128)], [xi])
                for hf in range(2):
                    ps = P[6] if hf == 0 else P[0]
                    for c in range(8):
                        MM(ps[:, :], zT[:, c, 128 * tl:128 * (tl + 1)], wo[:, c, 512 * hf:512 * (hf + 1)], c == 0, c == 7,
                           [zT, wo], [ps])
                    TT("dve", xi[:, 512 * hf:512 * (hf + 1)], xi[:, 512 * hf:512 * (hf + 1)], ps[:, :], ALU.add, [xi, ps], [xi])
                DMA("sp", xdst[r0:r0 + 128, :], xi[:], [xi], [DR(xdname, r0 // 128)])

    def phase_ffn(l, xsrc, xsname, xdst, xdname, seq_list):
        A = Alloc(S, CONST_END)
        wu = A.t([128, 8, 2 * DFF], BF16, "wu")
        wd = A.t([128, 22, DM], BF16, "wd")
        wu_r = {}
        wd_r = {}
        for cb in range(11):
            for half in range(2):
                c0 = half * DFF + 256 * cb
                r_ = Res("wu_%d_%d" % (cb, half))
                DMA("pool", wu[:, :, c0:c0 + 256], I["w_up"][l, :, c0:c0 + 256].rearrange("(k p) n -> p k n", p=128), [], [r_])
                wu_r[(cb, half)] = r_
        for cb in range(11):
            r_ = Res("wd_%d" % cb)
            DMA("pool", wd[:, 2 * cb:2 * cb + 2, :], I["w_down"][l, 256 * cb:256 * (cb + 1), :].rearrange("(c p) n -> p c n", p=128), [], [r_])
            wd_r[cb] = r_
        cwr = A.t([22, 4, 128], F32, "cwr")
        cw = A.t([128, 4, 22], F32, "cw")
        for j in range(3):
            DMA("sp", cwr[:, j, :], I["conv_w"][l, j].rearrange("(c p) -> c p", p=128), [], [cwr])
        DMA("sp", cwr[:, 3, :], I["conv_b"][l].rearrange("(c p) -> c p", p=128), [], [cwr])
        for j in range(4):
            TR(P[6][:, 22 * j:22 * (j + 1)], cwr[:, j, :], identf[0:22, 0:22], [cwr, identf], [P[6]])
        CP("act", cw[:].rearrange("p j c -> p (j c)"), P[6][:, 0:88], [P[6]], [cw])
        h2T = A.t([128, 8, 256], BF16, "h2T")
        uT = A.t([128, 22, 256], BF16, "uT")
        NB = 3
        ab = [A.t([128, 258], F32, "ab%d" % i) for i in range(NB)]
        t1 = [A.t([128, 256], F32, "t1_%d" % i) for i in range(NB)]
        ge = [A.t([128, 256], F32, "ge%d" % i) for i in range(NB)]
        pas = [Res("pa%d" % i, P[2 * i][:, 0:256]) for i in range(NB)]
        pgs = [Res("pg%d" % i, P[2 * i + 1][:, 0:256]) for i in range(NB)]
        halo = A.t([128, 22, 2], F32, "halo")
        grep = A.t([128, DM], F32, "grep")
        xin = [A.t([128, DM], F32, "xin%d" % i) for i in range(2)]
        sqj = A.t([128, DM], BF16, "sqj")
        hb = [A.t([128, DM], BF16, "hb%d" % i) for i in range(2)]
        st = [A.t([128, 4], F32, "st%d" % i) for i in range(2)]
        DMA("sp", grep[:], I["ffn_norm"][l].rearrange("(o d) -> o d", o=1).partition_broadcast(128), [], [grep])
        n = 0
        for s in seq_list:
            MEMSET("pool", halo[:], 0.0, [halo])
            for gi in range(8):
                row0 = NTOK // 2 * s + 256 * gi
                for t in range(2):
                    xi, h, s_ = xin[t], hb[t], st[t]
                    r0 = row0 + 128 * t
                    DMA("sp", xi[:], xsrc[r0:r0 + 128, :], [DR(xsname, r0 // 128)], [xi])
                    ACT(sqj[:], xi[:], AF.Square, [xi], [sqj, s_], accum=s_[:, 0:1])
                    ACT(s_[:, 1:2], s_[:, 0:1], AF.Sqrt, [s_], [s_], bias=EPS, scale=1.0 / DM)
                    RECIP(s_[:, 2:3], s_[:, 1:2], [s_], [s_])
                    STT(h[:], xi[:], s_[:, 2:3], grep[:], ALU.mult, ALU.mult, [xi, s_, grep], [h])
                    for c in range(8):
                        TR(pst[:, c, :], h[:, 128 * c:128 * (c + 1)], identb[:], [h, identb], [pst])
                    CP("act", h2T[:, :, 128 * t:128 * (t + 1)], pst[:], [pst], [h2T])
                for c in range(22):
                    pa, pg = pas[n % NB], pgs[n % NB]
                    a_, t_, g_ = ab[n % NB], t1[n % NB], ge[n % NB]
                    n += 1
                    for k in range(8):
                        MM(pa[:, :], wu[:, k, 128 * c:128 * (c + 1)], h2T[:, k, :], k == 0, k == 7, [wu_r[(c // 2, 0)], h2T], [pa])
                    for k in range(8):
                        MM(pg[:, :], wu[:, k, DFF + 128 * c:DFF + 128 * (c + 1)], h2T[:, k, :], k == 0, k == 7, [wu_r[(c // 2, 1)], h2T], [pg])
                    CP("pool", a_[:, 0:2], halo[:, c, :], [halo], [a_])
                    CP("act", a_[:, 2:258], pa[:, :], [pa], [a_])
                    CP("pool", halo[:, c, :], a_[:, 256:258], [a_], [halo])
                    TS("dve", t_[:], a_[:, 2:258], cw[:, 2, c:c + 1], cw[:, 3, c:c + 1], ALU.mult, ALU.add, [a_, cw], [t_])
                    STT(t_[:], a_[:, 1:257], cw[:, 1, c:c + 1], t_[:], ALU.mult, ALU.add, [a_, cw, t_], [t_])
                    STT(t_[:], a_[:, 0:256], cw[:, 0, c:c + 1], t_[:], ALU.mult, ALU.add, [a_, cw, t_], [t_])
                    ACT(g_[:], t_[:], AF.Gelu_apprx_tanh, [t_], [g_])
                    TT("dve", uT[:, c, :], g_[:], pg[:, :], ALU.mult, [g_, pg], [uT])
                for t in range(2):
                    xi = xin[t]
                    r0 = row0 + 128 * t
                    for hf in range(2):
                        ps = P[6]
                        for c in range(22):
                            MM(ps[:, :], uT[:, c, 128 * t:128 * (t + 1)], wd[:, c, 512 * hf:512 * (hf + 1)], c == 0, c == 21,
                               [uT, wd_r[c // 2]], [ps])
                        TT("dve", xi[:, 512 * hf:512 * (hf + 1)], xi[:, 512 * hf:512 * (hf + 1)], ps[:, :], ALU.add, [xi, ps], [xi])
                    DMA("sp", xdst[r0:r0 + 128, :], xi[:], [xi], [DR(xdname, r0 // 128)])

    prologue()
    cur, cname = I["x"], "x"
    for l in layers:
        mid, mname = (xA, "xA")
        for s in seqs:
            if "N" in phases:
                A = Alloc(S, PH_BASE)
                norm_T(A, cur, cname, NTOK // 2 * s, 16, I["attn_norm"][l], hT, 0)
                S.barrier()
            if "A" in phases:
                phase_moba(l, s)
                S.barrier()
            if "B" in phases:
                phase_nsa(l, s)
                S.barrier()
            if "C" in phases:
                phase_gla(l, s)
                S.barrier()
            if "M" in phases:
                phase_merge(l, s, cur, cname, mid, mname)
                S.barrier()
        last = (l == layers[-1])
        dst, dname = (yout, "y") if (last and not dbg) else (xB, "xB")
        if "F" in phases:
            phase_ffn(l, mid, mname, dst, dname, seqs)
            S.barrier()
        cur, cname = dst, dname
    S.emit()
    return nc, consts


_CACHE = {}


def kernel(**inputs):
    n = 8
    x = np.ascontiguousarray(np.asarray(inputs["x"], dtype=np.float32))
    if "prog" not in _CACHE:
        _CACHE["prog"] = build_program()
    nc, consts = _CACHE["prog"]
    base = {k: np.ascontiguousarray(np.asarray(inputs[k], dtype=np.float32)) for k in W_SHAPES}
    base.update(consts)
    in_maps = []
    for i in range(n):
        m = dict(base)
        m["x"] = x[2 * i:2 * i + 2].reshape(NTOK, DM)
        in_maps.append(m)
    res = run_bass_kernel_spmd(nc, in_maps, core_ids=list(range(n)))
    out = np.stack([np.asarray(r["y"], dtype=np.float32).reshape(2, SEQ, DM) for r in res.results], 0)
    return out.reshape(16, SEQ, DM)
```
